# Optimizing a Trainium2 kernel written in Bass

```python
import jax, jax.numpy as jnp
from jax import lax
import numpy as np

D_MODEL = 1024
BATCH = 2
SEQ = 8192
DEPTH = 1

N_HEADS = 16
HEAD_DIM = 64
N_KV_GROUPS = 4
HEADS_PER_GROUP = N_HEADS // N_KV_GROUPS
ROPE_DIM = HEAD_DIM // 4
ROPE_THETA = 500000.0
CMP_BLOCK = 32
CMP_STRIDE = 16
CMP_HIDDEN = 256
SLC_BLOCK = 64
SLC_TOPK = 16
WINDOW = 512
Q_BLOCK = 128
ATTN_Q_WIDTH = N_HEADS * HEAD_DIM
KV_WIDTH = N_KV_GROUPS * HEAD_DIM
POOL_WIDTH = D_MODEL // 2
POOL_WINDOWS = (2, 4, 8, 16)
POOL_GROUP = POOL_WIDTH // len(POOL_WINDOWS)
D_FF = ((8 * D_MODEL // 3 + 255) // 256) * 256
RMS_EPS = 1e-6
NEG_INF = -1e30

IN_SIZES = (POOL_WIDTH, ATTN_Q_WIDTH, KV_WIDTH, KV_WIDTH, KV_WIDTH, KV_WIDTH, KV_WIDTH, KV_WIDTH,
            N_HEADS * 3, D_MODEL, D_MODEL)
IN_WIDTH = sum(IN_SIZES)
SPLIT_POINTS = tuple(int(v) for v in np.cumsum(IN_SIZES)[:-1])

kernel_name = "hybrid_pool_nsa_gated_block"


def rms_norm(x, w):
    xf = x.astype(jnp.float32)
    y = xf * lax.rsqrt(jnp.mean(xf * xf, axis=-1, keepdims=True) + RMS_EPS)
    return (y * w.astype(jnp.float32)).astype(x.dtype)


def partial_rope(x, positions):
    half = ROPE_DIM // 2
    inv_freq = 1.0 / (ROPE_THETA ** (jnp.arange(0, ROPE_DIM, 2, dtype=jnp.float32) / ROPE_DIM))
    ang = positions.astype(jnp.float32)[:, None] * inv_freq[None, :]
    cos = jnp.cos(ang)[None, :, None, :]
    sin = jnp.sin(ang)[None, :, None, :]
    xf = x.astype(jnp.float32)
    x1, x2 = xf[..., :half], xf[..., half:ROPE_DIM]
    out = jnp.concatenate([x1 * cos - x2 * sin, x2 * cos + x1 * sin, xf[..., ROPE_DIM:]], axis=-1)
    return out.astype(x.dtype)


def pool_mixer(u, pool_w, pool_scale):
    B, S, _ = u.shape
    uf = u.astype(jnp.float32)
    csum = jnp.concatenate([jnp.zeros((B, 1, POOL_WIDTH), jnp.float32), jnp.cumsum(uf, axis=1)], axis=1)
    t = jnp.arange(S)
    outs = []
    for g, w in enumerate(POOL_WINDOWS):
        sl = slice(g * POOL_GROUP, (g + 1) * POOL_GROUP)
        lo = jnp.maximum(t + 1 - w, 0)
        cnt = (t + 1 - lo).astype(jnp.float32)
        mean = (csum[:, 1:, sl] - csum[:, lo, sl]) / cnt[None, :, None]
        outs.append(mean - uf[:, :, sl])
    pooled = jnp.stack(outs, axis=2)
    mixed = jnp.einsum('bsgc,gcd->bsgd', pooled, pool_w.astype(jnp.float32))
    return (mixed.reshape(B, S, POOL_WIDTH) * pool_scale.astype(jnp.float32)).astype(u.dtype)


def compress(kv, pe, w1, b1, w2):
    B, S, G, dk = kv.shape
    n_cmp = (S - CMP_BLOCK) // CMP_STRIDE + 1
    idx = jnp.arange(n_cmp)[:, None] * CMP_STRIDE + jnp.arange(CMP_BLOCK)[None, :]
    blocks = kv[:, idx] + pe[None, None, :, None, :]
    flat = blocks.transpose(0, 1, 3, 2, 4).reshape(B, n_cmp, G, CMP_BLOCK * dk)
    hid = jax.nn.gelu(flat @ w1 + b1)
    return hid @ w2


def cmp_to_slc_map(n_cmp, n_slc):
    start = np.arange(n_cmp) * CMP_STRIDE
    m = np.zeros((n_cmp, n_slc), np.float32)
    m[np.arange(n_cmp), start // SLC_BLOCK] = 1.0
    m[np.arange(n_cmp), (start + CMP_BLOCK - 1) // SLC_BLOCK] = 1.0
    return jnp.asarray(m)


def nsa_attention(q_plain, q_rope, k_cmp, v_cmp, k_slc, v_slc, k_win, v_win, gates):
    B, S, H, dk = q_plain.shape
    G, Hg = N_KV_GROUPS, HEADS_PER_GROUP
    scale = dk ** -0.5
    n_cmp = k_cmp.shape[1]
    n_slc = S // SLC_BLOCK
    n_top = min(SLC_TOPK, n_slc)
    n_qblk = S // Q_BLOCK
    qp_all = q_plain.reshape(B, S, G, Hg, dk)
    qr_all = q_rope.reshape(B, S, G, Hg, dk)
    g_all = jax.nn.sigmoid(gates.astype(jnp.float32)).reshape(B, S, G, Hg, 3)
    slc_map = cmp_to_slc_map(n_cmp, n_slc)
    cmp_end = jnp.arange(n_cmp) * CMP_STRIDE + (CMP_BLOCK - 1)

    def to_blocks(a):
        return a.reshape(B, n_slc, SLC_BLOCK, G, dk).transpose(0, 3, 1, 2, 4)

    ks_blk, vs_blk = to_blocks(k_slc), to_blocks(v_slc)
    pad = ((0, 0), (WINDOW, 0), (0, 0), (0, 0))
    kw_pad, vw_pad = jnp.pad(k_win, pad), jnp.pad(v_win, pad)
    bi = jnp.arange(B)[:, None, None, None]
    gi = jnp.arange(G)[None, :, None, None]
    blk_ids = jnp.arange(n_slc)
    in_blk = jnp.arange(SLC_BLOCK)
    win_off = jnp.arange(WINDOW + Q_BLOCK)

    def masked_softmax(s, mask):
        s = jnp.where(mask, s.astype(jnp.float32) * scale, NEG_INF)
        return jnp.where(mask, jax.nn.softmax(s, axis=-1), 0.0)

    def query_block(qb):
        t0 = qb * Q_BLOCK
        tq = t0 + jnp.arange(Q_BLOCK)
        qp = lax.dynamic_slice_in_dim(qp_all, t0, Q_BLOCK, axis=1)
        qr = lax.dynamic_slice_in_dim(qr_all, t0, Q_BLOCK, axis=1)
        g = lax.dynamic_slice_in_dim(g_all, t0, Q_BLOCK, axis=1)

        cmask = cmp_end[None, :] <= tq[:, None]
        p_cmp = masked_softmax(jnp.einsum('bqghd,bngd->bghqn', qp, k_cmp), cmask)
        o_cmp = jnp.einsum('bghqn,bngd->bqghd', p_cmp, v_cmp.astype(jnp.float32))

        jt = tq // SLC_BLOCK
        imp = jnp.einsum('bghqn,nj->bgqj', p_cmp, slc_map)
        causal = blk_ids[None, :] <= jt[:, None]
        forced = (blk_ids[None, :] == 0) | (blk_ids[None, :] == jt[:, None]) | (blk_ids[None, :] == jt[:, None] - 1)
        imp = jnp.where(forced, jnp.inf, jnp.where(causal, imp, -jnp.inf))
        _, sel = lax.top_k(imp, n_top)
        k_sel = ks_blk[bi, gi, sel]
        v_sel = vs_blk[bi, gi, sel]
        kpos = sel[..., None] * SLC_BLOCK + in_blk
        smask = (kpos <= tq[None, None, :, None, None]).reshape(B, G, 1, Q_BLOCK, n_top * SLC_BLOCK)
        s = jnp.einsum('bqghd,bgqkld->bghqkl', qr, k_sel).reshape(B, G, Hg, Q_BLOCK, n_top * SLC_BLOCK)
        p = masked_softmax(s, smask).reshape(B, G, Hg, Q_BLOCK, n_top, SLC_BLOCK)
        o_slc = jnp.einsum('bghqkl,bgqkld->bqghd', p, v_sel.astype(jnp.float32))

        kwb = lax.dynamic_slice_in_dim(kw_pad, t0, WINDOW + Q_BLOCK, axis=1)
        vwb = lax.dynamic_slice_in_dim(vw_pad, t0, WINDOW + Q_BLOCK, axis=1)
        kpos_w = t0 - WINDOW + win_off
        diff = tq[:, None] - kpos_w[None, :]
        wmask = (diff >= 0) & (diff < WINDOW) & (kpos_w[None, :] >= 0)
        p = masked_softmax(jnp.einsum('bqghd,bkgd->bghqk', qr, kwb), wmask)
        o_win = jnp.einsum('bghqk,bkgd->bqghd', p, vwb.astype(jnp.float32))

        o = g[..., 0:1] * o_cmp + g[..., 1:2] * o_slc + g[..., 2:3] * o_win
        return o.reshape(B, Q_BLOCK, H * dk)

    out = lax.map(query_block, jnp.arange(n_qblk))
    return out.transpose(1, 0, 2, 3).reshape(B, S, H * dk).astype(q_plain.dtype)


def setup_inputs(seed: int = 0) -> dict:
    key = jax.random.key(seed)
    k = jax.random.split(key, 24)
    L = DEPTH

    def nrm(kk, shape, scale):
        return jax.random.normal(kk, shape, jnp.float32) * scale

    def gain(kk, shape):
        return 1.0 + 0.05 * jax.random.normal(kk, shape, jnp.float32)

    return {
        "x": nrm(k[0], (BATCH, SEQ, D_MODEL), 1.0),
        "norm1_w": gain(k[1], (L, D_MODEL)),
        "w_in": nrm(k[2], (L, D_MODEL, IN_WIDTH), D_MODEL ** -0.5),
        "pool_w": nrm(k[3], (L, len(POOL_WINDOWS), POOL_GROUP, POOL_GROUP), POOL_GROUP ** -0.5),
        "pool_scale": gain(k[4], (L, POOL_WIDTH)),
        "cmp_pe_k": nrm(k[5], (L, CMP_BLOCK, HEAD_DIM), 0.5),
        "cmp_w1_k": nrm(k[6], (L, CMP_BLOCK * HEAD_DIM, CMP_HIDDEN), (CMP_BLOCK * HEAD_DIM) ** -0.5),
        "cmp_b1_k": nrm(k[7], (L, CMP_HIDDEN), 0.02),
        "cmp_w2_k": nrm(k[8], (L, CMP_HIDDEN, HEAD_DIM), CMP_HIDDEN ** -0.5),
        "cmp_pe_v": nrm(k[9], (L, CMP_BLOCK, HEAD_DIM), 0.5),
        "cmp_w1_v": nrm(k[10], (L, CMP_BLOCK * HEAD_DIM, CMP_HIDDEN), (CMP_BLOCK * HEAD_DIM) ** -0.5),
        "cmp_b1_v": nrm(k[11], (L, CMP_HIDDEN), 0.02),
        "cmp_w2_v": nrm(k[12], (L, CMP_HIDDEN, HEAD_DIM), CMP_HIDDEN ** -0.5),
        "w_proj_pool": nrm(k[13], (L, POOL_WIDTH, D_MODEL), POOL_WIDTH ** -0.5),
        "w_proj_attn": nrm(k[14], (L, ATTN_Q_WIDTH, D_MODEL), ATTN_Q_WIDTH ** -0.5),
        "w_out": nrm(k[15], (L, D_MODEL, D_MODEL), D_MODEL ** -0.5),
        "norm2_w": gain(k[16], (L, D_MODEL)),
        "w_ffn_gate": nrm(k[17], (L, D_MODEL, D_FF), D_MODEL ** -0.5),
        "w_ffn_up": nrm(k[18], (L, D_MODEL, D_FF), D_MODEL ** -0.5),
        "w_ffn_down": nrm(k[19], (L, D_FF, D_MODEL), D_FF ** -0.5),
        "norm_f_w": gain(k[20], (D_MODEL,)),
    }


def reference(x, norm1_w, w_in, pool_w, pool_scale, cmp_pe_k, cmp_w1_k, cmp_b1_k, cmp_w2_k,
              cmp_pe_v, cmp_w1_v, cmp_b1_v, cmp_w2_v, w_proj_pool, w_proj_attn, w_out,
              norm2_w, w_ffn_gate, w_ffn_up, w_ffn_down, norm_f_w):
    B, S, _ = x.shape
    G = N_KV_GROUPS
    positions = jnp.arange(S)

    def heads(a, n):
        return a.reshape(B, S, n, HEAD_DIM)

    for l in range(DEPTH):
        h = rms_norm(x, norm1_w[l])
        proj = h @ w_in[l]
        (u_pool, q, kc_raw, vc_raw, ks_raw, vs_raw, kw_raw, vw_raw,
         g_nsa, g_pool, g_attn) = jnp.split(proj, SPLIT_POINTS, axis=-1)
        q = heads(q, N_HEADS)
        q_rope = partial_rope(q, positions)
        k_cmp = compress(heads(kc_raw, G), cmp_pe_k[l], cmp_w1_k[l], cmp_b1_k[l], cmp_w2_k[l])
        v_cmp = compress(heads(vc_raw, G), cmp_pe_v[l], cmp_w1_v[l], cmp_b1_v[l], cmp_w2_v[l])
        k_slc = partial_rope(heads(ks_raw, G), positions)
        k_win = partial_rope(heads(kw_raw, G), positions)
        attn = nsa_attention(q, q_rope, k_cmp, v_cmp, k_slc, heads(vs_raw, G), k_win, heads(vw_raw, G),
                             g_nsa.reshape(B, S, N_HEADS, 3))
        pool = pool_mixer(u_pool, pool_w[l], pool_scale[l])
        merged = (jax.nn.sigmoid(g_pool) * (pool @ w_proj_pool[l])
                  + jax.nn.sigmoid(g_attn) * (attn @ w_proj_attn[l]))
        x = x + merged @ w_out[l]
        h = rms_norm(x, norm2_w[l])
        x = x + (jax.nn.silu(h @ w_ffn_gate[l]) * (h @ w_ffn_up[l])) @ w_ffn_down[l]
    return rms_norm(x, norm_f_w)
```

```python
import numpy as np
from contextlib import ExitStack
import concourse.bass as bass
import concourse.mybir as mybir
from concourse.bass_utils import run_bass_kernel_spmd

F32 = mybir.dt.float32
BF16 = mybir.dt.bfloat16
AF = mybir.ActivationFunctionType
ALU = mybir.AluOpType

D = 1024
S = 8192
NT = 64
NQ = 16
DFF = 2816
NEG = -30000.0
EPS = 1e-6
BIG = 1.0e9


class Op:
    __slots__ = ("eng", "fn", "idx", "deps", "bdeps", "signal", "sig", "is_dma", "key")

    def __init__(self, eng, fn, idx, is_dma, key):
        self.eng = eng
        self.fn = fn
        self.idx = idx
        self.deps = set()
        self.bdeps = set()
        self.signal = False
        self.sig = None
        self.is_dma = is_dma
        self.key = key


class Prog:
    ENGS = ("pe", "act", "dve", "pool", "sp")
    BLOCK_NAME = {"pe": "tensor", "act": "scalar", "dve": "vector", "pool": "gpsimd", "sp": "sync"}

    def __init__(self, nc):
        self.nc = nc
        self.ops = []
        self.last_writer = {}
        self.readers = {}
        self.bank_last = {}
        self.last_on_eng = {}
        self.pending_dmas = []

    def op(self, eng, fn, reads=(), writes=(), banks=(), dma=False, key=None):
        o = Op(eng, fn, len(self.ops), dma, key)
        deps = set()
        for r in reads:
            w = self.last_writer.get(r)
            if w is not None:
                deps.add(w)
            self.readers.setdefault(r, []).append(o)
        for r in writes:
            w = self.last_writer.get(r)
            if w is not None:
                deps.add(w)
            for rd in self.readers.get(r, ()):
                deps.add(rd)
            self.readers[r] = []
            self.last_writer[r] = o
        deps.discard(o)
        o.deps = deps
        for b in banks:
            w = self.bank_last.get(b)
            if w is not None and w is not o:
                o.bdeps.add(w)
            self.bank_last[b] = o
        self.ops.append(o)
        if dma:
            self.pending_dmas.append(o)
        else:
            self.last_on_eng[eng] = o
        return o

    def dma(self, eng, fn, reads=(), writes=(), key=None):
        assert key is not None
        return self.op(eng, fn, reads, writes, dma=True, key=key)

    def barrier(self):
        deps = set(self.last_on_eng.values()) | set(self.pending_dmas)
        for en in self.ENGS:
            o = Op(en, None, len(self.ops), False, None)
            o.deps = set(deps)
            self.ops.append(o)
        self.pending_dmas = []
        self.last_writer = {}
        self.readers = {}
        self.bank_last = {}

    def emit(self, stack, final_wait_eng="sp"):
        nc = self.nc
        ops = self.ops
        for o in ops:
            for d in o.deps:
                if d.is_dma:
                    continue
                if d.eng == "pe" and o.eng == "pe" and o.fn is not None:
                    continue
                d.signal = True
            for d in o.bdeps:
                if d.is_dma or d.eng == o.eng:
                    continue
                d.signal = True
        sems = {}
        counts = {}

        def get_sem(k):
            if k not in sems:
                sems[k] = stack.enter_context(nc.semaphore("s_" + "_".join(str(x) for x in k)))
                counts[k] = 0
            return sems[k]

        for o in ops:
            if o.is_dma:
                k = ("d", o.key)
                s = get_sem(k)
                counts[k] += 16
                o.sig = (k, s, counts[k])
            elif o.signal:
                k = ("e", o.eng)
                s = get_sem(k)
                counts[k] += 1
                o.sig = (k, s, counts[k])
        finals = [(k, sems[k], counts[k]) for k in sems if k[0] == "d"]
        block = stack.enter_context(nc.Block())
        for en in self.ENGS:
            eops = [o for o in ops if o.eng == en]

            def body(e, eops=eops, en=en):
                waited = {}
                for o in eops:
                    dl = [d for d in o.deps if not (d.eng == "pe" and en == "pe" and not d.is_dma and o.fn is not None)]
                    dl += [d for d in o.bdeps if d.is_dma or d.eng != en]
                    for d in sorted(dl, key=lambda d: d.idx):
                        if d.sig is None:
                            continue
                        k, s, v = d.sig
                        if waited.get(k, 0) < v:
                            e.wait_ge(s, v)
                            waited[k] = v
                    if o.fn is None:
                        continue
                    inst = o.fn(e)
                    if o.sig is not None:
                        k, s, v = o.sig
                        inst.then_inc(s, 16 if o.is_dma else 1)
                if en == final_wait_eng:
                    for k, s, v in finals:
                        if waited.get(k, 0) < v:
                            e.wait_ge(s, v)

            getattr(block, self.BLOCK_NAME[en])(body)
        return len(sems)


def seq(fns):
    def f(e):
        i = None
        for fn in fns:
            i = fn(e)
        return i
    return f


def MM(out, lhsT, rhs, start=True, stop=True):
    return lambda e: e.matmul(out, lhsT=lhsT, rhs=rhs, start=start, stop=stop)


def TR(out, in_, ident):
    return lambda e: e.transpose(out=out, in_=in_, identity=ident)


IN_SIZES = (512, 1024, 256, 256, 256, 256, 256, 256, 48, 1024, 1024)
_sp = np.cumsum((0,) + IN_SIZES)
COL = {n: (int(_sp[i]), int(_sp[i + 1])) for i, n in enumerate(
    ["pool", "q", "kc", "vc", "ks", "vs", "kw", "vw", "gnsa", "gpool", "gattn"])}


def _shared_inputs(inp):
    f = np.float32
    w_in = np.asarray(inp["w_in"], f)[0]

    def cols(name, lo, hi):
        a, _ = COL[name]
        return w_in[:, a + lo:a + hi]

    wkv = np.stack([np.concatenate([cols(n, 64 * g, 64 * g + 64) for n in ("kc", "ks", "vc", "kw", "vs", "vw")], axis=1)
                    for g in range(4)], 0)
    wq = np.stack([np.concatenate([cols("q", 256 * g, 256 * g + 256), cols("gnsa", 12 * g, 12 * g + 12)], axis=1)
                   for g in range(4)], 0)
    sh = {
        "wkv": np.ascontiguousarray(wkv), "wq": np.ascontiguousarray(wq),
        "wpool": np.ascontiguousarray(cols("pool", 0, 512)),
        "wgp": np.ascontiguousarray(cols("gpool", 0, 1024)),
        "wga": np.ascontiguousarray(cols("gattn", 0, 1024)),
        "n1w": np.asarray(inp["norm1_w"], f)[0], "n2w": np.asarray(inp["norm2_w"], f)[0],
        "nfw": np.asarray(inp["norm_f_w"], f),
        "poolw": np.asarray(inp["pool_w"], f)[0],
        "psc": np.ascontiguousarray(np.asarray(inp["pool_scale"], f)[0].reshape(4, 128).T),
        "w1k": np.asarray(inp["cmp_w1_k"], f)[0], "w1v": np.asarray(inp["cmp_w1_v"], f)[0],
        "b1t": np.ascontiguousarray(np.concatenate([np.asarray(inp["cmp_b1_k"], f)[0].reshape(2, 128).T,
                                                    np.asarray(inp["cmp_b1_v"], f)[0].reshape(2, 128).T], axis=1)),
        "w2k": np.asarray(inp["cmp_w2_k"], f)[0], "w2v": np.asarray(inp["cmp_w2_v"], f)[0],
        "pekT": np.ascontiguousarray(np.asarray(inp["cmp_pe_k"], f)[0].T),
        "pevT": np.ascontiguousarray(np.asarray(inp["cmp_pe_v"], f)[0].T),
        "wpp": np.asarray(inp["w_proj_pool"], f)[0], "wpa": np.asarray(inp["w_proj_attn"], f)[0],
        "wout": np.asarray(inp["w_out"], f)[0],
        "wg": np.asarray(inp["w_ffn_gate"], f)[0], "wu": np.asarray(inp["w_ffn_up"], f)[0],
        "wd": np.asarray(inp["w_ffn_down"], f)[0],
    }
    sh["ident"] = np.eye(128, dtype=f)
    k = np.arange(S)
    sh["epat"] = ((k[None, :] // 64) % 64 == np.arange(64)[:, None]).astype(f)
    z = np.zeros((128, 224), f)
    z[np.arange(32), 96 + np.arange(32)] = 1.0
    z[32, 128:] = 1.0
    sh["zsel"] = z
    n = np.arange(512)
    mm = np.zeros((512, 128), f)
    for nn in range(511):
        mm[nn, (16 * nn) // 64] = 1.0
        mm[nn, (16 * nn + 31) // 64] = 1.0
    sh["mmap"] = np.ascontiguousarray(mm.reshape(4, 128, 128).transpose(1, 0, 2))
    inv_freq = (1.0 / (np.float32(500000.0) ** (np.arange(0, 16, 2, dtype=f) / np.float32(16)))).astype(f)
    ang = np.arange(S, dtype=f)[:, None] * inv_freq[None, :]
    sh["_cos"] = np.cos(ang).astype(f)
    sh["_sin"] = np.sin(ang).astype(f)
    kk = np.arange(128)[:, None]
    qq = np.arange(128)[None, :]
    tri = np.where(kk <= qq, 0.0, NEG).astype(f)
    tri2 = np.where(kk > qq, 0.0, NEG).astype(f)
    sh["tm"] = np.ascontiguousarray(np.broadcast_to(tri[:, None, :], (128, 4, 128)).reshape(128, 512))
    sh["wm"] = np.ascontiguousarray(np.stack([np.broadcast_to(tri2[:, None, :], (128, 4, 128)).reshape(128, 512),
                                              np.broadcast_to(tri[:, None, :], (128, 4, 128)).reshape(128, 512)], axis=1))
    n32 = np.arange(32)[:, None, None]
    cm = np.where(16 * n32 + 31 <= 384 + np.arange(128)[None, None, :], 0.0, NEG).astype(f)
    cmf = np.zeros((128, 512), f)
    cmf[0:32] = np.broadcast_to(cm, (32, 4, 128)).reshape(32, 512)
    cmf[32] = NEG
    sh["cm"] = cmf
    ps = np.zeros((128, 128), f)
    for i in range(3):
        ps[i, :8 * (i + 1)] = 1.0
    sh["padsel"] = ps
    q1 = np.arange(128)[:, None]
    rel = np.arange(248)[None, :] - 120
    jt0 = 6 + (q1 >= 64)
    g_ = np.zeros((128, 248), f)
    g_[rel > jt0] = -BIG
    g_[(rel == jt0) | (rel == jt0 - 1)] = BIG
    sh["gtab"] = g_
    return sh


def _core_inputs(inp, sh, r):
    f = np.float32
    b, c = r // 4, r % 4
    x = np.asarray(inp["x"], f)
    xb = x[b]
    toks = (np.arange(NQ)[:, None] * 4 + c) * 128 + np.arange(128)[None, :]
    d = {k_: v for k_, v in sh.items() if not k_.startswith("_")}
    sft = 3 - c
    xkv = np.zeros((S, D), f)
    xkv[sft * 128:] = xb[:S - sft * 128]
    d["xkv"] = xkv
    d["xq"] = np.ascontiguousarray(xb[toks.reshape(-1)])
    ht = (np.arange(NQ)[:, None] * 4 + c) * 128 - 16 + np.arange(16)[None, :]
    xh = np.zeros((NQ * 16, D), f)
    valid = (ht >= 0).reshape(-1)
    xh[valid] = xb[ht.reshape(-1)[valid]]
    d["xh"] = xh
    d["cosq"] = np.ascontiguousarray(sh["_cos"][toks].transpose(1, 0, 2))
    d["sinq"] = np.ascontiguousarray(sh["_sin"][toks].transpose(1, 0, 2))
    kpos = np.clip(np.arange(S) - sft * 128, 0, None)
    d["cosk"] = np.ascontiguousarray(sh["_cos"][kpos].reshape(64, 128, 8).transpose(1, 0, 2))
    d["sink"] = np.ascontiguousarray(sh["_sin"][kpos].reshape(64, 128, 8).transpose(1, 0, 2))
    wm0 = np.zeros((128, 3, 512), f)
    for sl in range(3):
        if sl < sft:
            wm0[:, sl, :] = NEG
    d["wm0"] = wm0
    pm = np.zeros((128, 512), f)
    if sft > 0:
        pm[sft - 1] = NEG
    d["padm"] = pm
    g0 = np.zeros((128, 128), f)
    g0[:, :2 * sft] = -3 * BIG
    g0[:, 2 * sft] = BIG
    d["g0"] = g0
    wv = np.array([2, 4, 8, 16])[:, None]
    tt = c * 128 + np.arange(128)[None, :]
    invc = (1.0 / np.minimum(tt + 1, wv)).astype(f)
    d["invc"] = np.ascontiguousarray(np.broadcast_to(invc[None], (128, 4, 128)))
    return d


INPUT_SHAPES = {
    "xkv": [S, D], "xq": [2048, D], "xh": [256, D], "n1w": [D], "n2w": [D], "nfw": [D],
    "wkv": [4, D, 384], "wq": [4, D, 268], "wpool": [D, 512], "wgp": [D, D], "wga": [D, D],
    "poolw": [4, 128, 128], "psc": [128, 4], "w1k": [2048, 256], "w1v": [2048, 256],
    "b1t": [128, 4], "w2k": [256, 64], "w2v": [256, 64], "pekT": [64, 32], "pevT": [64, 32],
    "wpp": [512, D], "wpa": [D, D], "wout": [D, D], "wg": [D, DFF], "wu": [D, DFF], "wd": [DFF, D],
    "ident": [128, 128], "epat": [64, S], "zsel": [128, 224], "mmap": [128, 4, 128],
    "cosk": [128, 64, 8], "sink": [128, 64, 8], "cosq": [128, 16, 8], "sinq": [128, 16, 8],
    "tm": [128, 512], "wm": [128, 2, 512], "wm0": [128, 3, 512], "cm": [128, 512], "gtab": [128, 248], "invc": [128, 4, 128],
    "padsel": [128, 128], "padm": [128, 512], "g0": [128, 128],
}

ARENA_BYTES = 206 * 1024


def A_ACT(out, in_, func, **kw):
    return lambda e: e.activation(out=out, in_=in_, func=func, **kw)


def A_CP(out, in_):
    return lambda e: e.copy(out=out, in_=in_)


def V_CP(out, in_):
    return lambda e: e.tensor_copy(out=out, in_=in_)


def V_TT(out, a, b, op):
    return lambda e: e.tensor_tensor(out=out, in0=a, in1=b, op=op)


def V_TS(out, in0, s1, s2, op0, op1=None):
    if op1 is None:
        return lambda e: e.tensor_scalar(out=out, in0=in0, scalar1=s1, scalar2=None, op0=op0)
    return lambda e: e.tensor_scalar(out=out, in0=in0, scalar1=s1, scalar2=s2, op0=op0, op1=op1)


def V_STT(out, in0, scalar, in1, op0, op1):
    return lambda e: e.scalar_tensor_tensor(out=out, in0=in0, scalar=scalar, in1=in1, op0=op0, op1=op1)


def V_REC(out, in_):
    return lambda e: e.reciprocal(out=out, in_=in_)


def V_MEMSET(ap, val):
    return lambda e: e.memset(ap, val)


def V_MAX(out, in_):
    return lambda e: e.max(out=out, in_=in_)


def V_MR(out, rep, vals, imm):
    return lambda e: e.match_replace(out=out, in_to_replace=rep, in_values=vals, imm_value=imm)


def DMA(out, in_):
    return lambda e: e.dma_start(out=out, in_=in_)


def build_program(n_groups=4, phase3=True, dbg=False, n_qblocks=NQ):
    nc = bass.Bass("TRN2", target_bir_lowering=False)
    I = {n: nc.dram_tensor(n, shp, F32, kind="ExternalInput").ap() for n, shp in INPUT_SHAPES.items()}
    out_d = nc.dram_tensor("out", [2048, D], F32, kind="ExternalOutput").ap()
    scr = nc.dram_tensor("htq_scr", [18, 128, 1024], BF16, kind="Internal").ap()
    scr2 = nc.dram_tensor("hkv_scr", [NT, 128, 1024], BF16, kind="Internal").ap()
    st = ExitStack()
    with st:
        arena = st.enter_context(nc.sbuf_tensor("arena", [128, ARENA_BYTES // 4], F32))
        banks = [st.enter_context(nc.psum_tensor("bank%d" % i, [128, 512], F32)) for i in range(8)]
        P = Prog(nc)

        def carve(off, shape, dt, p0=0):
            esz = 4 if dt == F32 else 2
            n = int(np.prod(shape[1:]))
            nb = n * esz
            assert off % 4 == 0 and nb % 4 == 0, (off, shape)
            assert off + nb <= ARENA_BYTES, (off, shape)
            ap = arena[p0:p0 + shape[0], off // 4:(off + nb) // 4]
            if dt != F32:
                ap = ap.bitcast(dt)
            if len(shape) == 3:
                ap = ap.rearrange("p (a b) -> p a b", a=shape[1])
            elif len(shape) == 4:
                ap = ap.rearrange("p (a b c) -> p a b c", a=shape[1], b=shape[2])
            return ap

        class Bump:
            def __init__(self, off):
                self.off = off

            def __call__(self, shape, dt, p0=0):
                esz = 4 if dt == F32 else 2
                nb = (int(np.prod(shape[1:])) * esz + 31) // 32 * 32
                ap = carve(self.off, shape, dt, p0)
                self.off += nb
                return ap

        def pbank(i, shape, dt, p0=0, col0=0):
            n = int(np.prod(shape[1:]))
            if dt == F32:
                ap = banks[i][p0:p0 + shape[0], col0:col0 + n]
            else:
                ap = banks[i][p0:p0 + shape[0], col0:col0 + n // 2].bitcast(dt)
            if len(shape) == 3:
                ap = ap.rearrange("p (a b) -> p a b", a=shape[1])
            elif len(shape) == 4:
                ap = ap.rearrange("p (a b c) -> p a b c", a=shape[1], b=shape[2])
            return ap

        dbg_list = []

        def dump(name, ap, shape, dt):
            if not dbg:
                return
            P.barrier()
            dd = nc.dram_tensor("dbg_" + name, list(shape), dt, kind="ExternalOutput").ap()
            P.dma("sp", DMA(dd, ap), key="dbg_" + name)
            P.barrier()
            dbg_list.append(name)

        def load(eng, dst, src, name):
            P.dma(eng, DMA(dst, src), writes=[name], key=name)

        A_ = Bump(0)
        identb = A_([128, 128], BF16)
        identf = A_([128, 128], F32)
        nw = A_([128, 1024], F32)
        cosk = A_([128, 64, 8], F32)
        sink = A_([128, 64, 8], F32)
        cosq = A_([128, 16, 8], F32)
        sinq = A_([128, 16, 8], F32)
        tmb = A_([128, 512], BF16)
        wmb = A_([128, 2, 512], BF16)
        wm0b = A_([128, 3, 512], BF16)
        padselb = A_([128, 128], BF16)
        padmb = A_([128, 512], BF16)
        g0t = A_([128, 128], F32)
        cmb = A_([128, 512], BF16)
        zsel = A_([128, 224], BF16)
        mmapb = A_([128, 4, 128], BF16)
        gtab = A_([128, 248], F32)
        invc = A_([128, 4, 128], F32)
        beff = A_([128, 4], F32)
        b1t = A_([128, 4], F32)
        peT = A_([128, 32], BF16)
        w2b = A_([128, 2, 2, 64], BF16)
        mhalf = A_([128, 1], F32)
        psc = A_([128, 4], F32)
        C_END = (A_.off + 1023) // 1024 * 1024
        attnT = carve(C_END, [128, 8, 2048], BF16)
        P0 = C_END + 32768

        load("pool", identb, I["ident"], "identb")
        load("sp", identf, I["ident"], "identf")
        load("sp", nw, I["n1w"].partition_broadcast(128), "nw")
        load("sp", cosk, I["cosk"], "cosk")
        load("sp", sink, I["sink"], "sink")
        load("sp", cosq, I["cosq"], "cosq")
        load("sp", sinq, I["sinq"], "sinq")
        load("pool", tmb, I["tm"], "tmb")
        load("pool", wmb, I["wm"], "wmb")
        load("pool", wm0b, I["wm0"], "wm0b")
        load("pool", padselb, I["padsel"], "padselb")
        load("pool", padmb, I["padm"], "padmb")
        load("sp", g0t, I["g0"], "g0t")
        load("pool", cmb, I["cm"], "cmb")
        load("pool", zsel, I["zsel"], "zsel")
        load("pool", mmapb, I["mmap"], "mmapb")
        load("sp", gtab, I["gtab"], "gtab")
        load("sp", invc, I["invc"], "invc")
        load("sp", b1t, I["b1t"], "b1t")
        load("sp", psc, I["psc"], "psc")
        load("pool", peT[0:64, :], I["pekT"], "pek")
        load("pool", peT[64:128, :], I["pevT"], "pev")
        load("pool", w2b[:, 0, :, :], I["w2k"].rearrange("(j p) d -> p j d", p=128), "w2k")
        load("pool", w2b[:, 1, :, :], I["w2v"].rearrange("(j p) d -> p j d", p=128), "w2v")
        P.op("pool", V_MEMSET(mhalf, -0.5), writes=["mhalf"])

        B1 = Bump(P0)
        KA = B1([128, S], BF16)
        AT_ = B1([128, S], BF16)
        BW = B1([128, S], BF16)
        VSW = B1([128, 64, 2, 66], BF16)
        VC = B1([128, 4, 66], BF16)
        KCT = B1([128, 512], BF16)
        WKV = B1([128, 8, 384], BF16)
        WQ = B1([128, 8, 268], BF16)
        X0 = B1.off
        X1 = Bump(X0)
        XB = [X1([128, 1024], F32) for _ in range(3)]
        HB = [X1([128, 1024], BF16) for _ in range(2)]
        HT = [X1([128, 8, 128], BF16) for _ in range(4)]
        JUNK = X1([128, 1024], BF16)
        SSQ = [X1([128, 4], F32) for _ in range(3)]
        KVB = [X1([128, 2, 2, 64], BF16) for _ in range(3)]
        RT = [X1([128, 4, 2, 8], F32) for _ in range(3)]
        W1 = X1([128, 32, 256], BF16)
        GX = X1([128, 512], F32)
        GU = X1([128, 512], F32)
        GS = X1([128, 512], F32)
        HID = X1([128, 2, 2, 512], BF16)
        assert X1.off <= ARENA_BYTES, X1.off
        X2 = Bump(X0)
        HQ = [X2([128, 8, 128], BF16) for _ in range(2)]
        QPR = [X2([128, 4, 2, 64], BF16) for _ in range(2)]
        RS = 6
        GSB = [X2([128, 12], F32) for _ in range(RS)]
        GEX = X2([128, 12], F32)
        RT2 = [X2([128, 4, 4, 8], F32) for _ in range(2)]
        QPT = [X2([128, 512], BF16) for _ in range(2)]
        QA2 = [X2([128, 2, 512], BF16) for _ in range(RS)]
        PC = [X2([128, 512], BF16) for _ in range(4)]
        PT = [X2([128, 512], BF16) for _ in range(4)]
        OT = X2([128, 3, 512], F32)
        OTW = [X2([128, 512], F32) for _ in range(2)]
        IMPV = X2([128, 128], F32)
        V2 = X2([128, 128], F32)
        SELA = X2([128, 128], F32)
        SELB = X2([128, 128], F32)
        WK = X2([128, 128], F32)
        NMP = X2([128, 2, 128], BF16)
        M8 = X2([128, 16], F32)
        DEN = X2([128, 3, 4], F32)
        RDEN = X2([128, 3, 4], F32)
        COEF = X2([128, 3, 4], F32)
        ACC = [X2([128, 4, 64], F32) for _ in range(RS)]
        ATB = X2([128, 256], BF16)
        assert X2.off <= ARENA_BYTES, X2.off

        load("pool", KA[0:64, :], I["epat"], "KA_E")
        P.op("pool", V_MEMSET(VSW[:, :, :, 64:66], 1.0), writes=["VSW_ones"])
        P.op("pool", V_MEMSET(VC[:, :, 64:66], 1.0), writes=["VC_ones"])
        P.op("pool", V_MEMSET(KCT[:, 508:512], 0.0), writes=["KCT_pad"])
        P.op("pool", V_MEMSET(KCT[64:128, :], 0.0), writes=["KCT_z"])
        P.op("pool", V_MEMSET(BW[0:64, :], 0.0), writes=["BW_z"])

        def norm_dma(src_dram, xi):
            P.dma("sp", DMA(XB[xi], src_dram), writes=["xb%d" % xi], key="xb%d" % xi)

        def norm_a1(xi):
            xb, ssq = XB[xi], SSQ[xi]
            xn = "xb%d" % xi
            P.op("act", A_ACT(JUNK, xb, AF.Square, accum_out=ssq[:, 0:1]), reads=[xn], writes=["ss%d_0" % xi])
            P.op("dve", V_TS(ssq[:, 1:2], ssq[:, 0:1], 1.0 / D, EPS, ALU.mult, ALU.add), reads=["ss%d_0" % xi], writes=["ss%d_1" % xi])
            P.op("pool", V_TT(ssq[:, 2:3], ssq[:, 1:2], mhalf, ALU.pow), reads=["ss%d_1" % xi, "mhalf"], writes=["ss%d_2" % xi])

        def norm_a2(xi, hi):
            xb, ssq, hb = XB[xi], SSQ[xi], HB[hi]
            P.op("dve", V_STT(hb, xb, ssq[:, 2:3], nw, ALU.mult, ALU.mult),
                 reads=["xb%d" % xi, "ss%d_2" % xi, "nw"], writes=["hb%d" % hi])

        def norm_b(hi):
            hb, hts = HB[hi], HT[hi]
            tv = pbank(hi, [128, 8, 128], BF16)
            P.op("pe", seq([TR(tv[:, k, :], hb[:, k * 128:(k + 1) * 128], identb) for k in range(8)]),
                 reads=["hb%d" % hi, "identb"], banks=[hi])
            P.op("act", A_CP(hts, tv), writes=["hT%d" % hi], banks=[hi])

        def rope_ops(psrc, dst, cos_ap, sin_ap, rt, nh, rd, wr, rtname, bank):
            cb = cos_ap.unsqueeze(1).to_broadcast([128, nh, 8])
            sb_ = sin_ap.unsqueeze(1).to_broadcast([128, nh, 8])
            x1 = psrc[:, :, 0:8]
            x2 = psrc[:, :, 8:16]
            P.op("dve", seq([V_TT(rt[:, 0], x1, cb, ALU.mult), V_TT(rt[:, 1], x2, sb_, ALU.mult),
                             V_TT(rt[:, 2], x2, cb, ALU.mult), V_TT(rt[:, 3], x1, sb_, ALU.mult)]),
                 reads=rd, writes=[rtname], banks=[bank])
            P.op("dve", seq([V_TT(dst[:, :, 0:8], rt[:, 0], rt[:, 1], ALU.subtract),
                             V_TT(dst[:, :, 8:16], rt[:, 2], rt[:, 3], ALU.add)]),
                 reads=[rtname], writes=[wr])

        def pre_src(blk):
            return I["xq"][blk * 128:(blk + 1) * 128, :] if blk < 16 else I["xh"][(blk - 16) * 128:(blk - 15) * 128, :]

        norm_dma(pre_src(0), 0)
        norm_dma(pre_src(1), 1)
        for s_ in range(18 + 2):
            if s_ + 2 < 18:
                norm_dma(pre_src(s_ + 2), (s_ + 2) % 3)
            if s_ < 18:
                norm_a1(s_ % 3)
                norm_a2(s_ % 3, s_ % 2)
            if 0 <= s_ - 1 < 18:
                norm_b((s_ - 1) % 2)
            if 0 <= s_ - 2 < 18:
                blk = s_ - 2
                hi = blk % 2
                P.dma("sp", DMA(scr[blk].rearrange("p (k t) -> p k t", k=8), HT[hi]),
                      reads=["hT%d" % hi], writes=["scr%d" % blk], key="scrw%d" % hi)

        S_BANKS = (0, 1, 7)
        B_OC, B_OS, B_OW, B_I, B_QT, B_F = 2, 3, 4, 5, 6, 6
        tile_ctr = [0]

        def step_gen(gen):
            if gen is None:
                return None
            try:
                next(gen)
                return gen
            except StopIteration:
                return None

        def exhaust(gen):
            while gen is not None:
                gen = step_gen(gen)

        LAG = 2
        pt_ctr = [0]

        def chain(*gens):
            for g_ in gens:
                if g_ is not None:
                    yield from g_

        def tiles_gen(tiles):
            n = len(tiles)
            for ti in range(n + LAG):
                if ti < n:
                    T = tiles[ti]
                    sbk = S_BANKS[tile_ctr[0] % 3]
                    tile_ctr[0] += 1
                    if T["pbuf"] is None:
                        T["pbuf"] = PT[pt_ctr[0] % 4]
                        T["pname"] = "pt%d" % (pt_ctr[0] % 4)
                        pt_ctr[0] += 1
                    so = pbank(sbk, [128, 512], F32)
                    nq = len(T["qk"])
                    P.op("pe", seq([MM(so, a, b, qi == 0, qi == nq - 1) for qi, (a, b) in enumerate(T["qk"])]),
                         reads=T["rd"], banks=[sbk])
                    P.op("act", A_ACT(T["pbuf"], so, AF.Exp, scale=0.125), writes=[T["pname"]], banks=[sbk])
                if ti - LAG >= 0:
                    pend = tiles[ti - LAG]
                    lhsT, obank, first, last = pend["pv"]
                    oo = pbank(obank, [65, 512], F32)
                    P.op("pe", MM(oo, lhsT, pend["pbuf"], first, last), reads=[pend["pname"]] + pend["pvrd"], banks=[obank])
                    if pend.get("post") is not None:
                        pend["post"]()
                yield

        def run_tiles(tiles, side=None, stride=1):
            k = 0
            for _ in tiles_gen(tiles):
                k += 1
                if side is not None and k % stride == 0:
                    side = step_gen(side)
            return side

        for g in range(n_groups):
            load("pool", WKV, I["wkv"][g].rearrange("(kc kp) n -> kp kc n", kp=128), "WKV")
            load("pool", WQ, I["wq"][g].rearrange("(kc kp) n -> kp kc n", kp=128), "WQ")
            load("pool", W1[0:64], I["w1k"].rearrange("(l d) j -> d l j", d=64), "W1k")
            load("pool", W1[64:128], I["w1v"].rearrange("(l d) j -> d l j", d=64), "W1v")
            P.op("pool", V_MEMSET(HID[:, :, :, 508:512], 0.0), writes=["HID_pad"])
            def kv_s1(t, hi):
                j2 = t % 2
                j3 = t % 3
                pb = 2 + j2
                pv = pbank(pb, [128, 3, 2, 64], F32)
                P.op("pe", seq([MM(pbank(pb, [128, 384], F32), HT[hi][:, k, :], WKV[:, k, :], k == 0, k == 7) for k in range(8)]),
                     reads=["hT%d" % hi, "WKV"], banks=[pb])
                kvb = KVB[j3]
                P.op("act", seq([A_CP(kvb[:, :, 0, :], pv[:, 0:2, 0, :]), A_CP(kvb[:, :, 1, 16:64], pv[:, 0:2, 1, 16:64]),
                                 A_CP(VSW[:, t, :, 0:64], pv[:, 2, :, :])]),
                     reads=["VSW_ones"], writes=["kvb%d_a" % j3, "VSW"], banks=[pb])
                rope_ops(pv[:, 0:2, 1, :], kvb[:, :, 1, :], cosk[:, t, :], sink[:, t, :], RT[j3], 2,
                         ["cosk", "sink"], "kvb%d_r" % j3, "rt%d" % j3, pb)

            def kv_s2(t):
                j2 = t % 2
                j3 = t % 3
                kvb = KVB[j3]
                kb = 4 + j2
                kv = pbank(kb, [128, 3, 128], BF16)
                kflat = kvb.rearrange("p a b d -> p (a b d)")
                P.op("pe", seq([TR(kv[:, 0, :], kflat[:, 0:128], identb), TR(kv[:, 1, :], kflat[:, 128:256], identb),
                                TR(kv[:, 2, :], kflat[:, 64:192], identb)]),
                     reads=["kvb%d_a" % j3, "kvb%d_r" % j3, "identb"], banks=[kb])
                cs = slice(t * 128, (t + 1) * 128)
                atd = AT_.rearrange("p (s n) -> p s n", s=16)[:, :, t * 8:(t + 1) * 8]
                P.op("dve", seq([V_CP(atd[0:64], kv[0:64, 0, :].rearrange("p (n s) -> p s n", s=16)),
                                 V_CP(KA[64:128, cs], kv[64:128, 0, :])]),
                     writes=["AT_lo", "KA_hi"], banks=[kb])
                P.op("act", seq([A_CP(BW[64:128, cs], kv[64:128, 1, :]),
                                 A_CP(atd[64:128], kv[64:128, 2, :].rearrange("p (n s) -> p s n", s=16))]),
                     writes=["BW_hi", "AT_hi"], banks=[kb])

            def kv_src(t):
                return I["xkv"][t * 128:(t + 1) * 128, :]

            if g == 0:
                norm_dma(kv_src(0), 0)
                norm_dma(kv_src(1), 1)
                for s_ in range(NT + 3):
                    if s_ + 2 < NT:
                        norm_dma(kv_src(s_ + 2), (s_ + 2) % 3)
                    if s_ < NT:
                        norm_a1(s_ % 3)
                        norm_a2(s_ % 3, s_ % 2)
                    if 0 <= s_ - 1 < NT:
                        norm_b((s_ - 1) % 2)
                    if 0 <= s_ - 2 < NT:
                        t_ = s_ - 2
                        P.dma("sp", DMA(scr2[t_].rearrange("p (k t) -> p k t", k=8), HT[t_ % 2]),
                              reads=["hT%d" % (t_ % 2)], writes=["scr2_%d" % t_], key="scr2w%d" % (t_ % 2))
                        kv_s1(t_, t_ % 2)
                    if 0 <= s_ - 3 < NT:
                        kv_s2(s_ - 3)
            else:
                def ht_load(t_):
                    P.dma("sp", DMA(HT[t_ % 4], scr2[t_].rearrange("p (k t) -> p k t", k=8)),
                          writes=["hT%d" % (t_ % 4)], key="hTl%d" % (t_ % 4))
                ht_load(0)
                ht_load(1)
                ht_load(2)
                for s_ in range(NT + 2):
                    if s_ + 3 < NT:
                        ht_load(s_ + 3)
                    if s_ < NT:
                        kv_s1(s_, s_ % 4)
                    if 0 <= s_ - 2 < NT:
                        kv_s2(s_ - 2)

            if g == 0:
                for kv_ in range(2):
                    rows = slice(64 * kv_, 64 * kv_ + 64)
                    for jh in range(2):
                        bo = pbank(6 + kv_, [128, 1], F32, col0=2 * jh)
                        P.op("pe", seq([MM(bo, W1[rows, l, jh * 128:(jh + 1) * 128], peT[rows, l:l + 1], l == 0, l == 31)
                                        for l in range(32)]),
                             reads=["W1k", "W1v", "pek", "pev"], banks=[6 + kv_])
                for kv_ in range(2):
                    P.op("dve", V_TT(beff[:, 2 * kv_:2 * kv_ + 2], pbank(6 + kv_, [128, 2, 2], F32)[:, :, 0], b1t[:, 2 * kv_:2 * kv_ + 2], ALU.add),
                         reads=["b1t"], writes=["beff%d" % kv_], banks=[6 + kv_])
            for kv_ in range(2):
                rows = slice(64 * kv_, 64 * kv_ + 64)
                src = AT_[rows, :].rearrange("p (s n) -> p s n", s=16)
                for jh in range(2):
                    hb_ = 6 + jh
                    ho = pbank(hb_, [128, 511], F32)
                    mmz = []
                    for l in range(32):
                        rhs = src[:, l, 0:511] if l < 16 else src[:, l - 16, 1:512]
                        mmz.append(MM(ho, W1[rows, l, jh * 128:(jh + 1) * 128], rhs, l == 0, l == 31))
                    P.op("pe", seq(mmz), reads=["AT_lo", "AT_hi", "W1k", "W1v"], banks=[hb_])
                    gx, gu, gs_ = GX[:, 0:511], GU[:, 0:511], GS[:, 0:511]
                    bcol = beff[:, kv_ * 2 + jh:kv_ * 2 + jh + 1]
                    P.op("act", A_ACT(gx, ho, AF.Identity, bias=bcol), reads=["beff0", "beff1"], writes=["GX"], banks=[hb_])
                    P.op("pool", V_TT(gu, gx, gx, ALU.mult), reads=["GX"], writes=["GU"])
                    P.op("dve", V_TS(gu, gu, 0.044715, 1.0, ALU.mult, ALU.add), reads=["GU"], writes=["GU"])
                    P.op("pool", V_TT(gu, gu, gx, ALU.mult), reads=["GU", "GX"], writes=["GU"])
                    P.op("act", A_ACT(gs_, gu, AF.Sigmoid, scale=1.5957691216057308), reads=["GU"], writes=["GS"])
                    P.op("dve", V_TT(HID[:, kv_, jh, 0:511], gx, gs_, ALU.mult),
                         reads=["GX", "GS", "HID_pad"], writes=["HID%d%d" % (kv_, jh)])
            ko = pbank(6, [64, 511], F32)
            P.op("pe", seq([MM(ko, w2b[:, 0, jh, :], HID[:, 0, jh, 0:511], jh == 0, jh == 1) for jh in range(2)]),
                 reads=["HID00", "HID01", "w2k"], banks=[6])
            P.op("dve", V_CP(KCT[0:64, 0:511], ko), reads=["KCT_pad", "KCT_z"], writes=["KCT"], banks=[6])
            vo = pbank(7, [128, 4, 64], F32)
            P.op("pe", seq([MM(vo[:, nt_, :], HID[:, 1, jh, nt_ * 128:(nt_ + 1) * 128], w2b[:, 1, jh, :], jh == 0, jh == 1)
                            for nt_ in range(4) for jh in range(2)]),
                 reads=["HID10", "HID11", "w2v", "HID_pad"], banks=[7])
            P.op("dve", V_CP(VC[:, :, 0:64], vo), reads=["VC_ones"], writes=["VC"], banks=[7])
            if g == 0:
                dump("kct", KCT, [128, 512], BF16)
                dump("vc", VC, [128, 4, 66], BF16)
                dump("ka", KA[:, 0:1024], [128, 1024], BF16)
                dump("bw", BW[:, 0:1024], [128, 1024], BF16)
                dump("vsw", VSW[:, 0:8], [128, 8, 2, 66], BF16)
            P.barrier()

            P.op("pool", V_MEMSET(NMP, 0.0), writes=["NMP_z"])
            for i2 in range(2):
                P.op("pool", V_MEMSET(QPT[i2][64:128, :], 0.0), writes=["qpt%d_z" % i2])
            def ot_buf(j, i2):
                if j == 2:
                    return OTW[i2][0:65, :], "OTW%d" % i2
                return OT[0:65, j, :], "OT%d" % j

            def finalize_a(j, obank, i2=0):
                oo = pbank(obank, [65, 512], F32)
                dst, dn = ot_buf(j, i2)
                P.op("dve", V_CP(dst, oo), writes=[dn], banks=[obank])

            def finalize_b1(j, fbank, m):
                fo = pbank(fbank, [128, 4, 65], F32)
                src_, sn_ = ot_buf(j, m % 2)
                P.op("pe", seq([TR(fo[:, h, :], src_[:, h * 128:(h + 1) * 128], identf[0:65, 0:65]) for h in range(4)]),
                     reads=[sn_, "identf"], banks=[fbank])
                P.op("dve", V_TS(DEN[:, j, :], fo[:, :, 64], 1e-30, None, ALU.max), writes=["den%d" % j], banks=[fbank])
                P.op("dve", V_REC(RDEN[:, j, :], DEN[:, j, :]), reads=["den%d" % j], writes=["rden%d" % j])

            def finalize_b2(j, fbank, first, m):
                sl = m % RS
                gsb, acc = GSB[sl], ACC[sl]
                fo = pbank(fbank, [128, 4, 65], F32)
                P.op("dve", V_TT(COEF[:, j, :], RDEN[:, j, :], gsb.rearrange("p (h j) -> p j h", j=3)[:, j, :], ALU.mult),
                     reads=["rden%d" % j, "gsb%d" % sl], writes=["coef%d" % j])
                fns = []
                for h in range(4):
                    dst = ATB[:, h * 64:(h + 1) * 64] if j == 2 else acc[:, h, :]
                    if first:
                        fns.append(V_TS(dst, fo[:, h, 0:64], COEF[:, j, h:h + 1], None, ALU.mult))
                    else:
                        fns.append(V_STT(dst, fo[:, h, 0:64], COEF[:, j, h:h + 1], acc[:, h, :], ALU.mult, ALU.add))
                P.op("dve", seq(fns), reads=["coef%d" % j, "ACC%d" % sl], writes=["ATB"] if j == 2 else ["ACC%d" % sl], banks=[fbank])

            def PRO(m):
                i2 = m % 2
                sl = m % RS
                hq, qpr, qpt = HQ[i2], QPR[i2], QPT[i2]
                gsb, qa2 = GSB[sl], QA2[sl]
                P.dma("sp", DMA(hq, scr[m].rearrange("p (k t) -> p k t", k=8)), writes=["hq%d" % i2], key="hq%d" % i2)
                qo = pbank(B_QT, [128, 268], F32)
                P.op("pe", seq([MM(qo, hq[:, k, :], WQ[:, k, :], k == 0, k == 7) for k in range(8)]),
                     reads=["hq%d" % i2, "WQ"], banks=[B_QT])
                qv = pbank(B_QT, [128, 4, 64], F32)
                P.op("act", A_ACT(GEX, qo[:, 256:268], AF.Exp, scale=-1.0), writes=["GEX"], banks=[B_QT])
                P.op("dve", seq([V_CP(qpr[:, :, 0, :], qv), V_CP(qpr[:, :, 1, 16:64], qv[:, :, 16:64])]),
                     writes=["qpr%d_a" % i2], banks=[B_QT])
                P.op("dve", V_TS(GEX, GEX, 1.0, None, ALU.add), reads=["GEX"], writes=["GEX"])
                P.op("dve", V_REC(gsb, GEX), reads=["GEX"], writes=["gsb%d" % sl])
                rope_ops(qv, qpr[:, :, 1, :], cosq[:, m, :], sinq[:, m, :], RT2[i2], 4, ["cosq", "sinq"],
                         "qpr%d_r" % i2, "rt2_%d" % i2, B_QT)
                yield
                tq_ = pbank(B_QT, [128, 4, 128], BF16)
                P.op("pe", seq([TR(tq_[:, h, :], qpr[:, h, :, :].rearrange("p a d -> p (a d)"), identb) for h in range(4)]),
                     reads=["qpr%d_a" % i2, "qpr%d_r" % i2, "identb"], banks=[B_QT])
                P.op("dve", V_CP(qpt[0:64, :], tq_[0:64].rearrange("p h q -> p (h q)")),
                     reads=["qpt%d_z" % i2], writes=["qpt%d" % i2], banks=[B_QT])
                P.op("dve", V_CP(qa2[64:128], tq_[64:128].rearrange("p h q -> p (h q)").unsqueeze(1).to_broadcast([64, 2, 512])),
                     writes=["qa2%d_q" % sl], banks=[B_QT])
                yield
                ncmp = 32 * m + 32
                T_ = (ncmp + 127) // 128
                tiles = []
                for tt in range(T_):
                    n0 = tt * 128
                    qk = [(KCT[:, n0:n0 + 128], qpt)]
                    rd = ["KCT", "qpt%d" % i2]
                    if tt == T_ - 1:
                        off = ncmp - n0 - 32
                        qk.append((zsel[:, 96 - off:224 - off], cmb))
                        rd += ["zsel", "cmb"]
                    if tt == 0:
                        qk.append((padselb, padmb))
                        rd += ["padselb", "padmb"]
                    tiles.append(dict(qk=qk, rd=rd, pbuf=PC[tt], pname="pc%d" % tt,
                                      pv=(VC[:, tt, 0:65], B_OC, tt == 0, tt == T_ - 1), pvrd=["VC"]))
                for _ in tiles_gen(tiles):
                    yield
                io = pbank(B_I, [128, 4, 128], F32)
                P.op("pe", seq([MM(io[:, h, :], PC[tt][:, h * 128:(h + 1) * 128], mmapb[:, tt, :], tt == 0, tt == T_ - 1)
                                for h in range(4) for tt in range(T_)]),
                     reads=["pc%d" % tt for tt in range(T_)] + ["mmapb"], banks=[B_I])
                finalize_a(0, B_OC)
                yield
                finalize_b1(0, B_F, m)
                nb = 8 * m + 8 if m < 8 else 128
                P.op("dve", V_TS(IMPV[:, 0:nb], io[:, 0, 0:nb], RDEN[:, 0, 0:1], None, ALU.mult), reads=["rden0"], writes=["IMPV"], banks=[B_I])
                for h in range(1, 4):
                    P.op("dve", V_STT(IMPV[:, 0:nb], io[:, h, 0:nb], RDEN[:, 0, h:h + 1], IMPV[:, 0:nb], ALU.mult, ALU.add),
                         reads=["rden0", "IMPV"], writes=["IMPV"], banks=[B_I])
                P.op("dve", V_TT(V2[:, 0:nb], IMPV[:, 0:nb], gtab[:, 120 - 8 * m:120 - 8 * m + nb], ALU.add), reads=["IMPV", "gtab"], writes=["V2"])
                P.op("dve", V_TT(V2[:, 0:nb], V2[:, 0:nb], g0t[:, 0:nb], ALU.add), reads=["V2", "g0t"], writes=["V2"])
                P.op("dve", V_MAX(M8[:, 0:8], V2[:, 0:nb]), reads=["V2"], writes=["M8a"])
                P.op("dve", V_MR(WK[:, 0:nb], M8[:, 0:8], V2[:, 0:nb], -3.0e38), reads=["V2", "M8a"], writes=["WK"])
                P.op("dve", V_MAX(M8[:, 8:16], WK[:, 0:nb]), reads=["WK"], writes=["M8b"])
                if m < 2:
                    P.op("dve", V_TS(M8[:, 15:16], M8[:, 15:16], -1.0e8, None, ALU.max), reads=["M8b"], writes=["M8b"])
                if nb <= 64:
                    P.op("dve", V_TS(NMP[:, 0, 0:nb], V2[:, 0:nb], M8[:, 15:16], NEG, ALU.is_lt, ALU.mult),
                         reads=["V2", "M8b", "NMP_z"], writes=["NMP"])
                else:
                    P.op("dve", V_TS(NMP[:, :, 0:64], V2.rearrange("p (a b) -> p a b", a=2), M8[:, 15:16], NEG, ALU.is_lt, ALU.mult),
                         reads=["V2", "M8b", "NMP_z"], writes=["NMP"])
                finalize_b2(0, B_F, True, m)
                yield
                yield
                yield
                nt2 = pbank(B_QT, [128, 2, 128], BF16, col0=384)
                P.op("pe", seq([TR(nt2[:, 0, :], NMP[:, 0, :], identb), TR(nt2[:, 1, :], NMP[:, 1, :], identb)]),
                     reads=["NMP", "identb"], banks=[B_QT])
                P.op("dve", V_CP(qa2[0:64].rearrange("p a (h q) -> p a h q", h=4),
                                 nt2[0:64].unsqueeze(2).to_broadcast([64, 2, 4, 128])),
                     writes=["qa2%d_m" % sl], banks=[B_QT])

            def FIN_bc(m):
                finalize_b1(1, B_F, m)
                finalize_b2(1, B_F, False, m)
                yield
                finalize_b1(2, B_F, m)
                finalize_b2(2, B_F, False, m)
                yield
                ao = pbank(B_QT, [128, 2, 128], BF16)
                P.op("pe", seq([TR(ao[:, 0, :], ATB[:, 0:128], identb), TR(ao[:, 1, :], ATB[:, 128:256], identb)]),
                     reads=["ATB", "identb"], banks=[B_QT])
                P.op("dve", V_CP(attnT[:, 2 * g:2 * g + 2, m * 128:(m + 1) * 128], ao), writes=["attnT"], banks=[B_QT])

            def win_tiles(m):
                sl = m % RS
                qa2 = QA2[sl]
                tiles = []
                js = [j for j in range(5) if 4 * m - 1 + j >= 0]
                for ji, j in enumerate(js):
                    kt = 4 * m - 1 + j
                    cs = slice(kt * 128, (kt + 1) * 128)
                    qk = [(BW[:, cs], qa2[:, 0, :])]
                    rd = ["BW_hi", "BW_z", "qa2%d_q" % sl, "qa2%d_m" % sl]
                    if j == 0:
                        qk.append((identb, wmb[:, 0, :]))
                        rd += ["identb", "wmb"]
                    elif j == 4:
                        qk.append((identb, wmb[:, 1, :]))
                        rd += ["identb", "wmb"]
                    elif m == 0:
                        qk.append((identb, wm0b[:, kt, :]))
                        rd += ["identb", "wm0b"]
                    tiles.append(dict(qk=qk, rd=rd, pbuf=None, pname=None,
                                      pv=(VSW[:, kt, 1, 0:65], B_OW, ji == 0, ji == len(js) - 1), pvrd=["VSW", "VSW_ones"]))
                return tiles

            def slc_tiles(m):
                sl = m % RS
                qa2 = QA2[sl]
                tiles = []
                nk = 4 * m + 4
                for kt in range(nk):
                    cs = slice(kt * 128, (kt + 1) * 128)
                    half = kt // 32
                    qk = [(KA[:, cs], qa2[:, half, :])]
                    rd = ["KA_E", "KA_hi", "qa2%d_q" % sl, "qa2%d_m" % sl]
                    if kt == nk - 1:
                        qk.append((identb, tmb))
                        rd += ["identb", "tmb"]
                    tiles.append(dict(qk=qk, rd=rd, pbuf=None, pname=None,
                                      pv=(VSW[:, kt, 0, 0:65], B_OS, kt == 0, kt == nk - 1), pvrd=["VSW", "VSW_ones"]))
                return tiles

            order = list(range(n_qblocks - 1, -1, -1))
            created = set()
            pros = []

            def side_step(queue):
                while queue:
                    try:
                        next(queue[0][1])
                        return
                    except StopIteration:
                        queue.pop(0)

            created.add(order[0])
            exhaust(PRO(order[0]))
            prev_fin = None
            for idx, m in enumerate(order):
                queue = []
                if prev_fin is not None:
                    queue.append(("fin", prev_fin))
                for mm in order[idx + 1: idx + 1 + (RS - 2)]:
                    if mm not in created:
                        created.add(mm)
                        pros.append((mm, PRO(mm)))
                queue += pros
                wt = win_tiles(m)
                wt[-1]["post"] = (lambda m=m: finalize_a(2, B_OW, m % 2))
                tl = wt + slc_tiles(m)
                k_ = 0
                for _ in tiles_gen(tl):
                    k_ += 1
                    if k_ % 2 == 0:
                        side_step(queue)
                need = 1 if prev_fin is not None else 0
                nxt = order[idx + 1] if idx + 1 < len(order) else None
                while queue and (queue[0][0] == "fin" or queue[0][0] == nxt):
                    try:
                        next(queue[0][1])
                    except StopIteration:
                        queue.pop(0)
                pros = [q_ for q_ in queue if q_[0] != "fin"]
                finalize_a(1, B_OS)
                prev_fin = FIN_bc(m)
            exhaust(prev_fin)
            P.barrier()

        dump("attnT", attnT, [128, 8, 2048], BF16)

        if phase3:
            B3 = Bump(P0)
            HTQ = B3([128, 8, 2048], BF16)
            HTH = B3([128, 8, 256], BF16)
            POOLT = B3([128, 4, 2048], BF16)
            R0 = B3.off
            Ra = Bump(R0)
            UT = Ra([128, 4, 16, 144], F32)
            PP = [Ra([128, 16, 144], F32) for _ in range(2)]
            WPOOL = Ra([128, 8, 512], BF16)
            POOLW = Ra([128, 4, 128], BF16)
            POOLED = Ra([128, 4, 2048], BF16)
            assert Ra.off <= ARENA_BYTES, Ra.off
            for m in range(NQ):
                P.dma("sp", DMA(HTQ[:, :, m * 128:(m + 1) * 128], scr[m].rearrange("p (k t) -> p k t", k=8)),
                      writes=["HTQ"], key="HTQ%d" % (m % 4))
            for hb_ in range(2):
                P.dma("sp", DMA(HTH[:, :, hb_ * 128:(hb_ + 1) * 128], scr[16 + hb_].rearrange("p (k t) -> p k t", k=8)),
                      writes=["HTH"], key="HTH%d" % hb_)
            load("pool", WPOOL, I["wpool"].rearrange("(kc kp) n -> kp kc n", kp=128), "WPOOL")
            load("pool", POOLW, I["poolw"].rearrange("g c d -> c g d"), "POOLW")
            for gp in range(4):
                un = "UT%d" % gp
                for tt in range(4):
                    bk = tt % 2
                    uo = pbank(bk, [128, 512], F32)
                    P.op("pe", seq([MM(uo, WPOOL[:, k, gp * 128:(gp + 1) * 128], HTQ[:, k, tt * 512:(tt + 1) * 512], k == 0, k == 7)
                                    for k in range(8)]), reads=["WPOOL", "HTQ"], banks=[bk])
                    P.op("act", A_CP(UT[:, gp, 4 * tt:4 * tt + 4, 16:144], uo.rearrange("p (a b) -> p a b", a=4)),
                         writes=[un], banks=[bk])
                uh = pbank(2, [128, 256], F32)
                P.op("pe", seq([MM(uh, WPOOL[:, k, gp * 128:(gp + 1) * 128], HTH[:, k, :], k == 0, k == 7) for k in range(8)]),
                     reads=["WPOOL", "HTH"], banks=[2])
                P.op("act", A_CP(UT[:, gp, :, 0:16], uh.rearrange("p (a b) -> p a b", a=16)), writes=[un], banks=[2])
            for gp in range(4):
                un = "UT%d" % gp
                cur = UT[:, gp]
                curn = un
                lo = 0
                for si in range(gp + 1):
                    stp = 1 << si
                    lo2 = lo + stp
                    dst = PP[si % 2]
                    dn = "pp%d" % (si % 2)
                    P.op("dve", V_TT(dst[:, :, lo2:144], cur[:, :, lo2:144], cur[:, :, lo2 - stp:144 - stp], ALU.add),
                         reads=[curn], writes=[dn])
                    cur, curn, lo = dst, dn, lo2
                wsz = 2 << gp
                pl = POOLED[:, gp].rearrange("p (a b) -> p a b", a=16)
                pn = "POOLED%d" % gp
                P.op("dve", V_STT(pl[:, 1:16, :], cur[:, 1:16, 16:144], 1.0 / wsz, UT[:, gp, 1:16, 16:144], ALU.mult, ALU.subtract),
                     reads=[curn, un], writes=[pn])
                P.op("dve", V_TT(cur[:, 0, 16:144], cur[:, 0, 16:144], invc[:, gp, :], ALU.mult),
                     reads=[curn, "invc", pn], writes=[curn])
                P.op("dve", V_TT(pl[:, 0, :], cur[:, 0, 16:144], UT[:, gp, 0, 16:144], ALU.subtract),
                     reads=[curn, un, pn], writes=[pn])
                for tt in range(4):
                    bk = 3 + tt % 2
                    mo = pbank(bk, [128, 512], F32)
                    P.op("pe", MM(mo, POOLW[:, gp, :], POOLED[:, gp, tt * 512:(tt + 1) * 512]), reads=["POOLW", pn], banks=[bk])
                    P.op("act", A_ACT(POOLT[:, gp, tt * 512:(tt + 1) * 512], mo, AF.Copy, scale=psc[:, gp:gp + 1]),
                         reads=["psc"], writes=["POOLT"], banks=[bk])
            dump("poolT", POOLT, [128, 4, 2048], BF16)
            P.barrier()
            Rb = Bump(R0)
            MERGED = Rb([128, 8, 2048], BF16)
            WST = [Rb([128, 28, 128], BF16) for _ in range(2)]
            SAB = [Rb([128, 2, 512], F32) for _ in range(2)]
            T12 = [Rb([128, 2, 512], F32) for _ in range(2)]
            WOUT = Rb([128, 8, 1024], BF16)
            assert Rb.off <= ARENA_BYTES, Rb.off
            for j in range(8):
                ws = WST[j % 2]
                cs = slice(j * 128, (j + 1) * 128)
                nm_ = "wst%d" % (j % 2)
                load("pool", ws[:, 0:8, :], I["wgp"][:, cs].rearrange("(kc kp) n -> kp kc n", kp=128), nm_ + "a")
                load("pool", ws[:, 8:16, :], I["wga"][:, cs].rearrange("(kc kp) n -> kp kc n", kp=128), nm_ + "b")
                load("pool", ws[:, 16:20, :], I["wpp"][:, cs].rearrange("(kc kp) n -> kp kc n", kp=128), nm_ + "c")
                load("pool", ws[:, 20:28, :], I["wpa"][:, cs].rearrange("(kc kp) n -> kp kc n", kp=128), nm_ + "d")
                for tt in range(4):
                    ts_ = slice(tt * 512, (tt + 1) * 512)
                    q4 = (j * 4 + tt) % 2
                    ba, bb, bc, bd = (0, 1, 2, 3) if q4 == 0 else (4, 5, 6, 7)
                    pa, pb2, pc_, pd = (pbank(x, [128, 512], F32) for x in (ba, bb, bc, bd))
                    P.op("pe", seq([MM(pa, ws[:, k, :], HTQ[:, k, ts_], k == 0, k == 7) for k in range(8)]), reads=[nm_ + "a", "HTQ"], banks=[ba])
                    P.op("pe", seq([MM(pb2, ws[:, 8 + k, :], HTQ[:, k, ts_], k == 0, k == 7) for k in range(8)]), reads=[nm_ + "b", "HTQ"], banks=[bb])
                    P.op("pe", seq([MM(pc_, ws[:, 16 + k, :], POOLT[:, k, ts_], k == 0, k == 3) for k in range(4)]), reads=[nm_ + "c", "POOLT"], banks=[bc])
                    P.op("pe", seq([MM(pd, ws[:, 20 + k, :], attnT[:, k, ts_], k == 0, k == 7) for k in range(8)]), reads=[nm_ + "d", "attnT"], banks=[bd])
                    sab, t12 = SAB[q4], T12[q4]
                    P.op("act", A_ACT(sab[:, 0, :], pa, AF.Sigmoid), writes=["sab%da" % q4], banks=[ba])
                    P.op("act", A_ACT(sab[:, 1, :], pb2, AF.Sigmoid), writes=["sab%db" % q4], banks=[bb])
                    P.op("dve", V_TT(t12[:, 0, :], pc_, sab[:, 0, :], ALU.mult), reads=["sab%da" % q4], writes=["t12%da" % q4], banks=[bc])
                    P.op("dve", V_TT(t12[:, 1, :], pd, sab[:, 1, :], ALU.mult), reads=["sab%db" % q4], writes=["t12%db" % q4], banks=[bd])
                    P.op("pool", V_TT(MERGED[:, j, ts_], t12[:, 0, :], t12[:, 1, :], ALU.add),
                         reads=["t12%da" % q4, "t12%db" % q4], writes=["MERGED"])
            load("pool", WOUT, I["wout"].rearrange("(kc kp) n -> kp kc n", kp=128), "WOUT")
            dump("merged", MERGED, [128, 8, 2048], BF16)
            P.barrier()
            X1T = carve(C_END, [128, 16, 1024], F32)
            XQB = [carve(C_END + 65536 + 4096 * i, [128, 1024], F32) for i in range(2)]
            for m in range(NQ):
                xq_ = XQB[m % 2]
                xn = "xq%d" % (m % 2)
                P.dma("sp", DMA(xq_, I["xq"][m * 128:(m + 1) * 128, :]), writes=[xn], key=xn)
                for nh in range(2):
                    bk = (m * 2 + nh) % 4
                    po = pbank(bk, [128, 512], F32)
                    P.op("pe", seq([MM(po, MERGED[:, k, m * 128:(m + 1) * 128], WOUT[:, k, nh * 512:(nh + 1) * 512], k == 0, k == 7)
                                    for k in range(8)]), reads=["MERGED", "WOUT"], banks=[bk])
                    P.op("dve", V_TT(X1T[:, m, nh * 512:(nh + 1) * 512], po, xq_[:, nh * 512:(nh + 1) * 512], ALU.add),
                         reads=[xn], writes=["X1T%d" % m], banks=[bk])
            load("sp", nw, I["n2w"].partition_broadcast(128), "nw")
            dump("x1", X1T, [128, 16, 1024], F32)
            P.barrier()
            Bd = Bump(C_END + 65536)
            H2T = Bd([128, 8, 1024], BF16)
            ACTT = Bd([128, 22, 1024], BF16)
            WDB = Bd([128, 22, 512], BF16)
            WGU = [[Bd([128, 8, 256], BF16) for _ in range(2)] for _ in range(2)]
            HB2 = [Bd([128, 1024], BF16) for _ in range(2)]
            JUNK2 = Bd([128, 1024], BF16)
            SS2 = [Bd([128, 4], F32) for _ in range(2)]
            SG = [Bd([128, 512], F32) for _ in range(2)]
            OST = [carve(C_END + 65536 + 4096 * i, [128, 1024], F32) for i in range(2)]

            def rms_sb(xt, i, tag):
                ssq = SS2[i]
                P.op("act", A_ACT(JUNK2, xt, AF.Square, accum_out=ssq[:, 0:1]), reads=[tag], writes=["s2_%d_0" % i])
                P.op("dve", V_TS(ssq[:, 1:2], ssq[:, 0:1], 1.0 / D, EPS, ALU.mult, ALU.add), reads=["s2_%d_0" % i], writes=["s2_%d_1" % i])
                P.op("pool", V_TT(ssq[:, 2:3], ssq[:, 1:2], mhalf, ALU.pow), reads=["s2_%d_1" % i, "mhalf"], writes=["s2_%d_2" % i])

            NFW = Bd([128, 1024], F32)
            OSTX = Bd([128, 1024], F32)
            assert Bd.off <= ARENA_BYTES, Bd.off
            load("sp", NFW, I["nfw"].partition_broadcast(128), "NFW")

            def n2_steps(th):
                def n2_a(mm_):
                    m = th * 8 + mm_
                    rms_sb(X1T[:, m, :], m % 2, "X1T%d" % m)

                def n2_b(mm_):
                    m = th * 8 + mm_
                    i = m % 2
                    P.op("dve", V_STT(HB2[i], X1T[:, m, :], SS2[i][:, 2:3], nw, ALU.mult, ALU.mult),
                         reads=["X1T%d" % m, "s2_%d_2" % i, "nw"], writes=["hb2_%d" % i])

                def n2_c(mm_):
                    i = (th * 8 + mm_) % 2
                    tv = pbank(i, [128, 8, 128], BF16)
                    P.op("pe", seq([TR(tv[:, k, :], HB2[i][:, k * 128:(k + 1) * 128], identb) for k in range(8)]),
                         reads=["hb2_%d" % i, "identb"], banks=[i])
                    P.op("act", A_CP(H2T[:, :, mm_ * 128:(mm_ + 1) * 128], tv), writes=["H2T"], banks=[i])

                for s_ in range(8 + 2):
                    if s_ < 8:
                        n2_a(s_)
                    if 0 <= s_ - 2 < 8:
                        n2_c(s_ - 2)
                    if 0 <= s_ - 1 < 8:
                        n2_b(s_ - 1)
                    yield

            def fnorm_steps(ms, osts):
                for ii, m in enumerate(ms):
                    i = m % 2
                    rms_sb(X1T[:, m, :], i, "X1T%d" % m)
                    yield
                    ost, on = osts[ii % len(osts)]
                    P.op("dve", V_STT(ost, X1T[:, m, :], SS2[i][:, 2:3], NFW, ALU.mult, ALU.mult),
                         reads=["X1T%d" % m, "s2_%d_2" % i, "NFW"], writes=[on])
                    P.dma("sp", DMA(out_d[m * 128:(m + 1) * 128, :], ost), reads=[on], key=on)
                    yield

            exhaust(n2_steps(0))
            for th in range(2):
                load("pool", WDB, I["wd"][:, 0:512].rearrange("(j p) n -> p j n", p=128), "WDB")
                side = fnorm_steps(list(range(8)), [(OSTX, "ostx")]) if th == 1 else None
                for jp in range(11):
                    wgb, wub = WGU[jp % 2]
                    cs = slice(jp * 256, (jp + 1) * 256)
                    load("pool", wgb, I["wg"][:, cs].rearrange("(kc kp) n -> kp kc n", kp=128), "wg%d" % (jp % 2))
                    load("pool", wub, I["wu"][:, cs].rearrange("(kc kp) n -> kp kc n", kp=128), "wu%d" % (jp % 2))
                    for cc in range(2):
                        jf = jp * 2 + cc
                        for tt in range(2):
                            q4 = (jf * 2 + tt) % 2
                            bg, bu = (2, 3) if q4 == 0 else (4, 5)
                            pg, pu = pbank(bg, [128, 512], F32), pbank(bu, [128, 512], F32)
                            ts_ = slice(tt * 512, (tt + 1) * 512)
                            P.op("pe", seq([MM(pg, wgb[:, k, cc * 128:(cc + 1) * 128], H2T[:, k, ts_], k == 0, k == 7) for k in range(8)]),
                                 reads=["wg%d" % (jp % 2), "H2T"], banks=[bg])
                            P.op("pe", seq([MM(pu, wub[:, k, cc * 128:(cc + 1) * 128], H2T[:, k, ts_], k == 0, k == 7) for k in range(8)]),
                                 reads=["wu%d" % (jp % 2), "H2T"], banks=[bu])
                            sg = SG[q4]
                            P.op("act", A_ACT(sg, pg, AF.Silu), writes=["sg%d" % q4], banks=[bg])
                            P.op("dve", V_TT(ACTT[:, jf, ts_], pu, sg, ALU.mult), reads=["sg%d" % q4], writes=["ACTT"], banks=[bu])
                        side = step_gen(side)
                exhaust(side)
                side = n2_steps(1) if th == 0 else None
                for nh in range(2):
                    if nh == 1:
                        load("pool", WDB, I["wd"][:, 512:1024].rearrange("(j p) n -> p j n", p=128), "WDB")
                    for mm_ in range(8):
                        m = th * 8 + mm_
                        bk = 6 + mm_ % 2
                        po = pbank(bk, [128, 512], F32)
                        P.op("pe", seq([MM(po, ACTT[:, jf, mm_ * 128:(mm_ + 1) * 128], WDB[:, jf, :], jf == 0, jf == 21) for jf in range(22)]),
                             reads=["ACTT", "WDB"], banks=[bk])
                        xs = X1T[:, m, nh * 512:(nh + 1) * 512]
                        P.op("dve", V_TT(xs, po, xs, ALU.add), reads=["X1T%d" % m], writes=["X1T%d" % m], banks=[bk])
                        side = step_gen(side)
                exhaust(side)
            P.barrier()
            exhaust(fnorm_steps(list(range(8, NQ)), [(OST[0], "ost0"), (OST[1], "ost1")]))
        else:
            P.dma("sp", DMA(out_d[0:128, :], nw), reads=["nw"], key="ost0")
        nsem = P.emit(st)
    return nc, nsem, len(P.ops), dbg_list


_CACHE = {}


def kernel(**inputs):
    sh = _shared_inputs(inputs)
    in_maps = [_core_inputs(inputs, sh, r) for r in range(8)]
    if "nc" not in _CACHE:
        _CACHE["nc"] = build_program()[0]
    nc = _CACHE["nc"]
    res = run_bass_kernel_spmd(nc, in_maps, core_ids=list(range(8)))
    out = np.zeros((2, S, D), np.float32)
    for r in range(8):
        b, c = r // 4, r % 4
        o = np.asarray(res.results[r]["out"], np.float32).reshape(NQ, 128, D)
        for m in range(NQ):
            t0 = (4 * m + c) * 128
            out[b, t0:t0 + 128] = o[m]
    return out
```

```python
import numpy as np
from contextlib import ExitStack
import concourse.bass as bass
import concourse.mybir as mybir
from concourse.bass_utils import run_bass_kernel_spmd

F32 = mybir.dt.float32
BF16 = mybir.dt.bfloat16
AF = mybir.ActivationFunctionType
ALU = mybir.AluOpType

D = 1024
S = 8192
NT = 64
NQ = 16
DFF = 2816
NEG = -30000.0
EPS = 1e-6
BIG = 1.0e9


class Op:
    __slots__ = ("eng", "fn", "idx", "deps", "bdeps", "signal", "sig", "is_dma", "key")

    def __init__(self, eng, fn, idx, is_dma, key):
        self.eng = eng
        self.fn = fn
        self.idx = idx
        self.deps = set()
        self.bdeps = set()
        self.signal = False
        self.sig = None
        self.is_dma = is_dma
        self.key = key


class Prog:
    ENGS = ("pe", "act", "dve", "pool", "sp")
    BLOCK_NAME = {"pe": "tensor", "act": "scalar", "dve": "vector", "pool": "gpsimd", "sp": "sync"}

    def __init__(self, nc):
        self.nc = nc
        self.ops = []
        self.last_writer = {}
        self.readers = {}
        self.bank_last = {}
        self.last_on_eng = {}
        self.pending_dmas = []

    def op(self, eng, fn, reads=(), writes=(), banks=(), dma=False, key=None):
        o = Op(eng, fn, len(self.ops), dma, key)
        deps = set()
        for r in reads:
            w = self.last_writer.get(r)
            if w is not None:
                deps.add(w)
            self.readers.setdefault(r, []).append(o)
        for r in writes:
            w = self.last_writer.get(r)
            if w is not None:
                deps.add(w)
            for rd in self.readers.get(r, ()):
                deps.add(rd)
            self.readers[r] = []
            self.last_writer[r] = o
        deps.discard(o)
        o.deps = deps
        for b in banks:
            w = self.bank_last.get(b)
            if w is not None and w is not o:
                o.bdeps.add(w)
            self.bank_last[b] = o
        self.ops.append(o)
        if dma:
            self.pending_dmas.append(o)
        else:
            self.last_on_eng[eng] = o
        return o

    def dma(self, eng, fn, reads=(), writes=(), key=None):
        assert key is not None
        return self.op(eng, fn, reads, writes, dma=True, key=key)

    def barrier(self):
        deps = set(self.last_on_eng.values()) | set(self.pending_dmas)
        for en in self.ENGS:
            o = Op(en, None, len(self.ops), False, None)
            o.deps = set(deps)
            self.ops.append(o)
        self.pending_dmas = []
        self.last_writer = {}
        self.readers = {}
        self.bank_last = {}

    def emit(self, stack, final_wait_eng="sp"):
        nc = self.nc
        ops = self.ops
        for o in ops:
            for d in o.deps:
                if d.is_dma:
                    continue
                if d.eng == "pe" and o.eng == "pe" and o.fn is not None:
                    continue
                d.signal = True
            for d in o.bdeps:
                if d.is_dma or d.eng == o.eng:
                    continue
                d.signal = True
        sems = {}
        counts = {}

        def get_sem(k):
            if k not in sems:
                sems[k] = stack.enter_context(nc.semaphore("s_" + "_".join(str(x) for x in k)))
                counts[k] = 0
            return sems[k]

        for o in ops:
            if o.is_dma:
                k = ("d", o.key)
                s = get_sem(k)
                counts[k] += 16
                o.sig = (k, s, counts[k])
            elif o.signal:
                k = ("e", o.eng)
                s = get_sem(k)
                counts[k] += 1
                o.sig = (k, s, counts[k])
        finals = [(k, sems[k], counts[k]) for k in sems if k[0] == "d"]
        block = stack.enter_context(nc.Block())
        for en in self.ENGS:
            eops = [o for o in ops if o.eng == en]

            def body(e, eops=eops, en=en):
                waited = {}
                for o in eops:
                    dl = [d for d in o.deps if not (d.eng == "pe" and en == "pe" and not d.is_dma and o.fn is not None)]
                    dl += [d for d in o.bdeps if d.is_dma or d.eng != en]
                    for d in sorted(dl, key=lambda d: d.idx):
                        if d.sig is None:
                            continue
                        k, s, v = d.sig
                        if waited.get(k, 0) < v:
                            e.wait_ge(s, v)
                            waited[k] = v
                    if o.fn is None:
                        continue
                    inst = o.fn(e)
                    if o.sig is not None:
                        k, s, v = o.sig
                        inst.then_inc(s, 16 if o.is_dma else 1)
                if en == final_wait_eng:
                    for k, s, v in finals:
                        if waited.get(k, 0) < v:
                            e.wait_ge(s, v)

            getattr(block, self.BLOCK_NAME[en])(body)
        return len(sems)


def seq(fns):
    def f(e):
        i = None
        for fn in fns:
            i = fn(e)
        return i
    return f


def MM(out, lhsT, rhs, start=True, stop=True):
    return lambda e: e.matmul(out, lhsT=lhsT, rhs=rhs, start=start, stop=stop)


def TR(out, in_, ident):
    return lambda e: e.transpose(out=out, in_=in_, identity=ident)


IN_SIZES = (512, 1024, 256, 256, 256, 256, 256, 256, 48, 1024, 1024)
_sp = np.cumsum((0,) + IN_SIZES)
COL = {n: (int(_sp[i]), int(_sp[i + 1])) for i, n in enumerate(
    ["pool", "q", "kc", "vc", "ks", "vs", "kw", "vw", "gnsa", "gpool", "gattn"])}


def _shared_inputs(inp):
    f = np.float32
    w_in = np.asarray(inp["w_in"], f)[0]

    def cols(name, lo, hi):
        a, _ = COL[name]
        return w_in[:, a + lo:a + hi]

    wkv = np.stack([np.concatenate([cols(n, 64 * g, 64 * g + 64) for n in ("kc", "ks", "vc", "kw", "vs", "vw")], axis=1)
                    for g in range(4)], 0)
    wq = np.stack([np.concatenate([cols("q", 256 * g, 256 * g + 256), cols("gnsa", 12 * g, 12 * g + 12)], axis=1)
                   for g in range(4)], 0)
    sh = {
        "wkv": np.ascontiguousarray(wkv), "wq": np.ascontiguousarray(wq),
        "wpool": np.ascontiguousarray(cols("pool", 0, 512)),
        "wgp": np.ascontiguousarray(cols("gpool", 0, 1024)),
        "wga": np.ascontiguousarray(cols("gattn", 0, 1024)),
        "n1w": np.asarray(inp["norm1_w"], f)[0], "n2w": np.asarray(inp["norm2_w"], f)[0],
        "nfw": np.asarray(inp["norm_f_w"], f),
        "poolw": np.asarray(inp["pool_w"], f)[0],
        "psc": np.ascontiguousarray(np.asarray(inp["pool_scale"], f)[0].reshape(4, 128).T),
        "w1k": np.asarray(inp["cmp_w1_k"], f)[0], "w1v": np.asarray(inp["cmp_w1_v"], f)[0],
        "b1t": np.ascontiguousarray(np.concatenate([np.asarray(inp["cmp_b1_k"], f)[0].reshape(2, 128).T,
                                                    np.asarray(inp["cmp_b1_v"], f)[0].reshape(2, 128).T], axis=1)),
        "w2k": np.asarray(inp["cmp_w2_k"], f)[0], "w2v": np.asarray(inp["cmp_w2_v"], f)[0],
        "pekT": np.ascontiguousarray(np.asarray(inp["cmp_pe_k"], f)[0].T),
        "pevT": np.ascontiguousarray(np.asarray(inp["cmp_pe_v"], f)[0].T),
        "wpp": np.asarray(inp["w_proj_pool"], f)[0], "wpa": np.asarray(inp["w_proj_attn"], f)[0],
        "wout": np.asarray(inp["w_out"], f)[0],
        "wg": np.asarray(inp["w_ffn_gate"], f)[0], "wu": np.asarray(inp["w_ffn_up"], f)[0],
        "wd": np.asarray(inp["w_ffn_down"], f)[0],
    }
    sh["ident"] = np.eye(128, dtype=f)
    k = np.arange(S)
    sh["epat"] = ((k[None, :] // 64) % 64 == np.arange(64)[:, None]).astype(f)
    z = np.zeros((128, 224), f)
    z[np.arange(32), 96 + np.arange(32)] = 1.0
    z[32, 128:] = 1.0
    sh["zsel"] = z
    n = np.arange(512)
    mm = np.zeros((512, 128), f)
    for nn in range(511):
        mm[nn, (16 * nn) // 64] = 1.0
        mm[nn, (16 * nn + 31) // 64] = 1.0
    sh["mmap"] = np.ascontiguousarray(mm.reshape(4, 128, 128).transpose(1, 0, 2))
    inv_freq = (1.0 / (np.float32(500000.0) ** (np.arange(0, 16, 2, dtype=f) / np.float32(16)))).astype(f)
    ang = np.arange(S, dtype=f)[:, None] * inv_freq[None, :]
    sh["_cos"] = np.cos(ang).astype(f)
    sh["_sin"] = np.sin(ang).astype(f)
    kk = np.arange(128)[:, None]
    qq = np.arange(128)[None, :]
    tri = np.where(kk <= qq, 0.0, NEG).astype(f)
    tri2 = np.where(kk > qq, 0.0, NEG).astype(f)
    sh["tm"] = np.ascontiguousarray(np.broadcast_to(tri[:, None, :], (128, 4, 128)).reshape(128, 512))
    sh["wm"] = np.ascontiguousarray(np.stack([np.broadcast_to(tri2[:, None, :], (128, 4, 128)).reshape(128, 512),
                                              np.broadcast_to(tri[:, None, :], (128, 4, 128)).reshape(128, 512)], axis=1))
    n32 = np.arange(32)[:, None, None]
    cm = np.where(16 * n32 + 31 <= 384 + np.arange(128)[None, None, :], 0.0, NEG).astype(f)
    cmf = np.zeros((128, 512), f)
    cmf[0:32] = np.broadcast_to(cm, (32, 4, 128)).reshape(32, 512)
    cmf[32] = NEG
    sh["cm"] = cmf
    ps = np.zeros((128, 128), f)
    for i in range(3):
        ps[i, :8 * (i + 1)] = 1.0
    sh["padsel"] = ps
    q1 = np.arange(128)[:, None]
    rel = np.arange(248)[None, :] - 120
    jt0 = 6 + (q1 >= 64)
    g_ = np.zeros((128, 248), f)
    g_[rel > jt0] = -BIG
    g_[(rel == jt0) | (rel == jt0 - 1)] = BIG
    sh["gtab"] = g_
    return sh


def _core_inputs(inp, sh, r):
    f = np.float32
    b, c = r // 4, r % 4
    x = np.asarray(inp["x"], f)
    xb = x[b]
    toks = (np.arange(NQ)[:, None] * 4 + c) * 128 + np.arange(128)[None, :]
    d = {k_: v for k_, v in sh.items() if not k_.startswith("_")}
    sft = 3 - c
    xkv = np.zeros((S, D), f)
    xkv[sft * 128:] = xb[:S - sft * 128]
    d["xkv"] = xkv
    d["xq"] = np.ascontiguousarray(xb[toks.reshape(-1)])
    ht = (np.arange(NQ)[:, None] * 4 + c) * 128 - 16 + np.arange(16)[None, :]
    xh = np.zeros((NQ * 16, D), f)
    valid = (ht >= 0).reshape(-1)
    xh[valid] = xb[ht.reshape(-1)[valid]]
    d["xh"] = xh
    d["cosq"] = np.ascontiguousarray(sh["_cos"][toks].transpose(1, 0, 2))
    d["sinq"] = np.ascontiguousarray(sh["_sin"][toks].transpose(1, 0, 2))
    kpos = np.clip(np.arange(S) - sft * 128, 0, None)
    d["cosk"] = np.ascontiguousarray(sh["_cos"][kpos].reshape(64, 128, 8).transpose(1, 0, 2))
    d["sink"] = np.ascontiguousarray(sh["_sin"][kpos].reshape(64, 128, 8).transpose(1, 0, 2))
    wm0 = np.zeros((128, 3, 512), f)
    for sl in range(3):
        if sl < sft:
            wm0[:, sl, :] = NEG
    d["wm0"] = wm0
    pm = np.zeros((128, 512), f)
    if sft > 0:
        pm[sft - 1] = NEG
    d["padm"] = pm
    g0 = np.zeros((128, 128), f)
    g0[:, :2 * sft] = -3 * BIG
    g0[:, 2 * sft] = BIG
    d["g0"] = g0
    wv = np.array([2, 4, 8, 16])[:, None]
    tt = c * 128 + np.arange(128)[None, :]
    invc = (1.0 / np.minimum(tt + 1, wv)).astype(f)
    d["invc"] = np.ascontiguousarray(np.broadcast_to(invc[None], (128, 4, 128)))
    return d


INPUT_SHAPES = {
    "xkv": [S, D], "xq": [2048, D], "xh": [256, D], "n1w": [D], "n2w": [D], "nfw": [D],
    "wkv": [4, D, 384], "wq": [4, D, 268], "wpool": [D, 512], "wgp": [D, D], "wga": [D, D],
    "poolw": [4, 128, 128], "psc": [128, 4], "w1k": [2048, 256], "w1v": [2048, 256],
    "b1t": [128, 4], "w2k": [256, 64], "w2v": [256, 64], "pekT": [64, 32], "pevT": [64, 32],
    "wpp": [512, D], "wpa": [D, D], "wout": [D, D], "wg": [D, DFF], "wu": [D, DFF], "wd": [DFF, D],
    "ident": [128, 128], "epat": [64, S], "zsel": [128, 224], "mmap": [128, 4, 128],
    "cosk": [128, 64, 8], "sink": [128, 64, 8], "cosq": [128, 16, 8], "sinq": [128, 16, 8],
    "tm": [128, 512], "wm": [128, 2, 512], "wm0": [128, 3, 512], "cm": [128, 512], "gtab": [128, 248], "invc": [128, 4, 128],
    "padsel": [128, 128], "padm": [128, 512], "g0": [128, 128],
}

ARENA_BYTES = 206 * 1024


def A_ACT(out, in_, func, **kw):
    return lambda e: e.activation(out=out, in_=in_, func=func, **kw)


def A_CP(out, in_):
    return lambda e: e.copy(out=out, in_=in_)


def V_CP(out, in_):
    return lambda e: e.tensor_copy(out=out, in_=in_)


def V_TT(out, a, b, op):
    return lambda e: e.tensor_tensor(out=out, in0=a, in1=b, op=op)


def V_TS(out, in0, s1, s2, op0, op1=None):
    if op1 is None:
        return lambda e: e.tensor_scalar(out=out, in0=in0, scalar1=s1, scalar2=None, op0=op0)
    return lambda e: e.tensor_scalar(out=out, in0=in0, scalar1=s1, scalar2=s2, op0=op0, op1=op1)


def V_STT(out, in0, scalar, in1, op0, op1):
    return lambda e: e.scalar_tensor_tensor(out=out, in0=in0, scalar=scalar, in1=in1, op0=op0, op1=op1)


def V_REC(out, in_):
    return lambda e: e.reciprocal(out=out, in_=in_)


def V_MEMSET(ap, val):
    return lambda e: e.memset(ap, val)


def V_MAX(out, in_):
    return lambda e: e.max(out=out, in_=in_)


def V_MR(out, rep, vals, imm):
    return lambda e: e.match_replace(out=out, in_to_replace=rep, in_values=vals, imm_value=imm)


def DMA(out, in_):
    return lambda e: e.dma_start(out=out, in_=in_)


def build_program(n_groups=4, phase3=True, dbg=False, n_qblocks=NQ):
    nc = bass.Bass("TRN2", target_bir_lowering=False)
    I = {n: nc.dram_tensor(n, shp, F32, kind="ExternalInput").ap() for n, shp in INPUT_SHAPES.items()}
    out_d = nc.dram_tensor("out", [2048, D], F32, kind="ExternalOutput").ap()
    scr = nc.dram_tensor("htq_scr", [18, 128, 1024], BF16, kind="Internal").ap()
    scr2 = nc.dram_tensor("hkv_scr", [NT, 128, 1024], BF16, kind="Internal").ap()
    st = ExitStack()
    with st:
        arena = st.enter_context(nc.sbuf_tensor("arena", [128, ARENA_BYTES // 4], F32))
        banks = [st.enter_context(nc.psum_tensor("bank%d" % i, [128, 512], F32)) for i in range(8)]
        P = Prog(nc)

        def carve(off, shape, dt, p0=0):
            esz = 4 if dt == F32 else 2
            n = int(np.prod(shape[1:]))
            nb = n * esz
            assert off % 4 == 0 and nb % 4 == 0, (off, shape)
            assert off + nb <= ARENA_BYTES, (off, shape)
            ap = arena[p0:p0 + shape[0], off // 4:(off + nb) // 4]
            if dt != F32:
                ap = ap.bitcast(dt)
            if len(shape) == 3:
                ap = ap.rearrange("p (a b) -> p a b", a=shape[1])
            elif len(shape) == 4:
                ap = ap.rearrange("p (a b c) -> p a b c", a=shape[1], b=shape[2])
            return ap

        class Bump:
            def __init__(self, off):
                self.off = off

            def __call__(self, shape, dt, p0=0):
                esz = 4 if dt == F32 else 2
                nb = (int(np.prod(shape[1:])) * esz + 31) // 32 * 32
                ap = carve(self.off, shape, dt, p0)
                self.off += nb
                return ap

        def pbank(i, shape, dt, p0=0, col0=0):
            n = int(np.prod(shape[1:]))
            if dt == F32:
                ap = banks[i][p0:p0 + shape[0], col0:col0 + n]
            else:
                ap = banks[i][p0:p0 + shape[0], col0:col0 + n // 2].bitcast(dt)
            if len(shape) == 3:
                ap = ap.rearrange("p (a b) -> p a b", a=shape[1])
            elif len(shape) == 4:
                ap = ap.rearrange("p (a b c) -> p a b c", a=shape[1], b=shape[2])
            return ap

        dbg_list = []

        def dump(name, ap, shape, dt):
            if not dbg:
                return
            P.barrier()
            dd = nc.dram_tensor("dbg_" + name, list(shape), dt, kind="ExternalOutput").ap()
            P.dma("sp", DMA(dd, ap), key="dbg_" + name)
            P.barrier()
            dbg_list.append(name)

        def load(eng, dst, src, name):
            P.dma(eng, DMA(dst, src), writes=[name], key=name)

        A_ = Bump(0)
        identb = A_([128, 128], BF16)
        identf = A_([128, 128], F32)
        nw = A_([128, 1024], F32)
        cosk = A_([128, 64, 8], F32)
        sink = A_([128, 64, 8], F32)
        cosq = A_([128, 16, 8], F32)
        sinq = A_([128, 16, 8], F32)
        tmb = A_([128, 512], BF16)
        wmb = A_([128, 2, 512], BF16)
        wm0b = A_([128, 3, 512], BF16)
        padselb = A_([128, 128], BF16)
        padmb = A_([128, 512], BF16)
        g0t = A_([128, 128], F32)
        cmb = A_([128, 512], BF16)
        zsel = A_([128, 224], BF16)
        mmapb = A_([128, 4, 128], BF16)
        gtab = A_([128, 248], F32)
        invc = A_([128, 4, 128], F32)
        beff = A_([128, 4], F32)
        b1t = A_([128, 4], F32)
        peT = A_([128, 32], BF16)
        w2b = A_([128, 2, 2, 64], BF16)
        mhalf = A_([128, 1], F32)
        psc = A_([128, 4], F32)
        C_END = (A_.off + 1023) // 1024 * 1024
        attnT = carve(C_END, [128, 8, 2048], BF16)
        P0 = C_END + 32768

        load("pool", identb, I["ident"], "identb")
        load("sp", identf, I["ident"], "identf")
        load("sp", nw, I["n1w"].partition_broadcast(128), "nw")
        load("sp", cosk, I["cosk"], "cosk")
        load("sp", sink, I["sink"], "sink")
        load("sp", cosq, I["cosq"], "cosq")
        load("sp", sinq, I["sinq"], "sinq")
        load("pool", tmb, I["tm"], "tmb")
        load("pool", wmb, I["wm"], "wmb")
        load("pool", wm0b, I["wm0"], "wm0b")
        load("pool", padselb, I["padsel"], "padselb")
        load("pool", padmb, I["padm"], "padmb")
        load("sp", g0t, I["g0"], "g0t")
        load("pool", cmb, I["cm"], "cmb")
        load("pool", zsel, I["zsel"], "zsel")
        load("pool", mmapb, I["mmap"], "mmapb")
        load("sp", gtab, I["gtab"], "gtab")
        load("sp", invc, I["invc"], "invc")
        load("sp", b1t, I["b1t"], "b1t")
        load("sp", psc, I["psc"], "psc")
        load("pool", peT[0:64, :], I["pekT"], "pek")
        load("pool", peT[64:128, :], I["pevT"], "pev")
        load("pool", w2b[:, 0, :, :], I["w2k"].rearrange("(j p) d -> p j d", p=128), "w2k")
        load("pool", w2b[:, 1, :, :], I["w2v"].rearrange("(j p) d -> p j d", p=128), "w2v")
        P.op("pool", V_MEMSET(mhalf, -0.5), writes=["mhalf"])

        B1 = Bump(P0)
        KA = B1([128, S], BF16)
        AT_ = B1([128, S], BF16)
        BW = B1([128, S], BF16)
        VSW = B1([128, 64, 2, 66], BF16)
        VC = B1([128, 4, 66], BF16)
        KCT = B1([128, 512], BF16)
        WKV = B1([128, 8, 384], BF16)
        WQ = B1([128, 8, 268], BF16)
        X0 = B1.off
        X1 = Bump(X0)
        XB = [X1([128, 1024], F32) for _ in range(4)]
        HB = [X1([128, 1024], BF16) for _ in range(3)]
        HT = [X1([128, 8, 128], BF16) for _ in range(4)]
        JUNK = X1([128, 1024], BF16)
        SSQ = [X1([128, 4], F32) for _ in range(4)]
        KVB = [X1([128, 2, 2, 64], BF16) for _ in range(3)]
        RT = [X1([128, 4, 2, 8], F32) for _ in range(3)]
        W1 = X1([128, 32, 256], BF16)
        GX = X1([128, 512], F32)
        GU = X1([128, 512], F32)
        GS = X1([128, 512], F32)
        HID = X1([128, 2, 2, 512], BF16)
        assert X1.off <= ARENA_BYTES, X1.off
        X2 = Bump(X0)
        HQ = [X2([128, 8, 128], BF16) for _ in range(2)]
        QPR = [X2([128, 4, 2, 64], BF16) for _ in range(2)]
        RS = 6
        GSB = [X2([128, 12], F32) for _ in range(RS)]
        GEX = X2([128, 12], F32)
        RT2 = [X2([128, 4, 4, 8], F32) for _ in range(2)]
        QPT = [X2([128, 512], BF16) for _ in range(2)]
        QA2 = [X2([128, 2, 512], BF16) for _ in range(RS)]
        PC = [X2([128, 512], BF16) for _ in range(4)]
        PT = [X2([128, 512], BF16) for _ in range(4)]
        OT = X2([128, 3, 512], F32)
        OTW = [X2([128, 512], F32) for _ in range(2)]
        IMPV = X2([128, 128], F32)
        V2 = X2([128, 128], F32)
        SELA = X2([128, 128], F32)
        SELB = X2([128, 128], F32)
        WK = X2([128, 128], F32)
        NMP = X2([128, 2, 128], BF16)
        M8 = X2([128, 16], F32)
        DEN = X2([128, 3, 4], F32)
        RDEN = X2([128, 3, 4], F32)
        COEF = X2([128, 3, 4], F32)
        ACC = [X2([128, 4, 64], F32) for _ in range(RS)]
        ATB = X2([128, 256], BF16)
        assert X2.off <= ARENA_BYTES, X2.off

        load("pool", KA[0:64, :], I["epat"], "KA_E")
        P.op("pool", V_MEMSET(VSW[:, :, :, 64:66], 1.0), writes=["VSW_ones"])
        P.op("pool", V_MEMSET(VC[:, :, 64:66], 1.0), writes=["VC_ones"])
        P.op("pool", V_MEMSET(KCT[:, 508:512], 0.0), writes=["KCT_pad"])
        P.op("pool", V_MEMSET(KCT[64:128, :], 0.0), writes=["KCT_z"])
        P.op("pool", V_MEMSET(BW[0:64, :], 0.0), writes=["BW_z"])

        def norm_dma(src_dram, xi):
            P.dma("sp", DMA(XB[xi], src_dram), writes=["xb%d" % xi], key="xb%d" % xi)

        def norm_a1(xi):
            xb, ssq = XB[xi], SSQ[xi]
            xn = "xb%d" % xi
            P.op("act", A_ACT(JUNK, xb, AF.Square, accum_out=ssq[:, 0:1]), reads=[xn], writes=["ss%d_0" % xi])
            P.op("dve", V_TS(ssq[:, 1:2], ssq[:, 0:1], 1.0 / D, EPS, ALU.mult, ALU.add), reads=["ss%d_0" % xi], writes=["ss%d_1" % xi])
            P.op("pool", V_TT(ssq[:, 2:3], ssq[:, 1:2], mhalf, ALU.pow), reads=["ss%d_1" % xi, "mhalf"], writes=["ss%d_2" % xi])

        def norm_a2(xi, hi):
            xb, ssq, hb = XB[xi], SSQ[xi], HB[hi]
            P.op("dve", V_STT(hb, xb, ssq[:, 2:3], nw, ALU.mult, ALU.mult),
                 reads=["xb%d" % xi, "ss%d_2" % xi, "nw"], writes=["hb%d" % hi])

        def norm_b(hbi, hti, bank):
            hb, hts = HB[hbi], HT[hti]
            tv = pbank(bank, [128, 8, 128], BF16)
            P.op("pe", seq([TR(tv[:, k, :], hb[:, k * 128:(k + 1) * 128], identb) for k in range(8)]),
                 reads=["hb%d" % hbi, "identb"], banks=[bank])
            P.op("act", A_CP(hts, tv), writes=["hT%d" % hti], banks=[bank])

        def rope_ops(psrc, dst, cos_ap, sin_ap, rt, nh, rd, wr, rtname, bank):
            cb = cos_ap.unsqueeze(1).to_broadcast([128, nh, 8])
            sb_ = sin_ap.unsqueeze(1).to_broadcast([128, nh, 8])
            x1 = psrc[:, :, 0:8]
            x2 = psrc[:, :, 8:16]
            P.op("dve", seq([V_TT(rt[:, 0], x1, cb, ALU.mult), V_TT(rt[:, 1], x2, sb_, ALU.mult),
                             V_TT(rt[:, 2], x2, cb, ALU.mult), V_TT(rt[:, 3], x1, sb_, ALU.mult)]),
                 reads=rd, writes=[rtname], banks=[bank])
            P.op("dve", seq([V_TT(dst[:, :, 0:8], rt[:, 0], rt[:, 1], ALU.subtract),
                             V_TT(dst[:, :, 8:16], rt[:, 2], rt[:, 3], ALU.add)]),
                 reads=[rtname], writes=[wr])

        def pre_src(blk):
            return I["xq"][blk * 128:(blk + 1) * 128, :] if blk < 16 else I["xh"][(blk - 16) * 128:(blk - 15) * 128, :]

        norm_dma(pre_src(0), 0)
        norm_dma(pre_src(1), 1)
        for s_ in range(18 + 3):
            if s_ + 2 < 18:
                norm_dma(pre_src(s_ + 2), (s_ + 2) % 4)
            if s_ < 18:
                norm_a1(s_ % 4)
                norm_a2(s_ % 4, s_ % 3)
            if 0 <= s_ - 2 < 18:
                t_ = s_ - 2
                norm_b(t_ % 3, t_ % 4, t_ % 2)
            if 0 <= s_ - 3 < 18:
                blk = s_ - 3
                P.dma("sp", DMA(scr[blk].rearrange("p (k t) -> p k t", k=8), HT[blk % 4]),
                      reads=["hT%d" % (blk % 4)], writes=["scr%d" % blk], key="scrw%d" % (blk % 4))

        S_BANKS = (0, 1, 7)
        B_OC, B_OS, B_OW, B_I, B_QT, B_F = 2, 3, 4, 5, 6, 6
        tile_ctr = [0]

        def step_gen(gen):
            if gen is None:
                return None
            try:
                next(gen)
                return gen
            except StopIteration:
                return None

        def exhaust(gen):
            while gen is not None:
                gen = step_gen(gen)

        LAG = 2
        pt_ctr = [0]

        def chain(*gens):
            for g_ in gens:
                if g_ is not None:
                    yield from g_

        def tiles_gen(tiles):
            n = len(tiles)
            for ti in range(n + LAG):
                if ti < n:
                    T = tiles[ti]
                    sbk = S_BANKS[tile_ctr[0] % 3]
                    tile_ctr[0] += 1
                    if T["pbuf"] is None:
                        T["pbuf"] = PT[pt_ctr[0] % 4]
                        T["pname"] = "pt%d" % (pt_ctr[0] % 4)
                        pt_ctr[0] += 1
                    so = pbank(sbk, [128, 512], F32)
                    nq = len(T["qk"])
                    P.op("pe", seq([MM(so, a, b, qi == 0, qi == nq - 1) for qi, (a, b) in enumerate(T["qk"])]),
                         reads=T["rd"], banks=[sbk])
                    P.op("act", A_ACT(T["pbuf"], so, AF.Exp, scale=0.125), writes=[T["pname"]], banks=[sbk])
                if ti - LAG >= 0:
                    pend = tiles[ti - LAG]
                    lhsT, obank, first, last = pend["pv"]
                    oo = pbank(obank, [65, 512], F32)
                    P.op("pe", MM(oo, lhsT, pend["pbuf"], first, last), reads=[pend["pname"]] + pend["pvrd"], banks=[obank])
                    if pend.get("post") is not None:
                        pend["post"]()
                yield

        def run_tiles(tiles, side=None, stride=1):
            k = 0
            for _ in tiles_gen(tiles):
                k += 1
                if side is not None and k % stride == 0:
                    side = step_gen(side)
            return side

        for g in range(n_groups):
            load("pool", WKV, I["wkv"][g].rearrange("(kc kp) n -> kp kc n", kp=128), "WKV")
            load("pool", WQ, I["wq"][g].rearrange("(kc kp) n -> kp kc n", kp=128), "WQ")
            load("pool", W1[0:64], I["w1k"].rearrange("(l d) j -> d l j", d=64), "W1k")
            load("pool", W1[64:128], I["w1v"].rearrange("(l d) j -> d l j", d=64), "W1v")
            P.op("pool", V_MEMSET(HID[:, :, :, 508:512], 0.0), writes=["HID_pad"])
            def kv_s1(t, hi):
                j2 = t % 2
                j3 = t % 3
                pb = 2 + j2
                pv = pbank(pb, [128, 3, 2, 64], F32)
                P.op("pe", seq([MM(pbank(pb, [128, 384], F32), HT[hi][:, k, :], WKV[:, k, :], k == 0, k == 7) for k in range(8)]),
                     reads=["hT%d" % hi, "WKV"], banks=[pb])
                kvb = KVB[j3]
                P.op("act", seq([A_CP(kvb[:, :, 0, :], pv[:, 0:2, 0, :]), A_CP(kvb[:, :, 1, 16:64], pv[:, 0:2, 1, 16:64]),
                                 A_CP(VSW[:, t, :, 0:64], pv[:, 2, :, :])]),
                     reads=["VSW_ones"], writes=["kvb%d_a" % j3, "VSW"], banks=[pb])
                rope_ops(pv[:, 0:2, 1, :], kvb[:, :, 1, :], cosk[:, t, :], sink[:, t, :], RT[j3], 2,
                         ["cosk", "sink"], "kvb%d_r" % j3, "rt%d" % j3, pb)

            def kv_s2(t):
                j2 = t % 2
                j3 = t % 3
                kvb = KVB[j3]
                kb = 4 + j2
                kv = pbank(kb, [128, 3, 128], BF16)
                kflat = kvb.rearrange("p a b d -> p (a b d)")
                P.op("pe", seq([TR(kv[:, 0, :], kflat[:, 0:128], identb), TR(kv[:, 1, :], kflat[:, 128:256], identb),
                                TR(kv[:, 2, :], kflat[:, 64:192], identb)]),
                     reads=["kvb%d_a" % j3, "kvb%d_r" % j3, "identb"], banks=[kb])
                cs = slice(t * 128, (t + 1) * 128)
                atd = AT_.rearrange("p (s n) -> p s n", s=16)[:, :, t * 8:(t + 1) * 8]
                P.op("dve", seq([V_CP(atd[0:64], kv[0:64, 0, :].rearrange("p (n s) -> p s n", s=16)),
                                 V_CP(KA[64:128, cs], kv[64:128, 0, :])]),
                     writes=["AT_lo", "KA_hi"], banks=[kb])
                P.op("act", seq([A_CP(BW[64:128, cs], kv[64:128, 1, :]),
                                 A_CP(atd[64:128], kv[64:128, 2, :].rearrange("p (n s) -> p s n", s=16))]),
                     writes=["BW_hi", "AT_hi"], banks=[kb])

            def kv_src(t):
                return I["xkv"][t * 128:(t + 1) * 128, :]

            if g == 0:
                norm_dma(kv_src(0), 0)
                norm_dma(kv_src(1), 1)
                for s_ in range(NT + 5):
                    if s_ + 2 < NT:
                        norm_dma(kv_src(s_ + 2), (s_ + 2) % 4)
                    if s_ < NT:
                        norm_a1(s_ % 4)
                        norm_a2(s_ % 4, s_ % 3)
                    if 0 <= s_ - 2 < NT:
                        t_ = s_ - 2
                        norm_b(t_ % 3, t_ % 4, t_ % 2)
                    if 0 <= s_ - 4 < NT:
                        t_ = s_ - 4
                        P.dma("sp", DMA(scr2[t_].rearrange("p (k t) -> p k t", k=8), HT[t_ % 4]),
                              reads=["hT%d" % (t_ % 4)], writes=["scr2_%d" % t_], key="scr2w%d" % (t_ % 4))
                        kv_s1(t_, t_ % 4)
                    if 0 <= s_ - 5 < NT:
                        kv_s2(s_ - 5)
            else:
                def ht_load(t_):
                    P.dma("sp", DMA(HT[t_ % 4], scr2[t_].rearrange("p (k t) -> p k t", k=8)),
                          writes=["hT%d" % (t_ % 4)], key="hTl%d" % (t_ % 4))
                ht_load(0)
                ht_load(1)
                ht_load(2)
                for s_ in range(NT + 2):
                    if s_ + 3 < NT:
                        ht_load(s_ + 3)
                    if s_ < NT:
                        kv_s1(s_, s_ % 4)
                    if 0 <= s_ - 2 < NT:
                        kv_s2(s_ - 2)

            if g == 0:
                for kv_ in range(2):
                    rows = slice(64 * kv_, 64 * kv_ + 64)
                    for jh in range(2):
                        bo = pbank(6 + kv_, [128, 1], F32, col0=2 * jh)
                        P.op("pe", seq([MM(bo, W1[rows, l, jh * 128:(jh + 1) * 128], peT[rows, l:l + 1], l == 0, l == 31)
                                        for l in range(32)]),
                             reads=["W1k", "W1v", "pek", "pev"], banks=[6 + kv_])
                for kv_ in range(2):
                    P.op("dve", V_TT(beff[:, 2 * kv_:2 * kv_ + 2], pbank(6 + kv_, [128, 2, 2], F32)[:, :, 0], b1t[:, 2 * kv_:2 * kv_ + 2], ALU.add),
                         reads=["b1t"], writes=["beff%d" % kv_], banks=[6 + kv_])
            for kv_ in range(2):
                rows = slice(64 * kv_, 64 * kv_ + 64)
                src = AT_[rows, :].rearrange("p (s n) -> p s n", s=16)
                for jh in range(2):
                    hb_ = 6 + jh
                    ho = pbank(hb_, [128, 511], F32)
                    mmz = []
                    for l in range(32):
                        rhs = src[:, l, 0:511] if l < 16 else src[:, l - 16, 1:512]
                        mmz.append(MM(ho, W1[rows, l, jh * 128:(jh + 1) * 128], rhs, l == 0, l == 31))
                    P.op("pe", seq(mmz), reads=["AT_lo", "AT_hi", "W1k", "W1v"], banks=[hb_])
                    gx, gu, gs_ = GX[:, 0:511], GU[:, 0:511], GS[:, 0:511]
                    bcol = beff[:, kv_ * 2 + jh:kv_ * 2 + jh + 1]
                    P.op("act", A_ACT(gx, ho, AF.Identity, bias=bcol), reads=["beff0", "beff1"], writes=["GX"], banks=[hb_])
                    P.op("pool", V_TT(gu, gx, gx, ALU.mult), reads=["GX"], writes=["GU"])
                    P.op("dve", V_TS(gu, gu, 0.044715, 1.0, ALU.mult, ALU.add), reads=["GU"], writes=["GU"])
                    P.op("pool", V_TT(gu, gu, gx, ALU.mult), reads=["GU", "GX"], writes=["GU"])
                    P.op("act", A_ACT(gs_, gu, AF.Sigmoid, scale=1.5957691216057308), reads=["GU"], writes=["GS"])
                    P.op("dve", V_TT(HID[:, kv_, jh, 0:511], gx, gs_, ALU.mult),
                         reads=["GX", "GS", "HID_pad"], writes=["HID%d%d" % (kv_, jh)])
            ko = pbank(6, [64, 511], F32)
            P.op("pe", seq([MM(ko, w2b[:, 0, jh, :], HID[:, 0, jh, 0:511], jh == 0, jh == 1) for jh in range(2)]),
                 reads=["HID00", "HID01", "w2k"], banks=[6])
            P.op("dve", V_CP(KCT[0:64, 0:511], ko), reads=["KCT_pad", "KCT_z"], writes=["KCT"], banks=[6])
            vo = pbank(7, [128, 4, 64], F32)
            P.op("pe", seq([MM(vo[:, nt_, :], HID[:, 1, jh, nt_ * 128:(nt_ + 1) * 128], w2b[:, 1, jh, :], jh == 0, jh == 1)
                            for nt_ in range(4) for jh in range(2)]),
                 reads=["HID10", "HID11", "w2v", "HID_pad"], banks=[7])
            P.op("dve", V_CP(VC[:, :, 0:64], vo), reads=["VC_ones"], writes=["VC"], banks=[7])
            if g == 0:
                dump("kct", KCT, [128, 512], BF16)
                dump("vc", VC, [128, 4, 66], BF16)
                dump("ka", KA[:, 0:1024], [128, 1024], BF16)
                dump("bw", BW[:, 0:1024], [128, 1024], BF16)
                dump("vsw", VSW[:, 0:8], [128, 8, 2, 66], BF16)
            P.barrier()

            P.op("pool", V_MEMSET(NMP, 0.0), writes=["NMP_z"])
            for i2 in range(2):
                P.op("pool", V_MEMSET(QPT[i2][64:128, :], 0.0), writes=["qpt%d_z" % i2])
            def ot_buf(j, i2):
                if j == 2:
                    return OTW[i2][0:65, :], "OTW%d" % i2
                return OT[0:65, j, :], "OT%d" % j

            def finalize_a(j, obank, i2=0):
                oo = pbank(obank, [65, 512], F32)
                dst, dn = ot_buf(j, i2)
                P.op("dve", V_CP(dst, oo), writes=[dn], banks=[obank])

            def finalize_b1(j, fbank, m):
                fo = pbank(fbank, [128, 4, 65], F32)
                src_, sn_ = ot_buf(j, m % 2)
                P.op("pe", seq([TR(fo[:, h, :], src_[:, h * 128:(h + 1) * 128], identf[0:65, 0:65]) for h in range(4)]),
                     reads=[sn_, "identf"], banks=[fbank])
                P.op("dve", V_TS(DEN[:, j, :], fo[:, :, 64], 1e-30, None, ALU.max), writes=["den%d" % j], banks=[fbank])
                P.op("dve", V_REC(RDEN[:, j, :], DEN[:, j, :]), reads=["den%d" % j], writes=["rden%d" % j])

            def finalize_b2(j, fbank, first, m):
                sl = m % RS
                gsb, acc = GSB[sl], ACC[sl]
                fo = pbank(fbank, [128, 4, 65], F32)
                P.op("dve", V_TT(COEF[:, j, :], RDEN[:, j, :], gsb.rearrange("p (h j) -> p j h", j=3)[:, j, :], ALU.mult),
                     reads=["rden%d" % j, "gsb%d" % sl], writes=["coef%d" % j])
                fns = []
                for h in range(4):
                    dst = ATB[:, h * 64:(h + 1) * 64] if j == 2 else acc[:, h, :]
                    if first:
                        fns.append(V_TS(dst, fo[:, h, 0:64], COEF[:, j, h:h + 1], None, ALU.mult))
                    else:
                        fns.append(V_STT(dst, fo[:, h, 0:64], COEF[:, j, h:h + 1], acc[:, h, :], ALU.mult, ALU.add))
                P.op("dve", seq(fns), reads=["coef%d" % j, "ACC%d" % sl], writes=["ATB"] if j == 2 else ["ACC%d" % sl], banks=[fbank])

            def PRO(m):
                i2 = m % 2
                sl = m % RS
                hq, qpr, qpt = HQ[i2], QPR[i2], QPT[i2]
                gsb, qa2 = GSB[sl], QA2[sl]
                P.dma("sp", DMA(hq, scr[m].rearrange("p (k t) -> p k t", k=8)), writes=["hq%d" % i2], key="hq%d" % i2)
                qo = pbank(B_QT, [128, 268], F32)
                P.op("pe", seq([MM(qo, hq[:, k, :], WQ[:, k, :], k == 0, k == 7) for k in range(8)]),
                     reads=["hq%d" % i2, "WQ"], banks=[B_QT])
                qv = pbank(B_QT, [128, 4, 64], F32)
                P.op("act", A_ACT(GEX, qo[:, 256:268], AF.Exp, scale=-1.0), writes=["GEX"], banks=[B_QT])
                P.op("dve", seq([V_CP(qpr[:, :, 0, :], qv), V_CP(qpr[:, :, 1, 16:64], qv[:, :, 16:64])]),
                     writes=["qpr%d_a" % i2], banks=[B_QT])
                P.op("dve", V_TS(GEX, GEX, 1.0, None, ALU.add), reads=["GEX"], writes=["GEX"])
                P.op("dve", V_REC(gsb, GEX), reads=["GEX"], writes=["gsb%d" % sl])
                rope_ops(qv, qpr[:, :, 1, :], cosq[:, m, :], sinq[:, m, :], RT2[i2], 4, ["cosq", "sinq"],
                         "qpr%d_r" % i2, "rt2_%d" % i2, B_QT)
                yield
                tq_ = pbank(B_QT, [128, 4, 128], BF16)
                P.op("pe", seq([TR(tq_[:, h, :], qpr[:, h, :, :].rearrange("p a d -> p (a d)"), identb) for h in range(4)]),
                     reads=["qpr%d_a" % i2, "qpr%d_r" % i2, "identb"], banks=[B_QT])
                P.op("dve", V_CP(qpt[0:64, :], tq_[0:64].rearrange("p h q -> p (h q)")),
                     reads=["qpt%d_z" % i2], writes=["qpt%d" % i2], banks=[B_QT])
                P.op("dve", V_CP(qa2[64:128], tq_[64:128].rearrange("p h q -> p (h q)").unsqueeze(1).to_broadcast([64, 2, 512])),
                     writes=["qa2%d_q" % sl], banks=[B_QT])
                yield
                ncmp = 32 * m + 32
                T_ = (ncmp + 127) // 128
                tiles = []
                for tt in range(T_):
                    n0 = tt * 128
                    qk = [(KCT[:, n0:n0 + 128], qpt)]
                    rd = ["KCT", "qpt%d" % i2]
                    if tt == T_ - 1:
                        off = ncmp - n0 - 32
                        qk.append((zsel[:, 96 - off:224 - off], cmb))
                        rd += ["zsel", "cmb"]
                    if tt == 0:
                        qk.append((padselb, padmb))
                        rd += ["padselb", "padmb"]
                    tiles.append(dict(qk=qk, rd=rd, pbuf=PC[tt], pname="pc%d" % tt,
                                      pv=(VC[:, tt, 0:65], B_OC, tt == 0, tt == T_ - 1), pvrd=["VC"]))
                for _ in tiles_gen(tiles):
                    yield
                io = pbank(B_I, [128, 4, 128], F32)
                P.op("pe", seq([MM(io[:, h, :], PC[tt][:, h * 128:(h + 1) * 128], mmapb[:, tt, :], tt == 0, tt == T_ - 1)
                                for h in range(4) for tt in range(T_)]),
                     reads=["pc%d" % tt for tt in range(T_)] + ["mmapb"], banks=[B_I])
                finalize_a(0, B_OC)
                yield
                finalize_b1(0, B_F, m)
                nb = 8 * m + 8 if m < 8 else 128
                P.op("dve", V_TS(IMPV[:, 0:nb], io[:, 0, 0:nb], RDEN[:, 0, 0:1], None, ALU.mult), reads=["rden0"], writes=["IMPV"], banks=[B_I])
                for h in range(1, 4):
                    P.op("dve", V_STT(IMPV[:, 0:nb], io[:, h, 0:nb], RDEN[:, 0, h:h + 1], IMPV[:, 0:nb], ALU.mult, ALU.add),
                         reads=["rden0", "IMPV"], writes=["IMPV"], banks=[B_I])
                P.op("dve", V_TT(V2[:, 0:nb], IMPV[:, 0:nb], gtab[:, 120 - 8 * m:120 - 8 * m + nb], ALU.add), reads=["IMPV", "gtab"], writes=["V2"])
                P.op("dve", V_TT(V2[:, 0:nb], V2[:, 0:nb], g0t[:, 0:nb], ALU.add), reads=["V2", "g0t"], writes=["V2"])
                P.op("dve", V_MAX(M8[:, 0:8], V2[:, 0:nb]), reads=["V2"], writes=["M8a"])
                P.op("dve", V_MR(WK[:, 0:nb], M8[:, 0:8], V2[:, 0:nb], -3.0e38), reads=["V2", "M8a"], writes=["WK"])
                P.op("dve", V_MAX(M8[:, 8:16], WK[:, 0:nb]), reads=["WK"], writes=["M8b"])
                if m < 2:
                    P.op("dve", V_TS(M8[:, 15:16], M8[:, 15:16], -1.0e8, None, ALU.max), reads=["M8b"], writes=["M8b"])
                if nb <= 64:
                    P.op("dve", V_TS(NMP[:, 0, 0:nb], V2[:, 0:nb], M8[:, 15:16], NEG, ALU.is_lt, ALU.mult),
                         reads=["V2", "M8b", "NMP_z"], writes=["NMP"])
                else:
                    P.op("dve", V_TS(NMP[:, :, 0:64], V2.rearrange("p (a b) -> p a b", a=2), M8[:, 15:16], NEG, ALU.is_lt, ALU.mult),
                         reads=["V2", "M8b", "NMP_z"], writes=["NMP"])
                finalize_b2(0, B_F, True, m)
                yield
                yield
                yield
                nt2 = pbank(B_QT, [128, 2, 128], BF16, col0=384)
                P.op("pe", seq([TR(nt2[:, 0, :], NMP[:, 0, :], identb), TR(nt2[:, 1, :], NMP[:, 1, :], identb)]),
                     reads=["NMP", "identb"], banks=[B_QT])
                P.op("dve", V_CP(qa2[0:64].rearrange("p a (h q) -> p a h q", h=4),
                                 nt2[0:64].unsqueeze(2).to_broadcast([64, 2, 4, 128])),
                     writes=["qa2%d_m" % sl], banks=[B_QT])

            def FIN_bc(m):
                finalize_b1(1, B_F, m)
                finalize_b2(1, B_F, False, m)
                yield
                finalize_b1(2, B_F, m)
                finalize_b2(2, B_F, False, m)
                yield
                ao = pbank(B_QT, [128, 2, 128], BF16)
                P.op("pe", seq([TR(ao[:, 0, :], ATB[:, 0:128], identb), TR(ao[:, 1, :], ATB[:, 128:256], identb)]),
                     reads=["ATB", "identb"], banks=[B_QT])
                P.op("dve", V_CP(attnT[:, 2 * g:2 * g + 2, m * 128:(m + 1) * 128], ao), writes=["attnT"], banks=[B_QT])

            def win_tiles(m):
                sl = m % RS
                qa2 = QA2[sl]
                tiles = []
                js = [j for j in range(5) if 4 * m - 1 + j >= 0]
                for ji, j in enumerate(js):
                    kt = 4 * m - 1 + j
                    cs = slice(kt * 128, (kt + 1) * 128)
                    qk = [(BW[:, cs], qa2[:, 0, :])]
                    rd = ["BW_hi", "BW_z", "qa2%d_q" % sl, "qa2%d_m" % sl]
                    if j == 0:
                        qk.append((identb, wmb[:, 0, :]))
                        rd += ["identb", "wmb"]
                    elif j == 4:
                        qk.append((identb, wmb[:, 1, :]))
                        rd += ["identb", "wmb"]
                    elif m == 0:
                        qk.append((identb, wm0b[:, kt, :]))
                        rd += ["identb", "wm0b"]
                    tiles.append(dict(qk=qk, rd=rd, pbuf=None, pname=None,
                                      pv=(VSW[:, kt, 1, 0:65], B_OW, ji == 0, ji == len(js) - 1), pvrd=["VSW", "VSW_ones"]))
                return tiles

            def slc_tiles(m):
                sl = m % RS
                qa2 = QA2[sl]
                tiles = []
                nk = 4 * m + 4
                for kt in range(nk):
                    cs = slice(kt * 128, (kt + 1) * 128)
                    half = kt // 32
                    qk = [(KA[:, cs], qa2[:, half, :])]
                    rd = ["KA_E", "KA_hi", "qa2%d_q" % sl, "qa2%d_m" % sl]
                    if kt == nk - 1:
                        qk.append((identb, tmb))
                        rd += ["identb", "tmb"]
                    tiles.append(dict(qk=qk, rd=rd, pbuf=None, pname=None,
                                      pv=(VSW[:, kt, 0, 0:65], B_OS, kt == 0, kt == nk - 1), pvrd=["VSW", "VSW_ones"]))
                return tiles

            order = list(range(n_qblocks - 1, -1, -1))
            created = set()
            pros = []

            def side_step(queue):
                while queue:
                    try:
                        next(queue[0][1])
                        return
                    except StopIteration:
                        queue.pop(0)

            created.add(order[0])
            exhaust(PRO(order[0]))
            prev_fin = None
            for idx, m in enumerate(order):
                queue = []
                if prev_fin is not None:
                    queue.append(("fin", prev_fin))
                for mm in order[idx + 1: idx + 1 + (RS - 2)]:
                    if mm not in created:
                        created.add(mm)
                        pros.append((mm, PRO(mm)))
                queue += pros
                wt = win_tiles(m)
                wt[-1]["post"] = (lambda m=m: finalize_a(2, B_OW, m % 2))
                tl = wt + slc_tiles(m)
                k_ = 0
                for _ in tiles_gen(tl):
                    k_ += 1
                    if k_ % 2 == 0:
                        side_step(queue)
                need = 1 if prev_fin is not None else 0
                nxt = order[idx + 1] if idx + 1 < len(order) else None
                while queue and (queue[0][0] == "fin" or queue[0][0] == nxt):
                    try:
                        next(queue[0][1])
                    except StopIteration:
                        queue.pop(0)
                pros = [q_ for q_ in queue if q_[0] != "fin"]
                finalize_a(1, B_OS)
                prev_fin = FIN_bc(m)
            exhaust(prev_fin)
            P.barrier()

        dump("attnT", attnT, [128, 8, 2048], BF16)

        if phase3:
            B3 = Bump(P0)
            HTQ = B3([128, 8, 2048], BF16)
            HTH = B3([128, 8, 256], BF16)
            POOLT = B3([128, 4, 2048], BF16)
            R0 = B3.off
            Ra = Bump(R0)
            UT = Ra([128, 4, 16, 144], F32)
            PP = [Ra([128, 16, 144], F32) for _ in range(2)]
            WPOOL = Ra([128, 8, 512], BF16)
            POOLW = Ra([128, 4, 128], BF16)
            POOLED = Ra([128, 4, 2048], BF16)
            assert Ra.off <= ARENA_BYTES, Ra.off
            for m in range(NQ):
                P.dma("sp", DMA(HTQ[:, :, m * 128:(m + 1) * 128], scr[m].rearrange("p (k t) -> p k t", k=8)),
                      writes=["HTQ"], key="HTQ%d" % (m % 4))
            for hb_ in range(2):
                P.dma("sp", DMA(HTH[:, :, hb_ * 128:(hb_ + 1) * 128], scr[16 + hb_].rearrange("p (k t) -> p k t", k=8)),
                      writes=["HTH"], key="HTH%d" % hb_)
            load("pool", WPOOL, I["wpool"].rearrange("(kc kp) n -> kp kc n", kp=128), "WPOOL")
            load("pool", POOLW, I["poolw"].rearrange("g c d -> c g d"), "POOLW")
            for gp in range(4):
                un = "UT%d" % gp
                for tt in range(4):
                    bk = tt % 2
                    uo = pbank(bk, [128, 512], F32)
                    P.op("pe", seq([MM(uo, WPOOL[:, k, gp * 128:(gp + 1) * 128], HTQ[:, k, tt * 512:(tt + 1) * 512], k == 0, k == 7)
                                    for k in range(8)]), reads=["WPOOL", "HTQ"], banks=[bk])
                    P.op("act", A_CP(UT[:, gp, 4 * tt:4 * tt + 4, 16:144], uo.rearrange("p (a b) -> p a b", a=4)),
                         writes=[un], banks=[bk])
                uh = pbank(2, [128, 256], F32)
                P.op("pe", seq([MM(uh, WPOOL[:, k, gp * 128:(gp + 1) * 128], HTH[:, k, :], k == 0, k == 7) for k in range(8)]),
                     reads=["WPOOL", "HTH"], banks=[2])
                P.op("act", A_CP(UT[:, gp, :, 0:16], uh.rearrange("p (a b) -> p a b", a=16)), writes=[un], banks=[2])
            for gp in range(4):
                un = "UT%d" % gp
                cur = UT[:, gp]
                curn = un
                lo = 0
                for si in range(gp + 1):
                    stp = 1 << si
                    lo2 = lo + stp
                    dst = PP[si % 2]
                    dn = "pp%d" % (si % 2)
                    P.op("dve", V_TT(dst[:, :, lo2:144], cur[:, :, lo2:144], cur[:, :, lo2 - stp:144 - stp], ALU.add),
                         reads=[curn], writes=[dn])
                    cur, curn, lo = dst, dn, lo2
                wsz = 2 << gp
                pl = POOLED[:, gp].rearrange("p (a b) -> p a b", a=16)
                pn = "POOLED%d" % gp
                P.op("dve", V_STT(pl[:, 1:16, :], cur[:, 1:16, 16:144], 1.0 / wsz, UT[:, gp, 1:16, 16:144], ALU.mult, ALU.subtract),
                     reads=[curn, un], writes=[pn])
                P.op("dve", V_TT(cur[:, 0, 16:144], cur[:, 0, 16:144], invc[:, gp, :], ALU.mult),
                     reads=[curn, "invc", pn], writes=[curn])
                P.op("dve", V_TT(pl[:, 0, :], cur[:, 0, 16:144], UT[:, gp, 0, 16:144], ALU.subtract),
                     reads=[curn, un, pn], writes=[pn])
                for tt in range(4):
                    bk = 3 + tt % 2
                    mo = pbank(bk, [128, 512], F32)
                    P.op("pe", MM(mo, POOLW[:, gp, :], POOLED[:, gp, tt * 512:(tt + 1) * 512]), reads=["POOLW", pn], banks=[bk])
                    P.op("act", A_ACT(POOLT[:, gp, tt * 512:(tt + 1) * 512], mo, AF.Copy, scale=psc[:, gp:gp + 1]),
                         reads=["psc"], writes=["POOLT"], banks=[bk])
            dump("poolT", POOLT, [128, 4, 2048], BF16)
            P.barrier()
            Rb = Bump(R0)
            MERGED = Rb([128, 8, 2048], BF16)
            WST = [Rb([128, 28, 128], BF16) for _ in range(2)]
            SAB = [Rb([128, 2, 512], F32) for _ in range(2)]
            T12 = [Rb([128, 2, 512], F32) for _ in range(2)]
            WOUT = Rb([128, 8, 1024], BF16)
            assert Rb.off <= ARENA_BYTES, Rb.off
            for j in range(8):
                ws = WST[j % 2]
                cs = slice(j * 128, (j + 1) * 128)
                nm_ = "wst%d" % (j % 2)
                load("pool", ws[:, 0:8, :], I["wgp"][:, cs].rearrange("(kc kp) n -> kp kc n", kp=128), nm_ + "a")
                load("pool", ws[:, 8:16, :], I["wga"][:, cs].rearrange("(kc kp) n -> kp kc n", kp=128), nm_ + "b")
                load("pool", ws[:, 16:20, :], I["wpp"][:, cs].rearrange("(kc kp) n -> kp kc n", kp=128), nm_ + "c")
                load("pool", ws[:, 20:28, :], I["wpa"][:, cs].rearrange("(kc kp) n -> kp kc n", kp=128), nm_ + "d")
                for tt in range(4):
                    ts_ = slice(tt * 512, (tt + 1) * 512)
                    q4 = (j * 4 + tt) % 2
                    ba, bb, bc, bd = (0, 1, 2, 3) if q4 == 0 else (4, 5, 6, 7)
                    pa, pb2, pc_, pd = (pbank(x, [128, 512], F32) for x in (ba, bb, bc, bd))
                    P.op("pe", seq([MM(pa, ws[:, k, :], HTQ[:, k, ts_], k == 0, k == 7) for k in range(8)]), reads=[nm_ + "a", "HTQ"], banks=[ba])
                    P.op("pe", seq([MM(pb2, ws[:, 8 + k, :], HTQ[:, k, ts_], k == 0, k == 7) for k in range(8)]), reads=[nm_ + "b", "HTQ"], banks=[bb])
                    P.op("pe", seq([MM(pc_, ws[:, 16 + k, :], POOLT[:, k, ts_], k == 0, k == 3) for k in range(4)]), reads=[nm_ + "c", "POOLT"], banks=[bc])
                    P.op("pe", seq([MM(pd, ws[:, 20 + k, :], attnT[:, k, ts_], k == 0, k == 7) for k in range(8)]), reads=[nm_ + "d", "attnT"], banks=[bd])
                    sab, t12 = SAB[q4], T12[q4]
                    P.op("act", A_ACT(sab[:, 0, :], pa, AF.Sigmoid), writes=["sab%da" % q4], banks=[ba])
                    P.op("act", A_ACT(sab[:, 1, :], pb2, AF.Sigmoid), writes=["sab%db" % q4], banks=[bb])
                    P.op("dve", V_TT(t12[:, 0, :], pc_, sab[:, 0, :], ALU.mult), reads=["sab%da" % q4], writes=["t12%da" % q4], banks=[bc])
                    P.op("dve", V_TT(t12[:, 1, :], pd, sab[:, 1, :], ALU.mult), reads=["sab%db" % q4], writes=["t12%db" % q4], banks=[bd])
                    P.op("pool", V_TT(MERGED[:, j, ts_], t12[:, 0, :], t12[:, 1, :], ALU.add),
                         reads=["t12%da" % q4, "t12%db" % q4], writes=["MERGED"])
            load("pool", WOUT, I["wout"].rearrange("(kc kp) n -> kp kc n", kp=128), "WOUT")
            dump("merged", MERGED, [128, 8, 2048], BF16)
            P.barrier()
            X1T = carve(C_END, [128, 16, 1024], F32)
            XQB = [carve(C_END + 65536 + 4096 * i, [128, 1024], F32) for i in range(2)]
            for m in range(NQ):
                xq_ = XQB[m % 2]
                xn = "xq%d" % (m % 2)
                P.dma("sp", DMA(xq_, I["xq"][m * 128:(m + 1) * 128, :]), writes=[xn], key=xn)
                for nh in range(2):
                    bk = (m * 2 + nh) % 4
                    po = pbank(bk, [128, 512], F32)
                    P.op("pe", seq([MM(po, MERGED[:, k, m * 128:(m + 1) * 128], WOUT[:, k, nh * 512:(nh + 1) * 512], k == 0, k == 7)
                                    for k in range(8)]), reads=["MERGED", "WOUT"], banks=[bk])
                    P.op("dve", V_TT(X1T[:, m, nh * 512:(nh + 1) * 512], po, xq_[:, nh * 512:(nh + 1) * 512], ALU.add),
                         reads=[xn], writes=["X1T%d" % m], banks=[bk])
            load("sp", nw, I["n2w"].partition_broadcast(128), "nw")
            dump("x1", X1T, [128, 16, 1024], F32)
            P.barrier()
            Bd = Bump(C_END + 65536)
            H2T = Bd([128, 8, 1024], BF16)
            ACTT = Bd([128, 22, 1024], BF16)
            WDB = Bd([128, 22, 512], BF16)
            WGU = [[Bd([128, 8, 256], BF16) for _ in range(2)] for _ in range(2)]
            HB2 = [Bd([128, 1024], BF16) for _ in range(2)]
            JUNK2 = Bd([128, 1024], BF16)
            SS2 = [Bd([128, 4], F32) for _ in range(2)]
            SG = [Bd([128, 512], F32) for _ in range(2)]
            OST = [carve(C_END + 65536 + 4096 * i, [128, 1024], F32) for i in range(2)]
            assert Bd.off <= ARENA_BYTES, Bd.off

            def rms_sb(xt, i, tag):
                ssq = SS2[i]
                P.op("act", A_ACT(JUNK2, xt, AF.Square, accum_out=ssq[:, 0:1]), reads=[tag], writes=["s2_%d_0" % i])
                P.op("dve", V_TS(ssq[:, 1:2], ssq[:, 0:1], 1.0 / D, EPS, ALU.mult, ALU.add), reads=["s2_%d_0" % i], writes=["s2_%d_1" % i])
                P.op("pool", V_TT(ssq[:, 2:3], ssq[:, 1:2], mhalf, ALU.pow), reads=["s2_%d_1" % i, "mhalf"], writes=["s2_%d_2" % i])

            for th in range(2):
                load("pool", WDB, I["wd"][:, 0:512].rearrange("(j p) n -> p j n", p=128), "WDB")

                def n2_a(mm_):
                    m = th * 8 + mm_
                    rms_sb(X1T[:, m, :], m % 2, "X1T%d" % m)

                def n2_b(mm_):
                    m = th * 8 + mm_
                    i = m % 2
                    P.op("dve", V_STT(HB2[i], X1T[:, m, :], SS2[i][:, 2:3], nw, ALU.mult, ALU.mult),
                         reads=["X1T%d" % m, "s2_%d_2" % i, "nw"], writes=["hb2_%d" % i])

                def n2_c(mm_):
                    i = (th * 8 + mm_) % 2
                    tv = pbank(i, [128, 8, 128], BF16)
                    P.op("pe", seq([TR(tv[:, k, :], HB2[i][:, k * 128:(k + 1) * 128], identb) for k in range(8)]),
                         reads=["hb2_%d" % i, "identb"], banks=[i])
                    P.op("act", A_CP(H2T[:, :, mm_ * 128:(mm_ + 1) * 128], tv), writes=["H2T"], banks=[i])

                for s_ in range(8 + 2):
                    if s_ < 8:
                        n2_a(s_)
                    if 0 <= s_ - 2 < 8:
                        n2_c(s_ - 2)
                    if 0 <= s_ - 1 < 8:
                        n2_b(s_ - 1)
                for jp in range(11):
                    wgb, wub = WGU[jp % 2]
                    cs = slice(jp * 256, (jp + 1) * 256)
                    load("pool", wgb, I["wg"][:, cs].rearrange("(kc kp) n -> kp kc n", kp=128), "wg%d" % (jp % 2))
                    load("pool", wub, I["wu"][:, cs].rearrange("(kc kp) n -> kp kc n", kp=128), "wu%d" % (jp % 2))
                    for cc in range(2):
                        jf = jp * 2 + cc
                        for tt in range(2):
                            q4 = (jf * 2 + tt) % 2
                            bg, bu = (2, 3) if q4 == 0 else (4, 5)
                            pg, pu = pbank(bg, [128, 512], F32), pbank(bu, [128, 512], F32)
                            ts_ = slice(tt * 512, (tt + 1) * 512)
                            P.op("pe", seq([MM(pg, wgb[:, k, cc * 128:(cc + 1) * 128], H2T[:, k, ts_], k == 0, k == 7) for k in range(8)]),
                                 reads=["wg%d" % (jp % 2), "H2T"], banks=[bg])
                            P.op("pe", seq([MM(pu, wub[:, k, cc * 128:(cc + 1) * 128], H2T[:, k, ts_], k == 0, k == 7) for k in range(8)]),
                                 reads=["wu%d" % (jp % 2), "H2T"], banks=[bu])
                            sg = SG[q4]
                            P.op("act", A_ACT(sg, pg, AF.Silu), writes=["sg%d" % q4], banks=[bg])
                            P.op("dve", V_TT(ACTT[:, jf, ts_], pu, sg, ALU.mult), reads=["sg%d" % q4], writes=["ACTT"], banks=[bu])
                for nh in range(2):
                    if nh == 1:
                        load("pool", WDB, I["wd"][:, 512:1024].rearrange("(j p) n -> p j n", p=128), "WDB")
                    for mm_ in range(8):
                        m = th * 8 + mm_
                        bk = 6 + mm_ % 2
                        po = pbank(bk, [128, 512], F32)
                        P.op("pe", seq([MM(po, ACTT[:, jf, mm_ * 128:(mm_ + 1) * 128], WDB[:, jf, :], jf == 0, jf == 21) for jf in range(22)]),
                             reads=["ACTT", "WDB"], banks=[bk])
                        xs = X1T[:, m, nh * 512:(nh + 1) * 512]
                        P.op("dve", V_TT(xs, po, xs, ALU.add), reads=["X1T%d" % m], writes=["X1T%d" % m], banks=[bk])
            load("sp", nw, I["nfw"].partition_broadcast(128), "nw")
            P.barrier()
            for s_ in range(NQ + 1):
                if s_ < NQ:
                    rms_sb(X1T[:, s_, :], s_ % 2, "X1T%d" % s_)
                if s_ - 1 >= 0:
                    m = s_ - 1
                    i = m % 2
                    ost = OST[i]
                    P.op("dve", V_STT(ost, X1T[:, m, :], SS2[i][:, 2:3], nw, ALU.mult, ALU.mult),
                         reads=["X1T%d" % m, "s2_%d_2" % i, "nw"], writes=["ost%d" % i])
                    P.dma("sp", DMA(out_d[m * 128:(m + 1) * 128, :], ost), reads=["ost%d" % i], key="ost%d" % i)
        else:
            P.dma("sp", DMA(out_d[0:128, :], nw), reads=["nw"], key="ost0")
        nsem = P.emit(st)
    return nc, nsem, len(P.ops), dbg_list


_CACHE = {}


def kernel(**inputs):
    sh = _shared_inputs(inputs)
    in_maps = [_core_inputs(inputs, sh, r) for r in range(8)]
    if "nc" not in _CACHE:
        _CACHE["nc"] = build_program()[0]
    nc = _CACHE["nc"]
    res = run_bass_kernel_spmd(nc, in_maps, core_ids=list(range(8)))
    out = np.zeros((2, S, D), np.float32)
    for r in range(8):
        b, c = r // 4, r % 4
        o = np.asarray(res.results[r]["out"], np.float32).reshape(NQ, 128, D)
        for m in range(NQ):
            t0 = (4 * m + c) * 128
            out[b, t0:t0 + 128] = o[m]
    return out
```

```python
import numpy as np
from contextlib import ExitStack
import concourse.bass as bass
import concourse.mybir as mybir
from concourse.bass_utils import run_bass_kernel_spmd

F32 = mybir.dt.float32
BF16 = mybir.dt.bfloat16
AF = mybir.ActivationFunctionType
ALU = mybir.AluOpType

D = 1024
S = 8192
NT = 64
NQ = 16
DFF = 2816
NEG = -30000.0
EPS = 1e-6
BIG = 1.0e9


class Op:
    __slots__ = ("eng", "fn", "idx", "deps", "bdeps", "signal", "sig", "is_dma", "key")

    def __init__(self, eng, fn, idx, is_dma, key):
        self.eng = eng
        self.fn = fn
        self.idx = idx
        self.deps = set()
        self.bdeps = set()
        self.signal = False
        self.sig = None
        self.is_dma = is_dma
        self.key = key


class Prog:
    ENGS = ("pe", "act", "dve", "pool", "sp")
    BLOCK_NAME = {"pe": "tensor", "act": "scalar", "dve": "vector", "pool": "gpsimd", "sp": "sync"}

    def __init__(self, nc):
        self.nc = nc
        self.ops = []
        self.last_writer = {}
        self.readers = {}
        self.bank_last = {}
        self.last_on_eng = {}
        self.pending_dmas = []

    def op(self, eng, fn, reads=(), writes=(), banks=(), dma=False, key=None):
        o = Op(eng, fn, len(self.ops), dma, key)
        deps = set()
        for r in reads:
            w = self.last_writer.get(r)
            if w is not None:
                deps.add(w)
            self.readers.setdefault(r, []).append(o)
        for r in writes:
            w = self.last_writer.get(r)
            if w is not None:
                deps.add(w)
            for rd in self.readers.get(r, ()):
                deps.add(rd)
            self.readers[r] = []
            self.last_writer[r] = o
        deps.discard(o)
        o.deps = deps
        for b in banks:
            w = self.bank_last.get(b)
            if w is not None and w is not o:
                o.bdeps.add(w)
            self.bank_last[b] = o
        self.ops.append(o)
        if dma:
            self.pending_dmas.append(o)
        else:
            self.last_on_eng[eng] = o
        return o

    def dma(self, eng, fn, reads=(), writes=(), key=None):
        assert key is not None
        return self.op(eng, fn, reads, writes, dma=True, key=key)

    def barrier(self):
        deps = set(self.last_on_eng.values()) | set(self.pending_dmas)
        for en in self.ENGS:
            o = Op(en, None, len(self.ops), False, None)
            o.deps = set(deps)
            self.ops.append(o)
        self.pending_dmas = []
        self.last_writer = {}
        self.readers = {}
        self.bank_last = {}

    def emit(self, stack, final_wait_eng="sp"):
        nc = self.nc
        ops = self.ops
        for o in ops:
            for d in o.deps:
                if d.is_dma:
                    continue
                if d.eng == "pe" and o.eng == "pe" and o.fn is not None:
                    continue
                d.signal = True
            for d in o.bdeps:
                if d.is_dma or d.eng == o.eng:
                    continue
                d.signal = True
        sems = {}
        counts = {}

        def get_sem(k):
            if k not in sems:
                sems[k] = stack.enter_context(nc.semaphore("s_" + "_".join(str(x) for x in k)))
                counts[k] = 0
            return sems[k]

        for o in ops:
            if o.is_dma:
                k = ("d", o.key)
                s = get_sem(k)
                counts[k] += 16
                o.sig = (k, s, counts[k])
            elif o.signal:
                k = ("e", o.eng)
                s = get_sem(k)
                counts[k] += 1
                o.sig = (k, s, counts[k])
        finals = [(k, sems[k], counts[k]) for k in sems if k[0] == "d"]
        block = stack.enter_context(nc.Block())
        for en in self.ENGS:
            eops = [o for o in ops if o.eng == en]

            def body(e, eops=eops, en=en):
                waited = {}
                for o in eops:
                    dl = [d for d in o.deps if not (d.eng == "pe" and en == "pe" and not d.is_dma and o.fn is not None)]
                    dl += [d for d in o.bdeps if d.is_dma or d.eng != en]
                    for d in sorted(dl, key=lambda d: d.idx):
                        if d.sig is None:
                            continue
                        k, s, v = d.sig
                        if waited.get(k, 0) < v:
                            e.wait_ge(s, v)
                            waited[k] = v
                    if o.fn is None:
                        continue
                    inst = o.fn(e)
                    if o.sig is not None:
                        k, s, v = o.sig
                        inst.then_inc(s, 16 if o.is_dma else 1)
                if en == final_wait_eng:
                    for k, s, v in finals:
                        if waited.get(k, 0) < v:
                            e.wait_ge(s, v)

            getattr(block, self.BLOCK_NAME[en])(body)
        return len(sems)


def seq(fns):
    def f(e):
        i = None
        for fn in fns:
            i = fn(e)
        return i
    return f


def MM(out, lhsT, rhs, start=True, stop=True):
    return lambda e: e.matmul(out, lhsT=lhsT, rhs=rhs, start=start, stop=stop)


def TR(out, in_, ident):
    return lambda e: e.transpose(out=out, in_=in_, identity=ident)


IN_SIZES = (512, 1024, 256, 256, 256, 256, 256, 256, 48, 1024, 1024)
_sp = np.cumsum((0,) + IN_SIZES)
COL = {n: (int(_sp[i]), int(_sp[i + 1])) for i, n in enumerate(
    ["pool", "q", "kc", "vc", "ks", "vs", "kw", "vw", "gnsa", "gpool", "gattn"])}


def _shared_inputs(inp):
    f = np.float32
    w_in = np.asarray(inp["w_in"], f)[0]

    def cols(name, lo, hi):
        a, _ = COL[name]
        return w_in[:, a + lo:a + hi]

    wkv = np.stack([np.concatenate([cols(n, 64 * g, 64 * g + 64) for n in ("kc", "ks", "vc", "kw", "vs", "vw")], axis=1)
                    for g in range(4)], 0)
    wq = np.stack([np.concatenate([cols("q", 256 * g, 256 * g + 256), cols("gnsa", 12 * g, 12 * g + 12)], axis=1)
                   for g in range(4)], 0)
    sh = {
        "wkv": np.ascontiguousarray(wkv), "wq": np.ascontiguousarray(wq),
        "wpool": np.ascontiguousarray(cols("pool", 0, 512)),
        "wgp": np.ascontiguousarray(cols("gpool", 0, 1024)),
        "wga": np.ascontiguousarray(cols("gattn", 0, 1024)),
        "n1w": np.asarray(inp["norm1_w"], f)[0], "n2w": np.asarray(inp["norm2_w"], f)[0],
        "nfw": np.asarray(inp["norm_f_w"], f),
        "poolw": np.asarray(inp["pool_w"], f)[0],
        "psc": np.ascontiguousarray(np.asarray(inp["pool_scale"], f)[0].reshape(4, 128).T),
        "w1k": np.asarray(inp["cmp_w1_k"], f)[0], "w1v": np.asarray(inp["cmp_w1_v"], f)[0],
        "b1t": np.ascontiguousarray(np.concatenate([np.asarray(inp["cmp_b1_k"], f)[0].reshape(2, 128).T,
                                                    np.asarray(inp["cmp_b1_v"], f)[0].reshape(2, 128).T], axis=1)),
        "w2k": np.asarray(inp["cmp_w2_k"], f)[0], "w2v": np.asarray(inp["cmp_w2_v"], f)[0],
        "pekT": np.ascontiguousarray(np.asarray(inp["cmp_pe_k"], f)[0].T),
        "pevT": np.ascontiguousarray(np.asarray(inp["cmp_pe_v"], f)[0].T),
        "wpp": np.asarray(inp["w_proj_pool"], f)[0], "wpa": np.asarray(inp["w_proj_attn"], f)[0],
        "wout": np.asarray(inp["w_out"], f)[0],
        "wg": np.asarray(inp["w_ffn_gate"], f)[0], "wu": np.asarray(inp["w_ffn_up"], f)[0],
        "wd": np.asarray(inp["w_ffn_down"], f)[0],
    }
    sh["ident"] = np.eye(128, dtype=f)
    k = np.arange(S)
    sh["epat"] = ((k[None, :] // 64) % 64 == np.arange(64)[:, None]).astype(f)
    z = np.zeros((128, 224), f)
    z[np.arange(32), 96 + np.arange(32)] = 1.0
    z[32, 128:] = 1.0
    sh["zsel"] = z
    n = np.arange(512)
    mm = np.zeros((512, 128), f)
    for nn in range(511):
        mm[nn, (16 * nn) // 64] = 1.0
        mm[nn, (16 * nn + 31) // 64] = 1.0
    sh["mmap"] = np.ascontiguousarray(mm.reshape(4, 128, 128).transpose(1, 0, 2))
    inv_freq = (1.0 / (np.float32(500000.0) ** (np.arange(0, 16, 2, dtype=f) / np.float32(16)))).astype(f)
    ang = np.arange(S, dtype=f)[:, None] * inv_freq[None, :]
    sh["_cos"] = np.cos(ang).astype(f)
    sh["_sin"] = np.sin(ang).astype(f)
    kk = np.arange(128)[:, None]
    qq = np.arange(128)[None, :]
    tri = np.where(kk <= qq, 0.0, NEG).astype(f)
    tri2 = np.where(kk > qq, 0.0, NEG).astype(f)
    sh["tm"] = np.ascontiguousarray(np.broadcast_to(tri[:, None, :], (128, 4, 128)).reshape(128, 512))
    sh["wm"] = np.ascontiguousarray(np.stack([np.broadcast_to(tri2[:, None, :], (128, 4, 128)).reshape(128, 512),
                                              np.broadcast_to(tri[:, None, :], (128, 4, 128)).reshape(128, 512)], axis=1))
    n32 = np.arange(32)[:, None, None]
    cm = np.where(16 * n32 + 31 <= 384 + np.arange(128)[None, None, :], 0.0, NEG).astype(f)
    cmf = np.zeros((128, 512), f)
    cmf[0:32] = np.broadcast_to(cm, (32, 4, 128)).reshape(32, 512)
    cmf[32] = NEG
    sh["cm"] = cmf
    ps = np.zeros((128, 128), f)
    for i in range(3):
        ps[i, :8 * (i + 1)] = 1.0
    sh["padsel"] = ps
    q1 = np.arange(128)[:, None]
    rel = np.arange(248)[None, :] - 120
    jt0 = 6 + (q1 >= 64)
    g_ = np.zeros((128, 248), f)
    g_[rel > jt0] = -BIG
    g_[(rel == jt0) | (rel == jt0 - 1)] = BIG
    sh["gtab"] = g_
    return sh


def _core_inputs(inp, sh, r):
    f = np.float32
    b, c = r // 4, r % 4
    x = np.asarray(inp["x"], f)
    xb = x[b]
    toks = (np.arange(NQ)[:, None] * 4 + c) * 128 + np.arange(128)[None, :]
    d = {k_: v for k_, v in sh.items() if not k_.startswith("_")}
    sft = 3 - c
    xkv = np.zeros((S, D), f)
    xkv[sft * 128:] = xb[:S - sft * 128]
    d["xkv"] = xkv
    d["xq"] = np.ascontiguousarray(xb[toks.reshape(-1)])
    ht = (np.arange(NQ)[:, None] * 4 + c) * 128 - 16 + np.arange(16)[None, :]
    xh = np.zeros((NQ * 16, D), f)
    valid = (ht >= 0).reshape(-1)
    xh[valid] = xb[ht.reshape(-1)[valid]]
    d["xh"] = xh
    d["cosq"] = np.ascontiguousarray(sh["_cos"][toks].transpose(1, 0, 2))
    d["sinq"] = np.ascontiguousarray(sh["_sin"][toks].transpose(1, 0, 2))
    kpos = np.clip(np.arange(S) - sft * 128, 0, None)
    d["cosk"] = np.ascontiguousarray(sh["_cos"][kpos].reshape(64, 128, 8).transpose(1, 0, 2))
    d["sink"] = np.ascontiguousarray(sh["_sin"][kpos].reshape(64, 128, 8).transpose(1, 0, 2))
    wm0 = np.zeros((128, 3, 512), f)
    for sl in range(3):
        if sl < sft:
            wm0[:, sl, :] = NEG
    d["wm0"] = wm0
    pm = np.zeros((128, 512), f)
    if sft > 0:
        pm[sft - 1] = NEG
    d["padm"] = pm
    g0 = np.zeros((128, 128), f)
    g0[:, :2 * sft] = -3 * BIG
    g0[:, 2 * sft] = BIG
    d["g0"] = g0
    wv = np.array([2, 4, 8, 16])[:, None]
    tt = c * 128 + np.arange(128)[None, :]
    invc = (1.0 / np.minimum(tt + 1, wv)).astype(f)
    d["invc"] = np.ascontiguousarray(np.broadcast_to(invc[None], (128, 4, 128)))
    return d


INPUT_SHAPES = {
    "xkv": [S, D], "xq": [2048, D], "xh": [256, D], "n1w": [D], "n2w": [D], "nfw": [D],
    "wkv": [4, D, 384], "wq": [4, D, 268], "wpool": [D, 512], "wgp": [D, D], "wga": [D, D],
    "poolw": [4, 128, 128], "psc": [128, 4], "w1k": [2048, 256], "w1v": [2048, 256],
    "b1t": [128, 4], "w2k": [256, 64], "w2v": [256, 64], "pekT": [64, 32], "pevT": [64, 32],
    "wpp": [512, D], "wpa": [D, D], "wout": [D, D], "wg": [D, DFF], "wu": [D, DFF], "wd": [DFF, D],
    "ident": [128, 128], "epat": [64, S], "zsel": [128, 224], "mmap": [128, 4, 128],
    "cosk": [128, 64, 8], "sink": [128, 64, 8], "cosq": [128, 16, 8], "sinq": [128, 16, 8],
    "tm": [128, 512], "wm": [128, 2, 512], "wm0": [128, 3, 512], "cm": [128, 512], "gtab": [128, 248], "invc": [128, 4, 128],
    "padsel": [128, 128], "padm": [128, 512], "g0": [128, 128],
}

ARENA_BYTES = 206 * 1024


def A_ACT(out, in_, func, **kw):
    return lambda e: e.activation(out=out, in_=in_, func=func, **kw)


def A_CP(out, in_):
    return lambda e: e.copy(out=out, in_=in_)


def V_CP(out, in_):
    return lambda e: e.tensor_copy(out=out, in_=in_)


def V_TT(out, a, b, op):
    return lambda e: e.tensor_tensor(out=out, in0=a, in1=b, op=op)


def V_TS(out, in0, s1, s2, op0, op1=None):
    if op1 is None:
        return lambda e: e.tensor_scalar(out=out, in0=in0, scalar1=s1, scalar2=None, op0=op0)
    return lambda e: e.tensor_scalar(out=out, in0=in0, scalar1=s1, scalar2=s2, op0=op0, op1=op1)


def V_STT(out, in0, scalar, in1, op0, op1):
    return lambda e: e.scalar_tensor_tensor(out=out, in0=in0, scalar=scalar, in1=in1, op0=op0, op1=op1)


def V_REC(out, in_):
    return lambda e: e.reciprocal(out=out, in_=in_)


def V_MEMSET(ap, val):
    return lambda e: e.memset(ap, val)


def V_MAX(out, in_):
    return lambda e: e.max(out=out, in_=in_)


def V_MR(out, rep, vals, imm):
    return lambda e: e.match_replace(out=out, in_to_replace=rep, in_values=vals, imm_value=imm)


def DMA(out, in_):
    return lambda e: e.dma_start(out=out, in_=in_)


def build_program(n_groups=4, phase3=True, dbg=False, n_qblocks=NQ):
    nc = bass.Bass("TRN2", target_bir_lowering=False)
    I = {n: nc.dram_tensor(n, shp, F32, kind="ExternalInput").ap() for n, shp in INPUT_SHAPES.items()}
    out_d = nc.dram_tensor("out", [2048, D], F32, kind="ExternalOutput").ap()
    scr = nc.dram_tensor("htq_scr", [18, 128, 1024], BF16, kind="Internal").ap()
    scr2 = nc.dram_tensor("hkv_scr", [NT, 128, 1024], BF16, kind="Internal").ap()
    st = ExitStack()
    with st:
        arena = st.enter_context(nc.sbuf_tensor("arena", [128, ARENA_BYTES // 4], F32))
        banks = [st.enter_context(nc.psum_tensor("bank%d" % i, [128, 512], F32)) for i in range(8)]
        P = Prog(nc)

        def carve(off, shape, dt, p0=0):
            esz = 4 if dt == F32 else 2
            n = int(np.prod(shape[1:]))
            nb = n * esz
            assert off % 4 == 0 and nb % 4 == 0, (off, shape)
            assert off + nb <= ARENA_BYTES, (off, shape)
            ap = arena[p0:p0 + shape[0], off // 4:(off + nb) // 4]
            if dt != F32:
                ap = ap.bitcast(dt)
            if len(shape) == 3:
                ap = ap.rearrange("p (a b) -> p a b", a=shape[1])
            elif len(shape) == 4:
                ap = ap.rearrange("p (a b c) -> p a b c", a=shape[1], b=shape[2])
            return ap

        class Bump:
            def __init__(self, off):
                self.off = off

            def __call__(self, shape, dt, p0=0):
                esz = 4 if dt == F32 else 2
                nb = (int(np.prod(shape[1:])) * esz + 31) // 32 * 32
                ap = carve(self.off, shape, dt, p0)
                self.off += nb
                return ap

        def pbank(i, shape, dt, p0=0, col0=0):
            n = int(np.prod(shape[1:]))
            if dt == F32:
                ap = banks[i][p0:p0 + shape[0], col0:col0 + n]
            else:
                ap = banks[i][p0:p0 + shape[0], col0:col0 + n // 2].bitcast(dt)
            if len(shape) == 3:
                ap = ap.rearrange("p (a b) -> p a b", a=shape[1])
            elif len(shape) == 4:
                ap = ap.rearrange("p (a b c) -> p a b c", a=shape[1], b=shape[2])
            return ap

        dbg_list = []

        def dump(name, ap, shape, dt):
            if not dbg:
                return
            P.barrier()
            dd = nc.dram_tensor("dbg_" + name, list(shape), dt, kind="ExternalOutput").ap()
            P.dma("sp", DMA(dd, ap), key="dbg_" + name)
            P.barrier()
            dbg_list.append(name)

        def load(eng, dst, src, name):
            P.dma(eng, DMA(dst, src), writes=[name], key=name)

        A_ = Bump(0)
        identb = A_([128, 128], BF16)
        identf = A_([128, 128], F32)
        nw = A_([128, 1024], F32)
        cosk = A_([128, 64, 8], F32)
        sink = A_([128, 64, 8], F32)
        cosq = A_([128, 16, 8], F32)
        sinq = A_([128, 16, 8], F32)
        tmb = A_([128, 512], BF16)
        wmb = A_([128, 2, 512], BF16)
        wm0b = A_([128, 3, 512], BF16)
        padselb = A_([128, 128], BF16)
        padmb = A_([128, 512], BF16)
        g0t = A_([128, 128], F32)
        cmb = A_([128, 512], BF16)
        zsel = A_([128, 224], BF16)
        mmapb = A_([128, 4, 128], BF16)
        gtab = A_([128, 248], F32)
        invc = A_([128, 4, 128], F32)
        beff = A_([128, 4], F32)
        b1t = A_([128, 4], F32)
        peT = A_([128, 32], BF16)
        w2b = A_([128, 2, 2, 64], BF16)
        mhalf = A_([128, 1], F32)
        psc = A_([128, 4], F32)
        C_END = (A_.off + 1023) // 1024 * 1024
        attnT = carve(C_END, [128, 8, 2048], BF16)
        P0 = C_END + 32768

        load("pool", identb, I["ident"], "identb")
        load("sp", identf, I["ident"], "identf")
        load("sp", nw, I["n1w"].partition_broadcast(128), "nw")
        load("sp", cosk, I["cosk"], "cosk")
        load("sp", sink, I["sink"], "sink")
        load("sp", cosq, I["cosq"], "cosq")
        load("sp", sinq, I["sinq"], "sinq")
        load("pool", tmb, I["tm"], "tmb")
        load("pool", wmb, I["wm"], "wmb")
        load("pool", wm0b, I["wm0"], "wm0b")
        load("pool", padselb, I["padsel"], "padselb")
        load("pool", padmb, I["padm"], "padmb")
        load("sp", g0t, I["g0"], "g0t")
        load("pool", cmb, I["cm"], "cmb")
        load("pool", zsel, I["zsel"], "zsel")
        load("pool", mmapb, I["mmap"], "mmapb")
        load("sp", gtab, I["gtab"], "gtab")
        load("sp", invc, I["invc"], "invc")
        load("sp", b1t, I["b1t"], "b1t")
        load("sp", psc, I["psc"], "psc")
        load("pool", peT[0:64, :], I["pekT"], "pek")
        load("pool", peT[64:128, :], I["pevT"], "pev")
        load("pool", w2b[:, 0, :, :], I["w2k"].rearrange("(j p) d -> p j d", p=128), "w2k")
        load("pool", w2b[:, 1, :, :], I["w2v"].rearrange("(j p) d -> p j d", p=128), "w2v")
        P.op("pool", V_MEMSET(mhalf, -0.5), writes=["mhalf"])

        B1 = Bump(P0)
        KA = B1([128, S], BF16)
        AT_ = B1([128, S], BF16)
        BW = B1([128, S], BF16)
        VSW = B1([128, 64, 2, 66], BF16)
        VC = B1([128, 4, 66], BF16)
        KCT = B1([128, 512], BF16)
        WKV = B1([128, 8, 384], BF16)
        WQ = B1([128, 8, 268], BF16)
        X0 = B1.off
        X1 = Bump(X0)
        XB = [X1([128, 1024], F32) for _ in range(3)]
        HB = [X1([128, 1024], BF16) for _ in range(2)]
        HT = [X1([128, 8, 128], BF16) for _ in range(4)]
        JUNK = X1([128, 1024], BF16)
        SSQ = [X1([128, 4], F32) for _ in range(3)]
        KVB = [X1([128, 2, 2, 64], BF16) for _ in range(3)]
        RT = [X1([128, 4, 2, 8], F32) for _ in range(3)]
        W1 = X1([128, 32, 256], BF16)
        GX = X1([128, 512], F32)
        GU = X1([128, 512], F32)
        GS = X1([128, 512], F32)
        HID = X1([128, 2, 2, 512], BF16)
        assert X1.off <= ARENA_BYTES, X1.off
        X2 = Bump(X0)
        HQ = [X2([128, 8, 128], BF16) for _ in range(2)]
        QPR = [X2([128, 4, 2, 64], BF16) for _ in range(2)]
        RS = 6
        GSB = [X2([128, 12], F32) for _ in range(RS)]
        GEX = X2([128, 12], F32)
        RT2 = [X2([128, 4, 4, 8], F32) for _ in range(2)]
        QPT = [X2([128, 512], BF16) for _ in range(2)]
        QA2 = [X2([128, 2, 512], BF16) for _ in range(RS)]
        PC = [X2([128, 512], BF16) for _ in range(4)]
        PT = [X2([128, 512], BF16) for _ in range(4)]
        OT = X2([128, 3, 512], F32)
        OTW = [X2([128, 512], F32) for _ in range(2)]
        IMPV = X2([128, 128], F32)
        V2 = X2([128, 128], F32)
        SELA = X2([128, 128], F32)
        SELB = X2([128, 128], F32)
        WK = X2([128, 128], F32)
        NMP = X2([128, 2, 128], BF16)
        M8 = X2([128, 16], F32)
        DEN = X2([128, 3, 4], F32)
        RDEN = X2([128, 3, 4], F32)
        COEF = X2([128, 3, 4], F32)
        ACC = [X2([128, 4, 64], F32) for _ in range(RS)]
        ATB = X2([128, 256], BF16)
        assert X2.off <= ARENA_BYTES, X2.off

        load("pool", KA[0:64, :], I["epat"], "KA_E")
        P.op("pool", V_MEMSET(VSW[:, :, :, 64:66], 1.0), writes=["VSW_ones"])
        P.op("pool", V_MEMSET(VC[:, :, 64:66], 1.0), writes=["VC_ones"])
        P.op("pool", V_MEMSET(KCT[:, 508:512], 0.0), writes=["KCT_pad"])
        P.op("pool", V_MEMSET(KCT[64:128, :], 0.0), writes=["KCT_z"])
        P.op("pool", V_MEMSET(BW[0:64, :], 0.0), writes=["BW_z"])

        def norm_dma(src_dram, xi):
            P.dma("sp", DMA(XB[xi], src_dram), writes=["xb%d" % xi], key="xb%d" % xi)

        def norm_a1(xi):
            xb, ssq = XB[xi], SSQ[xi]
            xn = "xb%d" % xi
            P.op("act", A_ACT(JUNK, xb, AF.Square, accum_out=ssq[:, 0:1]), reads=[xn], writes=["ss%d_0" % xi])
            P.op("dve", V_TS(ssq[:, 1:2], ssq[:, 0:1], 1.0 / D, EPS, ALU.mult, ALU.add), reads=["ss%d_0" % xi], writes=["ss%d_1" % xi])
            P.op("pool", V_TT(ssq[:, 2:3], ssq[:, 1:2], mhalf, ALU.pow), reads=["ss%d_1" % xi, "mhalf"], writes=["ss%d_2" % xi])

        def norm_a2(xi, hi):
            xb, ssq, hb = XB[xi], SSQ[xi], HB[hi]
            P.op("dve", V_STT(hb, xb, ssq[:, 2:3], nw, ALU.mult, ALU.mult),
                 reads=["xb%d" % xi, "ss%d_2" % xi, "nw"], writes=["hb%d" % hi])

        def norm_b(hi):
            hb, hts = HB[hi], HT[hi]
            tv = pbank(hi, [128, 8, 128], BF16)
            P.op("pe", seq([TR(tv[:, k, :], hb[:, k * 128:(k + 1) * 128], identb) for k in range(8)]),
                 reads=["hb%d" % hi, "identb"], banks=[hi])
            P.op("act", A_CP(hts, tv), writes=["hT%d" % hi], banks=[hi])

        def rope_ops(psrc, dst, cos_ap, sin_ap, rt, nh, rd, wr, rtname, bank):
            cb = cos_ap.unsqueeze(1).to_broadcast([128, nh, 8])
            sb_ = sin_ap.unsqueeze(1).to_broadcast([128, nh, 8])
            x1 = psrc[:, :, 0:8]
            x2 = psrc[:, :, 8:16]
            P.op("dve", seq([V_TT(rt[:, 0], x1, cb, ALU.mult), V_TT(rt[:, 1], x2, sb_, ALU.mult),
                             V_TT(rt[:, 2], x2, cb, ALU.mult), V_TT(rt[:, 3], x1, sb_, ALU.mult)]),
                 reads=rd, writes=[rtname], banks=[bank])
            P.op("dve", seq([V_TT(dst[:, :, 0:8], rt[:, 0], rt[:, 1], ALU.subtract),
                             V_TT(dst[:, :, 8:16], rt[:, 2], rt[:, 3], ALU.add)]),
                 reads=[rtname], writes=[wr])

        def pre_src(blk):
            return I["xq"][blk * 128:(blk + 1) * 128, :] if blk < 16 else I["xh"][(blk - 16) * 128:(blk - 15) * 128, :]

        norm_dma(pre_src(0), 0)
        norm_dma(pre_src(1), 1)
        for s_ in range(18 + 2):
            if s_ + 2 < 18:
                norm_dma(pre_src(s_ + 2), (s_ + 2) % 3)
            if s_ < 18:
                norm_a1(s_ % 3)
                norm_a2(s_ % 3, s_ % 2)
            if 0 <= s_ - 1 < 18:
                norm_b((s_ - 1) % 2)
            if 0 <= s_ - 2 < 18:
                blk = s_ - 2
                hi = blk % 2
                P.dma("sp", DMA(scr[blk].rearrange("p (k t) -> p k t", k=8), HT[hi]),
                      reads=["hT%d" % hi], writes=["scr%d" % blk], key="scrw%d" % hi)

        S_BANKS = (0, 1, 7)
        B_OC, B_OS, B_OW, B_I, B_QT, B_F = 2, 3, 4, 5, 6, 6
        tile_ctr = [0]

        def step_gen(gen):
            if gen is None:
                return None
            try:
                next(gen)
                return gen
            except StopIteration:
                return None

        def exhaust(gen):
            while gen is not None:
                gen = step_gen(gen)

        LAG = 2
        pt_ctr = [0]

        def chain(*gens):
            for g_ in gens:
                if g_ is not None:
                    yield from g_

        def tiles_gen(tiles):
            n = len(tiles)
            for ti in range(n + LAG):
                if ti < n:
                    T = tiles[ti]
                    sbk = S_BANKS[tile_ctr[0] % 3]
                    tile_ctr[0] += 1
                    if T["pbuf"] is None:
                        T["pbuf"] = PT[pt_ctr[0] % 4]
                        T["pname"] = "pt%d" % (pt_ctr[0] % 4)
                        pt_ctr[0] += 1
                    so = pbank(sbk, [128, 512], F32)
                    nq = len(T["qk"])
                    P.op("pe", seq([MM(so, a, b, qi == 0, qi == nq - 1) for qi, (a, b) in enumerate(T["qk"])]),
                         reads=T["rd"], banks=[sbk])
                    P.op("act", A_ACT(T["pbuf"], so, AF.Exp, scale=0.125), writes=[T["pname"]], banks=[sbk])
                if ti - LAG >= 0:
                    pend = tiles[ti - LAG]
                    lhsT, obank, first, last = pend["pv"]
                    oo = pbank(obank, [65, 512], F32)
                    P.op("pe", MM(oo, lhsT, pend["pbuf"], first, last), reads=[pend["pname"]] + pend["pvrd"], banks=[obank])
                    if pend.get("post") is not None:
                        pend["post"]()
                yield

        def run_tiles(tiles, side=None, stride=1):
            k = 0
            for _ in tiles_gen(tiles):
                k += 1
                if side is not None and k % stride == 0:
                    side = step_gen(side)
            return side

        for g in range(n_groups):
            load("pool", WKV, I["wkv"][g].rearrange("(kc kp) n -> kp kc n", kp=128), "WKV")
            load("pool", WQ, I["wq"][g].rearrange("(kc kp) n -> kp kc n", kp=128), "WQ")
            load("pool", W1[0:64], I["w1k"].rearrange("(l d) j -> d l j", d=64), "W1k")
            load("pool", W1[64:128], I["w1v"].rearrange("(l d) j -> d l j", d=64), "W1v")
            P.op("pool", V_MEMSET(HID[:, :, :, 508:512], 0.0), writes=["HID_pad"])
            def kv_s1(t, hi):
                j2 = t % 2
                j3 = t % 3
                pb = 2 + j2
                pv = pbank(pb, [128, 3, 2, 64], F32)
                P.op("pe", seq([MM(pbank(pb, [128, 384], F32), HT[hi][:, k, :], WKV[:, k, :], k == 0, k == 7) for k in range(8)]),
                     reads=["hT%d" % hi, "WKV"], banks=[pb])
                kvb = KVB[j3]
                P.op("act", seq([A_CP(kvb[:, :, 0, :], pv[:, 0:2, 0, :]), A_CP(kvb[:, :, 1, 16:64], pv[:, 0:2, 1, 16:64]),
                                 A_CP(VSW[:, t, :, 0:64], pv[:, 2, :, :])]),
                     reads=["VSW_ones"], writes=["kvb%d_a" % j3, "VSW"], banks=[pb])
                rope_ops(pv[:, 0:2, 1, :], kvb[:, :, 1, :], cosk[:, t, :], sink[:, t, :], RT[j3], 2,
                         ["cosk", "sink"], "kvb%d_r" % j3, "rt%d" % j3, pb)

            def kv_s2(t):
                j2 = t % 2
                j3 = t % 3
                kvb = KVB[j3]
                kb = 4 + j2
                kv = pbank(kb, [128, 3, 128], BF16)
                kflat = kvb.rearrange("p a b d -> p (a b d)")
                P.op("pe", seq([TR(kv[:, 0, :], kflat[:, 0:128], identb), TR(kv[:, 1, :], kflat[:, 128:256], identb),
                                TR(kv[:, 2, :], kflat[:, 64:192], identb)]),
                     reads=["kvb%d_a" % j3, "kvb%d_r" % j3, "identb"], banks=[kb])
                cs = slice(t * 128, (t + 1) * 128)
                atd = AT_.rearrange("p (s n) -> p s n", s=16)[:, :, t * 8:(t + 1) * 8]
                P.op("dve", seq([V_CP(atd[0:64], kv[0:64, 0, :].rearrange("p (n s) -> p s n", s=16)),
                                 V_CP(KA[64:128, cs], kv[64:128, 0, :])]),
                     writes=["AT_lo", "KA_hi"], banks=[kb])
                P.op("act", seq([A_CP(BW[64:128, cs], kv[64:128, 1, :]),
                                 A_CP(atd[64:128], kv[64:128, 2, :].rearrange("p (n s) -> p s n", s=16))]),
                     writes=["BW_hi", "AT_hi"], banks=[kb])

            def kv_src(t):
                return I["xkv"][t * 128:(t + 1) * 128, :]

            if g == 0:
                norm_dma(kv_src(0), 0)
                norm_dma(kv_src(1), 1)
                for s_ in range(NT + 3):
                    if s_ + 2 < NT:
                        norm_dma(kv_src(s_ + 2), (s_ + 2) % 3)
                    if s_ < NT:
                        norm_a1(s_ % 3)
                        norm_a2(s_ % 3, s_ % 2)
                    if 0 <= s_ - 1 < NT:
                        norm_b((s_ - 1) % 2)
                    if 0 <= s_ - 2 < NT:
                        t_ = s_ - 2
                        P.dma("sp", DMA(scr2[t_].rearrange("p (k t) -> p k t", k=8), HT[t_ % 2]),
                              reads=["hT%d" % (t_ % 2)], writes=["scr2_%d" % t_], key="scr2w%d" % (t_ % 2))
                        kv_s1(t_, t_ % 2)
                    if 0 <= s_ - 3 < NT:
                        kv_s2(s_ - 3)
            else:
                def ht_load(t_):
                    P.dma("sp", DMA(HT[t_ % 4], scr2[t_].rearrange("p (k t) -> p k t", k=8)),
                          writes=["hT%d" % (t_ % 4)], key="hTl%d" % (t_ % 4))
                ht_load(0)
                ht_load(1)
                ht_load(2)
                for s_ in range(NT + 2):
                    if s_ + 3 < NT:
                        ht_load(s_ + 3)
                    if s_ < NT:
                        kv_s1(s_, s_ % 4)
                    if 0 <= s_ - 2 < NT:
                        kv_s2(s_ - 2)

            if g == 0:
                for kv_ in range(2):
                    rows = slice(64 * kv_, 64 * kv_ + 64)
                    for jh in range(2):
                        bo = pbank(6 + kv_, [128, 1], F32, col0=2 * jh)
                        P.op("pe", seq([MM(bo, W1[rows, l, jh * 128:(jh + 1) * 128], peT[rows, l:l + 1], l == 0, l == 31)
                                        for l in range(32)]),
                             reads=["W1k", "W1v", "pek", "pev"], banks=[6 + kv_])
                for kv_ in range(2):
                    P.op("dve", V_TT(beff[:, 2 * kv_:2 * kv_ + 2], pbank(6 + kv_, [128, 2, 2], F32)[:, :, 0], b1t[:, 2 * kv_:2 * kv_ + 2], ALU.add),
                         reads=["b1t"], writes=["beff%d" % kv_], banks=[6 + kv_])
            for jh in range(2):
                cbanks = (0, 1) if jh == 0 else (2, 3)
                mmz = []
                for l in range(32):
                    for kv_ in range(2):
                        rows = slice(64 * kv_, 64 * kv_ + 64)
                        src = AT_[rows, :].rearrange("p (s n) -> p s n", s=16)
                        rhs = src[:, l, 0:511] if l < 16 else src[:, l - 16, 1:512]
                        mmz.append(MM(pbank(cbanks[kv_], [128, 511], F32), W1[rows, l, jh * 128:(jh + 1) * 128], rhs, l == 0, l == 31))
                P.op("pe", seq(mmz), reads=["AT_lo", "AT_hi", "W1k", "W1v"], banks=list(cbanks))
                for kv_ in range(2):
                    ho = pbank(cbanks[kv_], [128, 511], F32)
                    gx, gu, gs_ = GX[:, 0:511], GU[:, 0:511], GS[:, 0:511]
                    bcol = beff[:, kv_ * 2 + jh:kv_ * 2 + jh + 1]
                    P.op("act", A_ACT(gx, ho, AF.Identity, bias=bcol), reads=["beff0", "beff1"], writes=["GX"], banks=[cbanks[kv_]])
                    P.op("pool", V_TT(gu, gx, gx, ALU.mult), reads=["GX"], writes=["GU"])
                    P.op("dve", V_TS(gu, gu, 0.044715, 1.0, ALU.mult, ALU.add), reads=["GU"], writes=["GU"])
                    P.op("pool", V_TT(gu, gu, gx, ALU.mult), reads=["GU", "GX"], writes=["GU"])
                    P.op("act", A_ACT(gs_, gu, AF.Sigmoid, scale=1.5957691216057308), reads=["GU"], writes=["GS"])
                    P.op("dve", V_TT(HID[:, kv_, jh, 0:511], gx, gs_, ALU.mult),
                         reads=["GX", "GS", "HID_pad"], writes=["HID%d%d" % (kv_, jh)])
            ko = pbank(6, [64, 511], F32)
            P.op("pe", seq([MM(ko, w2b[:, 0, jh, :], HID[:, 0, jh, 0:511], jh == 0, jh == 1) for jh in range(2)]),
                 reads=["HID00", "HID01", "w2k"], banks=[6])
            P.op("dve", V_CP(KCT[0:64, 0:511], ko), reads=["KCT_pad", "KCT_z"], writes=["KCT"], banks=[6])
            vo = pbank(7, [128, 4, 64], F32)
            P.op("pe", seq([MM(vo[:, nt_, :], HID[:, 1, jh, nt_ * 128:(nt_ + 1) * 128], w2b[:, 1, jh, :], jh == 0, jh == 1)
                            for nt_ in range(4) for jh in range(2)]),
                 reads=["HID10", "HID11", "w2v", "HID_pad"], banks=[7])
            P.op("dve", V_CP(VC[:, :, 0:64], vo), reads=["VC_ones"], writes=["VC"], banks=[7])
            if g == 0:
                dump("kct", KCT, [128, 512], BF16)
                dump("vc", VC, [128, 4, 66], BF16)
                dump("ka", KA[:, 0:1024], [128, 1024], BF16)
                dump("bw", BW[:, 0:1024], [128, 1024], BF16)
                dump("vsw", VSW[:, 0:8], [128, 8, 2, 66], BF16)
            P.barrier()

            P.op("pool", V_MEMSET(NMP, 0.0), writes=["NMP_z"])
            for i2 in range(2):
                P.op("pool", V_MEMSET(QPT[i2][64:128, :], 0.0), writes=["qpt%d_z" % i2])
            def ot_buf(j, i2):
                if j == 2:
                    return OTW[i2][0:65, :], "OTW%d" % i2
                return OT[0:65, j, :], "OT%d" % j

            def finalize_a(j, obank, i2=0):
                oo = pbank(obank, [65, 512], F32)
                dst, dn = ot_buf(j, i2)
                P.op("dve", V_CP(dst, oo), writes=[dn], banks=[obank])

            def finalize_b1(j, fbank, m):
                fo = pbank(fbank, [128, 4, 65], F32)
                src_, sn_ = ot_buf(j, m % 2)
                P.op("pe", seq([TR(fo[:, h, :], src_[:, h * 128:(h + 1) * 128], identf[0:65, 0:65]) for h in range(4)]),
                     reads=[sn_, "identf"], banks=[fbank])
                P.op("dve", V_TS(DEN[:, j, :], fo[:, :, 64], 1e-30, None, ALU.max), writes=["den%d" % j], banks=[fbank])
                P.op("dve", V_REC(RDEN[:, j, :], DEN[:, j, :]), reads=["den%d" % j], writes=["rden%d" % j])

            def finalize_b2(j, fbank, first, m):
                sl = m % RS
                gsb, acc = GSB[sl], ACC[sl]
                fo = pbank(fbank, [128, 4, 65], F32)
                P.op("dve", V_TT(COEF[:, j, :], RDEN[:, j, :], gsb.rearrange("p (h j) -> p j h", j=3)[:, j, :], ALU.mult),
                     reads=["rden%d" % j, "gsb%d" % sl], writes=["coef%d" % j])
                fns = []
                for h in range(4):
                    dst = ATB[:, h * 64:(h + 1) * 64] if j == 2 else acc[:, h, :]
                    if first:
                        fns.append(V_TS(dst, fo[:, h, 0:64], COEF[:, j, h:h + 1], None, ALU.mult))
                    else:
                        fns.append(V_STT(dst, fo[:, h, 0:64], COEF[:, j, h:h + 1], acc[:, h, :], ALU.mult, ALU.add))
                P.op("dve", seq(fns), reads=["coef%d" % j, "ACC%d" % sl], writes=["ATB"] if j == 2 else ["ACC%d" % sl], banks=[fbank])

            def PRO(m):
                i2 = m % 2
                sl = m % RS
                hq, qpr, qpt = HQ[i2], QPR[i2], QPT[i2]
                gsb, qa2 = GSB[sl], QA2[sl]
                P.dma("sp", DMA(hq, scr[m].rearrange("p (k t) -> p k t", k=8)), writes=["hq%d" % i2], key="hq%d" % i2)
                qo = pbank(B_QT, [128, 268], F32)
                P.op("pe", seq([MM(qo, hq[:, k, :], WQ[:, k, :], k == 0, k == 7) for k in range(8)]),
                     reads=["hq%d" % i2, "WQ"], banks=[B_QT])
                qv = pbank(B_QT, [128, 4, 64], F32)
                P.op("act", A_ACT(GEX, qo[:, 256:268], AF.Exp, scale=-1.0), writes=["GEX"], banks=[B_QT])
                P.op("dve", seq([V_CP(qpr[:, :, 0, :], qv), V_CP(qpr[:, :, 1, 16:64], qv[:, :, 16:64])]),
                     writes=["qpr%d_a" % i2], banks=[B_QT])
                P.op("dve", V_TS(GEX, GEX, 1.0, None, ALU.add), reads=["GEX"], writes=["GEX"])
                P.op("dve", V_REC(gsb, GEX), reads=["GEX"], writes=["gsb%d" % sl])
                rope_ops(qv, qpr[:, :, 1, :], cosq[:, m, :], sinq[:, m, :], RT2[i2], 4, ["cosq", "sinq"],
                         "qpr%d_r" % i2, "rt2_%d" % i2, B_QT)
                yield
                tq_ = pbank(B_QT, [128, 4, 128], BF16)
                P.op("pe", seq([TR(tq_[:, h, :], qpr[:, h, :, :].rearrange("p a d -> p (a d)"), identb) for h in range(4)]),
                     reads=["qpr%d_a" % i2, "qpr%d_r" % i2, "identb"], banks=[B_QT])
                P.op("dve", V_CP(qpt[0:64, :], tq_[0:64].rearrange("p h q -> p (h q)")),
                     reads=["qpt%d_z" % i2], writes=["qpt%d" % i2], banks=[B_QT])
                P.op("dve", V_CP(qa2[64:128], tq_[64:128].rearrange("p h q -> p (h q)").unsqueeze(1).to_broadcast([64, 2, 512])),
                     writes=["qa2%d_q" % sl], banks=[B_QT])
                yield
                ncmp = 32 * m + 32
                T_ = (ncmp + 127) // 128
                tiles = []
                for tt in range(T_):
                    n0 = tt * 128
                    qk = [(KCT[:, n0:n0 + 128], qpt)]
                    rd = ["KCT", "qpt%d" % i2]
                    if tt == T_ - 1:
                        off = ncmp - n0 - 32
                        qk.append((zsel[:, 96 - off:224 - off], cmb))
                        rd += ["zsel", "cmb"]
                    if tt == 0:
                        qk.append((padselb, padmb))
                        rd += ["padselb", "padmb"]
                    tiles.append(dict(qk=qk, rd=rd, pbuf=PC[tt], pname="pc%d" % tt,
                                      pv=(VC[:, tt, 0:65], B_OC, tt == 0, tt == T_ - 1), pvrd=["VC"]))
                for _ in tiles_gen(tiles):
                    yield
                io = pbank(B_I, [128, 4, 128], F32)
                P.op("pe", seq([MM(io[:, h, :], PC[tt][:, h * 128:(h + 1) * 128], mmapb[:, tt, :], tt == 0, tt == T_ - 1)
                                for h in range(4) for tt in range(T_)]),
                     reads=["pc%d" % tt for tt in range(T_)] + ["mmapb"], banks=[B_I])
                finalize_a(0, B_OC)
                yield
                finalize_b1(0, B_F, m)
                nb = 8 * m + 8 if m < 8 else 128
                P.op("dve", V_TS(IMPV[:, 0:nb], io[:, 0, 0:nb], RDEN[:, 0, 0:1], None, ALU.mult), reads=["rden0"], writes=["IMPV"], banks=[B_I])
                for h in range(1, 4):
                    P.op("dve", V_STT(IMPV[:, 0:nb], io[:, h, 0:nb], RDEN[:, 0, h:h + 1], IMPV[:, 0:nb], ALU.mult, ALU.add),
                         reads=["rden0", "IMPV"], writes=["IMPV"], banks=[B_I])
                P.op("dve", V_TT(V2[:, 0:nb], IMPV[:, 0:nb], gtab[:, 120 - 8 * m:120 - 8 * m + nb], ALU.add), reads=["IMPV", "gtab"], writes=["V2"])
                P.op("dve", V_TT(V2[:, 0:nb], V2[:, 0:nb], g0t[:, 0:nb], ALU.add), reads=["V2", "g0t"], writes=["V2"])
                P.op("dve", V_MAX(M8[:, 0:8], V2[:, 0:nb]), reads=["V2"], writes=["M8a"])
                P.op("dve", V_MR(WK[:, 0:nb], M8[:, 0:8], V2[:, 0:nb], -3.0e38), reads=["V2", "M8a"], writes=["WK"])
                P.op("dve", V_MAX(M8[:, 8:16], WK[:, 0:nb]), reads=["WK"], writes=["M8b"])
                if m < 2:
                    P.op("dve", V_TS(M8[:, 15:16], M8[:, 15:16], -1.0e8, None, ALU.max), reads=["M8b"], writes=["M8b"])
                if nb <= 64:
                    P.op("dve", V_TS(NMP[:, 0, 0:nb], V2[:, 0:nb], M8[:, 15:16], NEG, ALU.is_lt, ALU.mult),
                         reads=["V2", "M8b", "NMP_z"], writes=["NMP"])
                else:
                    P.op("dve", V_TS(NMP[:, :, 0:64], V2.rearrange("p (a b) -> p a b", a=2), M8[:, 15:16], NEG, ALU.is_lt, ALU.mult),
                         reads=["V2", "M8b", "NMP_z"], writes=["NMP"])
                finalize_b2(0, B_F, True, m)
                yield
                yield
                yield
                nt2 = pbank(B_QT, [128, 2, 128], BF16, col0=384)
                P.op("pe", seq([TR(nt2[:, 0, :], NMP[:, 0, :], identb), TR(nt2[:, 1, :], NMP[:, 1, :], identb)]),
                     reads=["NMP", "identb"], banks=[B_QT])
                P.op("dve", V_CP(qa2[0:64].rearrange("p a (h q) -> p a h q", h=4),
                                 nt2[0:64].unsqueeze(2).to_broadcast([64, 2, 4, 128])),
                     writes=["qa2%d_m" % sl], banks=[B_QT])

            def FIN_bc(m):
                finalize_b1(1, B_F, m)
                finalize_b2(1, B_F, False, m)
                yield
                finalize_b1(2, B_F, m)
                finalize_b2(2, B_F, False, m)
                yield
                ao = pbank(B_QT, [128, 2, 128], BF16)
                P.op("pe", seq([TR(ao[:, 0, :], ATB[:, 0:128], identb), TR(ao[:, 1, :], ATB[:, 128:256], identb)]),
                     reads=["ATB", "identb"], banks=[B_QT])
                P.op("dve", V_CP(attnT[:, 2 * g:2 * g + 2, m * 128:(m + 1) * 128], ao), writes=["attnT"], banks=[B_QT])

            def win_tiles(m):
                sl = m % RS
                qa2 = QA2[sl]
                tiles = []
                js = [j for j in range(5) if 4 * m - 1 + j >= 0]
                for ji, j in enumerate(js):
                    kt = 4 * m - 1 + j
                    cs = slice(kt * 128, (kt + 1) * 128)
                    qk = [(BW[:, cs], qa2[:, 0, :])]
                    rd = ["BW_hi", "BW_z", "qa2%d_q" % sl, "qa2%d_m" % sl]
                    if j == 0:
                        qk.append((identb, wmb[:, 0, :]))
                        rd += ["identb", "wmb"]
                    elif j == 4:
                        qk.append((identb, wmb[:, 1, :]))
                        rd += ["identb", "wmb"]
                    elif m == 0:
                        qk.append((identb, wm0b[:, kt, :]))
                        rd += ["identb", "wm0b"]
                    tiles.append(dict(qk=qk, rd=rd, pbuf=None, pname=None,
                                      pv=(VSW[:, kt, 1, 0:65], B_OW, ji == 0, ji == len(js) - 1), pvrd=["VSW", "VSW_ones"]))
                return tiles

            def slc_tiles(m):
                sl = m % RS
                qa2 = QA2[sl]
                tiles = []
                nk = 4 * m + 4
                for kt in range(nk):
                    cs = slice(kt * 128, (kt + 1) * 128)
                    half = kt // 32
                    qk = [(KA[:, cs], qa2[:, half, :])]
                    rd = ["KA_E", "KA_hi", "qa2%d_q" % sl, "qa2%d_m" % sl]
                    if kt == nk - 1:
                        qk.append((identb, tmb))
                        rd += ["identb", "tmb"]
                    tiles.append(dict(qk=qk, rd=rd, pbuf=None, pname=None,
                                      pv=(VSW[:, kt, 0, 0:65], B_OS, kt == 0, kt == nk - 1), pvrd=["VSW", "VSW_ones"]))
                return tiles

            order = list(range(n_qblocks - 1, -1, -1))
            created = set()
            pros = []

            def side_step(queue):
                while queue:
                    try:
                        next(queue[0][1])
                        return
                    except StopIteration:
                        queue.pop(0)

            created.add(order[0])
            exhaust(PRO(order[0]))
            prev_fin = None
            for idx, m in enumerate(order):
                queue = []
                if prev_fin is not None:
                    queue.append(("fin", prev_fin))
                for mm in order[idx + 1: idx + 1 + (RS - 2)]:
                    if mm not in created:
                        created.add(mm)
                        pros.append((mm, PRO(mm)))
                queue += pros
                wt = win_tiles(m)
                wt[-1]["post"] = (lambda m=m: finalize_a(2, B_OW, m % 2))
                tl = wt + slc_tiles(m)
                k_ = 0
                for _ in tiles_gen(tl):
                    k_ += 1
                    if k_ % 2 == 0:
                        side_step(queue)
                need = 1 if prev_fin is not None else 0
                nxt = order[idx + 1] if idx + 1 < len(order) else None
                while queue and (queue[0][0] == "fin" or queue[0][0] == nxt):
                    try:
                        next(queue[0][1])
                    except StopIteration:
                        queue.pop(0)
                pros = [q_ for q_ in queue if q_[0] != "fin"]
                finalize_a(1, B_OS)
                prev_fin = FIN_bc(m)
            exhaust(prev_fin)
            P.barrier()

        dump("attnT", attnT, [128, 8, 2048], BF16)

        if phase3:
            B3 = Bump(P0)
            HTQ = B3([128, 8, 2048], BF16)
            HTH = B3([128, 8, 256], BF16)
            POOLT = B3([128, 4, 2048], BF16)
            R0 = B3.off
            Ra = Bump(R0)
            UT = Ra([128, 4, 16, 144], F32)
            PP = [Ra([128, 16, 144], F32) for _ in range(2)]
            WPOOL = Ra([128, 8, 512], BF16)
            POOLW = Ra([128, 4, 128], BF16)
            POOLED = Ra([128, 4, 2048], BF16)
            assert Ra.off <= ARENA_BYTES, Ra.off
            for m in range(NQ):
                P.dma("sp", DMA(HTQ[:, :, m * 128:(m + 1) * 128], scr[m].rearrange("p (k t) -> p k t", k=8)),
                      writes=["HTQ"], key="HTQ%d" % (m % 4))
            for hb_ in range(2):
                P.dma("sp", DMA(HTH[:, :, hb_ * 128:(hb_ + 1) * 128], scr[16 + hb_].rearrange("p (k t) -> p k t", k=8)),
                      writes=["HTH"], key="HTH%d" % hb_)
            load("pool", WPOOL, I["wpool"].rearrange("(kc kp) n -> kp kc n", kp=128), "WPOOL")
            load("pool", POOLW, I["poolw"].rearrange("g c d -> c g d"), "POOLW")
            for gp in range(4):
                un = "UT%d" % gp
                for tt in range(4):
                    bk = tt % 2
                    uo = pbank(bk, [128, 512], F32)
                    P.op("pe", seq([MM(uo, WPOOL[:, k, gp * 128:(gp + 1) * 128], HTQ[:, k, tt * 512:(tt + 1) * 512], k == 0, k == 7)
                                    for k in range(8)]), reads=["WPOOL", "HTQ"], banks=[bk])
                    P.op("act", A_CP(UT[:, gp, 4 * tt:4 * tt + 4, 16:144], uo.rearrange("p (a b) -> p a b", a=4)),
                         writes=[un], banks=[bk])
                uh = pbank(2, [128, 256], F32)
                P.op("pe", seq([MM(uh, WPOOL[:, k, gp * 128:(gp + 1) * 128], HTH[:, k, :], k == 0, k == 7) for k in range(8)]),
                     reads=["WPOOL", "HTH"], banks=[2])
                P.op("act", A_CP(UT[:, gp, :, 0:16], uh.rearrange("p (a b) -> p a b", a=16)), writes=[un], banks=[2])
            for gp in range(4):
                un = "UT%d" % gp
                cur = UT[:, gp]
                curn = un
                lo = 0
                for si in range(gp + 1):
                    stp = 1 << si
                    lo2 = lo + stp
                    dst = PP[si % 2]
                    dn = "pp%d" % (si % 2)
                    P.op("dve", V_TT(dst[:, :, lo2:144], cur[:, :, lo2:144], cur[:, :, lo2 - stp:144 - stp], ALU.add),
                         reads=[curn], writes=[dn])
                    cur, curn, lo = dst, dn, lo2
                wsz = 2 << gp
                pl = POOLED[:, gp].rearrange("p (a b) -> p a b", a=16)
                pn = "POOLED%d" % gp
                P.op("dve", V_STT(pl[:, 1:16, :], cur[:, 1:16, 16:144], 1.0 / wsz, UT[:, gp, 1:16, 16:144], ALU.mult, ALU.subtract),
                     reads=[curn, un], writes=[pn])
                P.op("dve", V_TT(cur[:, 0, 16:144], cur[:, 0, 16:144], invc[:, gp, :], ALU.mult),
                     reads=[curn, "invc", pn], writes=[curn])
                P.op("dve", V_TT(pl[:, 0, :], cur[:, 0, 16:144], UT[:, gp, 0, 16:144], ALU.subtract),
                     reads=[curn, un, pn], writes=[pn])
                for tt in range(4):
                    bk = 3 + tt % 2
                    mo = pbank(bk, [128, 512], F32)
                    P.op("pe", MM(mo, POOLW[:, gp, :], POOLED[:, gp, tt * 512:(tt + 1) * 512]), reads=["POOLW", pn], banks=[bk])
                    P.op("act", A_ACT(POOLT[:, gp, tt * 512:(tt + 1) * 512], mo, AF.Copy, scale=psc[:, gp:gp + 1]),
                         reads=["psc"], writes=["POOLT"], banks=[bk])
            dump("poolT", POOLT, [128, 4, 2048], BF16)
            P.barrier()
            Rb = Bump(R0)
            MERGED = Rb([128, 8, 2048], BF16)
            WST = [Rb([128, 28, 128], BF16) for _ in range(2)]
            SAB = [Rb([128, 2, 512], F32) for _ in range(2)]
            T12 = [Rb([128, 2, 512], F32) for _ in range(2)]
            WOUT = Rb([128, 8, 1024], BF16)
            assert Rb.off <= ARENA_BYTES, Rb.off
            for j in range(8):
                ws = WST[j % 2]
                cs = slice(j * 128, (j + 1) * 128)
                nm_ = "wst%d" % (j % 2)
                load("pool", ws[:, 0:8, :], I["wgp"][:, cs].rearrange("(kc kp) n -> kp kc n", kp=128), nm_ + "a")
                load("pool", ws[:, 8:16, :], I["wga"][:, cs].rearrange("(kc kp) n -> kp kc n", kp=128), nm_ + "b")
                load("pool", ws[:, 16:20, :], I["wpp"][:, cs].rearrange("(kc kp) n -> kp kc n", kp=128), nm_ + "c")
                load("pool", ws[:, 20:28, :], I["wpa"][:, cs].rearrange("(kc kp) n -> kp kc n", kp=128), nm_ + "d")
                for tt in range(4):
                    ts_ = slice(tt * 512, (tt + 1) * 512)
                    q4 = (j * 4 + tt) % 2
                    ba, bb, bc, bd = (0, 1, 2, 3) if q4 == 0 else (4, 5, 6, 7)
                    pa, pb2, pc_, pd = (pbank(x, [128, 512], F32) for x in (ba, bb, bc, bd))
                    P.op("pe", seq([MM(pa, ws[:, k, :], HTQ[:, k, ts_], k == 0, k == 7) for k in range(8)]), reads=[nm_ + "a", "HTQ"], banks=[ba])
                    P.op("pe", seq([MM(pb2, ws[:, 8 + k, :], HTQ[:, k, ts_], k == 0, k == 7) for k in range(8)]), reads=[nm_ + "b", "HTQ"], banks=[bb])
                    P.op("pe", seq([MM(pc_, ws[:, 16 + k, :], POOLT[:, k, ts_], k == 0, k == 3) for k in range(4)]), reads=[nm_ + "c", "POOLT"], banks=[bc])
                    P.op("pe", seq([MM(pd, ws[:, 20 + k, :], attnT[:, k, ts_], k == 0, k == 7) for k in range(8)]), reads=[nm_ + "d", "attnT"], banks=[bd])
                    sab, t12 = SAB[q4], T12[q4]
                    P.op("act", A_ACT(sab[:, 0, :], pa, AF.Sigmoid), writes=["sab%da" % q4], banks=[ba])
                    P.op("act", A_ACT(sab[:, 1, :], pb2, AF.Sigmoid), writes=["sab%db" % q4], banks=[bb])
                    P.op("dve", V_TT(t12[:, 0, :], pc_, sab[:, 0, :], ALU.mult), reads=["sab%da" % q4], writes=["t12%da" % q4], banks=[bc])
                    P.op("dve", V_TT(t12[:, 1, :], pd, sab[:, 1, :], ALU.mult), reads=["sab%db" % q4], writes=["t12%db" % q4], banks=[bd])
                    P.op("pool", V_TT(MERGED[:, j, ts_], t12[:, 0, :], t12[:, 1, :], ALU.add),
                         reads=["t12%da" % q4, "t12%db" % q4], writes=["MERGED"])
            load("pool", WOUT, I["wout"].rearrange("(kc kp) n -> kp kc n", kp=128), "WOUT")
            dump("merged", MERGED, [128, 8, 2048], BF16)
            P.barrier()
            X1T = carve(C_END, [128, 16, 1024], F32)
            XQB = [carve(C_END + 65536 + 4096 * i, [128, 1024], F32) for i in range(2)]
            for m in range(NQ):
                xq_ = XQB[m % 2]
                xn = "xq%d" % (m % 2)
                P.dma("sp", DMA(xq_, I["xq"][m * 128:(m + 1) * 128, :]), writes=[xn], key=xn)
                for nh in range(2):
                    bk = (m * 2 + nh) % 4
                    po = pbank(bk, [128, 512], F32)
                    P.op("pe", seq([MM(po, MERGED[:, k, m * 128:(m + 1) * 128], WOUT[:, k, nh * 512:(nh + 1) * 512], k == 0, k == 7)
                                    for k in range(8)]), reads=["MERGED", "WOUT"], banks=[bk])
                    P.op("dve", V_TT(X1T[:, m, nh * 512:(nh + 1) * 512], po, xq_[:, nh * 512:(nh + 1) * 512], ALU.add),
                         reads=[xn], writes=["X1T%d" % m], banks=[bk])
            load("sp", nw, I["n2w"].partition_broadcast(128), "nw")
            dump("x1", X1T, [128, 16, 1024], F32)
            P.barrier()
            Bd = Bump(C_END + 65536)
            H2T = Bd([128, 8, 1024], BF16)
            ACTT = Bd([128, 22, 1024], BF16)
            WDB = Bd([128, 22, 512], BF16)
            WGU = [[Bd([128, 8, 256], BF16) for _ in range(2)] for _ in range(2)]
            HB2 = [Bd([128, 1024], BF16) for _ in range(2)]
            JUNK2 = Bd([128, 1024], BF16)
            SS2 = [Bd([128, 4], F32) for _ in range(2)]
            SG = [Bd([128, 512], F32) for _ in range(2)]
            OST = [carve(C_END + 65536 + 4096 * i, [128, 1024], F32) for i in range(2)]
            assert Bd.off <= ARENA_BYTES, Bd.off

            def rms_sb(xt, i, tag):
                ssq = SS2[i]
                P.op("act", A_ACT(JUNK2, xt, AF.Square, accum_out=ssq[:, 0:1]), reads=[tag], writes=["s2_%d_0" % i])
                P.op("dve", V_TS(ssq[:, 1:2], ssq[:, 0:1], 1.0 / D, EPS, ALU.mult, ALU.add), reads=["s2_%d_0" % i], writes=["s2_%d_1" % i])
                P.op("pool", V_TT(ssq[:, 2:3], ssq[:, 1:2], mhalf, ALU.pow), reads=["s2_%d_1" % i, "mhalf"], writes=["s2_%d_2" % i])

            for th in range(2):
                load("pool", WDB, I["wd"][:, 0:512].rearrange("(j p) n -> p j n", p=128), "WDB")

                def n2_a(mm_):
                    m = th * 8 + mm_
                    rms_sb(X1T[:, m, :], m % 2, "X1T%d" % m)

                def n2_b(mm_):
                    m = th * 8 + mm_
                    i = m % 2
                    P.op("dve", V_STT(HB2[i], X1T[:, m, :], SS2[i][:, 2:3], nw, ALU.mult, ALU.mult),
                         reads=["X1T%d" % m, "s2_%d_2" % i, "nw"], writes=["hb2_%d" % i])

                def n2_c(mm_):
                    i = (th * 8 + mm_) % 2
                    tv = pbank(i, [128, 8, 128], BF16)
                    P.op("pe", seq([TR(tv[:, k, :], HB2[i][:, k * 128:(k + 1) * 128], identb) for k in range(8)]),
                         reads=["hb2_%d" % i, "identb"], banks=[i])
                    P.op("act", A_CP(H2T[:, :, mm_ * 128:(mm_ + 1) * 128], tv), writes=["H2T"], banks=[i])

                for s_ in range(8 + 2):
                    if s_ < 8:
                        n2_a(s_)
                    if 0 <= s_ - 2 < 8:
                        n2_c(s_ - 2)
                    if 0 <= s_ - 1 < 8:
                        n2_b(s_ - 1)
                for jp in range(11):
                    wgb, wub = WGU[jp % 2]
                    cs = slice(jp * 256, (jp + 1) * 256)
                    load("pool", wgb, I["wg"][:, cs].rearrange("(kc kp) n -> kp kc n", kp=128), "wg%d" % (jp % 2))
                    load("pool", wub, I["wu"][:, cs].rearrange("(kc kp) n -> kp kc n", kp=128), "wu%d" % (jp % 2))
                    for cc in range(2):
                        jf = jp * 2 + cc
                        for tt in range(2):
                            q4 = (jf * 2 + tt) % 2
                            bg, bu = (2, 3) if q4 == 0 else (4, 5)
                            pg, pu = pbank(bg, [128, 512], F32), pbank(bu, [128, 512], F32)
                            ts_ = slice(tt * 512, (tt + 1) * 512)
                            P.op("pe", seq([MM(pg, wgb[:, k, cc * 128:(cc + 1) * 128], H2T[:, k, ts_], k == 0, k == 7) for k in range(8)]),
                                 reads=["wg%d" % (jp % 2), "H2T"], banks=[bg])
                            P.op("pe", seq([MM(pu, wub[:, k, cc * 128:(cc + 1) * 128], H2T[:, k, ts_], k == 0, k == 7) for k in range(8)]),
                                 reads=["wu%d" % (jp % 2), "H2T"], banks=[bu])
                            sg = SG[q4]
                            P.op("act", A_ACT(sg, pg, AF.Silu), writes=["sg%d" % q4], banks=[bg])
                            P.op("dve", V_TT(ACTT[:, jf, ts_], pu, sg, ALU.mult), reads=["sg%d" % q4], writes=["ACTT"], banks=[bu])
                for nh in range(2):
                    if nh == 1:
                        load("pool", WDB, I["wd"][:, 512:1024].rearrange("(j p) n -> p j n", p=128), "WDB")
                    for mm_ in range(8):
                        m = th * 8 + mm_
                        bk = 6 + mm_ % 2
                        po = pbank(bk, [128, 512], F32)
                        P.op("pe", seq([MM(po, ACTT[:, jf, mm_ * 128:(mm_ + 1) * 128], WDB[:, jf, :], jf == 0, jf == 21) for jf in range(22)]),
                             reads=["ACTT", "WDB"], banks=[bk])
                        xs = X1T[:, m, nh * 512:(nh + 1) * 512]
                        P.op("dve", V_TT(xs, po, xs, ALU.add), reads=["X1T%d" % m], writes=["X1T%d" % m], banks=[bk])
            load("sp", nw, I["nfw"].partition_broadcast(128), "nw")
            P.barrier()
            for s_ in range(NQ + 1):
                if s_ < NQ:
                    rms_sb(X1T[:, s_, :], s_ % 2, "X1T%d" % s_)
                if s_ - 1 >= 0:
                    m = s_ - 1
                    i = m % 2
                    ost = OST[i]
                    P.op("dve", V_STT(ost, X1T[:, m, :], SS2[i][:, 2:3], nw, ALU.mult, ALU.mult),
                         reads=["X1T%d" % m, "s2_%d_2" % i, "nw"], writes=["ost%d" % i])
                    P.dma("sp", DMA(out_d[m * 128:(m + 1) * 128, :], ost), reads=["ost%d" % i], key="ost%d" % i)
        else:
            P.dma("sp", DMA(out_d[0:128, :], nw), reads=["nw"], key="ost0")
        nsem = P.emit(st)
    return nc, nsem, len(P.ops), dbg_list


_CACHE = {}


def kernel(**inputs):
    sh = _shared_inputs(inputs)
    in_maps = [_core_inputs(inputs, sh, r) for r in range(8)]
    if "nc" not in _CACHE:
        _CACHE["nc"] = build_program()[0]
    nc = _CACHE["nc"]
    res = run_bass_kernel_spmd(nc, in_maps, core_ids=list(range(8)))
    out = np.zeros((2, S, D), np.float32)
    for r in range(8):
        b, c = r // 4, r % 4
        o = np.asarray(res.results[r]["out"], np.float32).reshape(NQ, 128, D)
        for m in range(NQ):
            t0 = (4 * m + c) * 128
            out[b, t0:t0 + 128] = o[m]
    return out
```
